# Optimizing a Trainium2 kernel written in Bass

```python
import math
import jax, jax.numpy as jnp
from jax import lax
import numpy as np

D_MODEL = 1024
BATCH = 8
SEQ = 2048
DEPTH = 1
DEC_BATCH = 128
DEC_SEQ = 4
PAST_LEN = 16384
PAGE_SIZE = 128

C_A = D_MODEL
CONV_W = 31
RET_DK = 256
RET_H = D_MODEL // RET_DK
RET_DV = 2 * RET_DK
RET_CHUNK = 128
ROPE_BASE = 10000.0
D_FF = ((8 * D_MODEL // 3 + 255) // 256) * 256
PLE_DIM = 256
EPS = 1e-6

SPLIT_SIZES = (C_A, C_A, RET_H * RET_DK, RET_H * RET_DK, RET_H * RET_DV, RET_H * RET_DV, D_MODEL, D_MODEL)
SPLIT_POINTS = tuple(int(v) for v in np.cumsum(SPLIT_SIZES)[:-1])
N_IN = sum(SPLIT_SIZES)

kernel_name = "conformer_retention_gated_hybrid_step"


def rmsnorm(x, g):
    xf = x.astype(jnp.float32)
    r = lax.rsqrt(jnp.mean(xf * xf, axis=-1, keepdims=True) + EPS)
    return (xf * r).astype(x.dtype) * g


def layernorm(x, g, b):
    xf = x.astype(jnp.float32)
    mu = jnp.mean(xf, axis=-1, keepdims=True)
    var = jnp.mean(jnp.square(xf - mu), axis=-1, keepdims=True)
    return ((xf - mu) * lax.rsqrt(var + 1e-5)).astype(x.dtype) * g + b


def rotary(x, pos):
    half = x.shape[-1] // 2
    inv = 1.0 / (ROPE_BASE ** (jnp.arange(half, dtype=jnp.float32) / half))
    ang = pos[:, None] * inv[None, :]
    cos = jnp.cos(ang)[None, :, None, :]
    sin = jnp.sin(ang)[None, :, None, :]
    xf = x.astype(jnp.float32)
    x1, x2 = xf[..., :half], xf[..., half:]
    return jnp.concatenate([x1 * cos - x2 * sin, x1 * sin + x2 * cos], axis=-1)


def retention_block(q, k, v, s0, log_g):
    L = q.shape[2]
    idx = jnp.arange(L, dtype=jnp.float32)
    diff = idx[:, None] - idx[None, :]
    decay = jnp.where(diff >= 0, jnp.exp(log_g[:, None, None] * jnp.maximum(diff, 0.0)), 0.0)
    scores = jnp.einsum('bhld,bhmd->bhlm', q, k) * decay
    inner = jnp.einsum('bhlm,bhmv->bhlv', scores, v)
    cross = jnp.einsum('bhld,bhdv->bhlv', q, s0) * jnp.exp(log_g[:, None] * (idx + 1.0))[None, :, :, None]
    k_dec = k * jnp.exp(log_g[:, None] * (L - 1.0 - idx))[None, :, :, None]
    s1 = jnp.exp(log_g * L)[None, :, None, None] * s0 + jnp.einsum('bhld,bhlv->bhdv', k_dec, v)
    return inner + cross, s1


def retention(q, k, v, s0, log_g):
    B, H, L, _ = q.shape
    C = RET_CHUNK if L % RET_CHUNK == 0 else L
    n = L // C

    def to_chunks(t):
        return jnp.moveaxis(t.reshape(B, H, n, C, t.shape[-1]), 2, 0)

    def step(s, inp):
        qc, kc, vc = inp
        o, s1 = retention_block(qc, kc, vc, s, log_g)
        return s1, o

    s_final, o = lax.scan(step, s0, (to_chunks(q), to_chunks(k), to_chunks(v)))
    o = jnp.moveaxis(o, 0, 2).reshape(B, H, L, v.shape[-1])
    return o, s_final


def token_mixers(h, conv_prev, ret_prev, pos, w_in, conv_w, conv_b, conv_ln_g, conv_ln_b,
                 w_conv_out, ret_gn_g, ret_gn_b, w_ret_out, w_o):
    B, L, _ = h.shape
    z = h @ w_in
    a_val, a_gate, q, k, v, g, gate_a, gate_b = jnp.split(z, SPLIT_POINTS, axis=-1)

    u = a_val * jax.nn.sigmoid(a_gate)
    ucat = jnp.concatenate([conv_prev.astype(u.dtype), u], axis=1)
    c = lax.conv_general_dilated(ucat, conv_w[:, None, :].astype(u.dtype), (1,), 'VALID',
                                 dimension_numbers=('NWC', 'WIO', 'NWC'),
                                 feature_group_count=C_A) + conv_b
    c = jax.nn.silu(layernorm(c, conv_ln_g, conv_ln_b))
    y_a = c @ w_conv_out
    new_conv = ucat[:, -(CONV_W - 1):]

    log_g = jnp.log1p(-jnp.exp2(-5.0 - jnp.arange(RET_H, dtype=jnp.float32)))
    qr = jnp.transpose(rotary(q.reshape(B, L, RET_H, RET_DK), pos), (0, 2, 1, 3))
    kr = jnp.transpose(rotary(k.reshape(B, L, RET_H, RET_DK), pos), (0, 2, 1, 3)) * (RET_DK ** -0.5)
    vr = jnp.transpose(v.reshape(B, L, RET_H, RET_DV), (0, 2, 1, 3)).astype(jnp.float32)
    o, s_new = retention(qr, kr, vr, ret_prev.astype(jnp.float32), log_g)
    mu = jnp.mean(o, axis=-1, keepdims=True)
    var = jnp.mean(jnp.square(o - mu), axis=-1, keepdims=True)
    o = (o - mu) * lax.rsqrt(var + 1e-5)
    o = jnp.transpose(o, (0, 2, 1, 3)).reshape(B, L, RET_H * RET_DV).astype(h.dtype)
    o = o * ret_gn_g + ret_gn_b
    y_b = (jax.nn.silu(g) * o) @ w_ret_out

    mix = (jax.nn.sigmoid(gate_a) * y_a + jax.nn.sigmoid(gate_b) * y_b) @ w_o
    return mix, new_conv, s_new


def decoder_layer(x, p, conv_prev, ret_prev, pos, ln_mix_g, w_in, conv_w, conv_b, conv_ln_g, conv_ln_b,
                  w_conv_out, ret_gn_g, ret_gn_b, w_ret_out, w_o, ln_ffn_g, w_ffn_gate, w_ffn_up,
                  w_ffn_down, ln_ple_g, w_ple_gate, w_ple_proj):
    h = rmsnorm(x, ln_mix_g)
    mix, new_conv, new_ret = token_mixers(h, conv_prev, ret_prev, pos, w_in, conv_w, conv_b, conv_ln_g,
                                          conv_ln_b, w_conv_out, ret_gn_g, ret_gn_b, w_ret_out, w_o)
    x = x + mix
    hf = rmsnorm(x, ln_ffn_g)
    x = x + (jax.nn.silu(hf @ w_ffn_gate) * (hf @ w_ffn_up)) @ w_ffn_down
    gate = jax.nn.sigmoid(rmsnorm(x, ln_ple_g) @ w_ple_gate)
    x = x + gate * (p @ w_ple_proj)
    return x, new_conv, new_ret


def setup_inputs(seed: int = 0) -> dict:
    key = jax.random.key(seed)
    ks = iter(jax.random.split(key, 32))
    f32 = jnp.float32

    def nrm(shape, scale):
        return jax.random.normal(next(ks), shape, f32) * scale

    def gain(shape):
        return 1.0 + nrm(shape, 0.02)

    return {
        "x_prompt": nrm((BATCH, SEQ, D_MODEL), 1.0),
        "x_sample": nrm((DEC_BATCH, DEC_SEQ, D_MODEL), 1.0),
        "state_conv": nrm((DEPTH, DEC_BATCH, CONV_W - 1, C_A), 0.5),
        "state_ret": nrm((DEPTH, DEC_BATCH, RET_H, RET_DK, RET_DV), 0.5),
        "p_prompt": nrm((DEPTH, BATCH, SEQ, PLE_DIM), 1.0),
        "p_sample": nrm((DEPTH, DEC_BATCH, DEC_SEQ, PLE_DIM), 1.0),
        "ln_mix_g": gain((DEPTH, D_MODEL)),
        "w_in": nrm((DEPTH, D_MODEL, N_IN), D_MODEL ** -0.5),
        "conv_w": nrm((DEPTH, CONV_W, C_A), CONV_W ** -0.5),
        "conv_b": nrm((DEPTH, C_A), 0.02),
        "conv_ln_g": gain((DEPTH, C_A)),
        "conv_ln_b": nrm((DEPTH, C_A), 0.02),
        "w_conv_out": nrm((DEPTH, C_A, D_MODEL), C_A ** -0.5),
        "ret_gn_g": gain((DEPTH, RET_H * RET_DV)),
        "ret_gn_b": nrm((DEPTH, RET_H * RET_DV), 0.02),
        "w_ret_out": nrm((DEPTH, RET_H * RET_DV, D_MODEL), (RET_H * RET_DV) ** -0.5),
        "w_o": nrm((DEPTH, D_MODEL, D_MODEL), D_MODEL ** -0.5),
        "ln_ffn_g": gain((DEPTH, D_MODEL)),
        "w_ffn_gate": nrm((DEPTH, D_MODEL, D_FF), D_MODEL ** -0.5),
        "w_ffn_up": nrm((DEPTH, D_MODEL, D_FF), D_MODEL ** -0.5),
        "w_ffn_down": nrm((DEPTH, D_FF, D_MODEL), D_FF ** -0.5),
        "ln_ple_g": gain((DEPTH, D_MODEL)),
        "w_ple_gate": nrm((DEPTH, D_MODEL, D_MODEL), D_MODEL ** -0.5),
        "w_ple_proj": nrm((DEPTH, PLE_DIM, D_MODEL), PLE_DIM ** -0.5),
        "ln_final_g": gain((D_MODEL,)),
    }


def reference(x_prompt, x_sample, state_conv, state_ret, p_prompt, p_sample, ln_mix_g, w_in, conv_w,
              conv_b, conv_ln_g, conv_ln_b, w_conv_out, ret_gn_g, ret_gn_b, w_ret_out, w_o, ln_ffn_g,
              w_ffn_gate, w_ffn_up, w_ffn_down, ln_ple_g, w_ple_gate, w_ple_proj, ln_final_g):
    B, S, _ = x_prompt.shape
    pos_prompt = jnp.arange(S, dtype=jnp.float32)
    pos_sample = PAST_LEN + jnp.arange(x_sample.shape[1], dtype=jnp.float32)
    xp, xs = x_prompt, x_sample
    conv_p, ret_p, conv_s, ret_s = [], [], [], []
    for l in range(DEPTH):
        lw = (ln_mix_g[l], w_in[l], conv_w[l], conv_b[l], conv_ln_g[l], conv_ln_b[l], w_conv_out[l],
              ret_gn_g[l], ret_gn_b[l], w_ret_out[l], w_o[l], ln_ffn_g[l], w_ffn_gate[l], w_ffn_up[l],
              w_ffn_down[l], ln_ple_g[l], w_ple_gate[l], w_ple_proj[l])
        zero_conv = jnp.zeros((B, CONV_W - 1, C_A), xp.dtype)
        zero_ret = jnp.zeros((B, RET_H, RET_DK, RET_DV), jnp.float32)
        xp, nc_p, ns_p = decoder_layer(xp, p_prompt[l], zero_conv, zero_ret, pos_prompt, *lw)
        xs, nc_s, ns_s = decoder_layer(xs, p_sample[l], state_conv[l], state_ret[l], pos_sample, *lw)
        conv_p.append(nc_p)
        ret_p.append(ns_p.astype(xp.dtype))
        conv_s.append(nc_s.astype(state_conv.dtype))
        ret_s.append(ns_s.astype(state_ret.dtype))
    y_prompt = rmsnorm(xp, ln_final_g)
    y_sample = rmsnorm(xs, ln_final_g)
    return (y_prompt, y_sample, jnp.stack(conv_p), jnp.stack(ret_p), jnp.stack(conv_s), jnp.stack(ret_s))
```

```python
import numpy as np
import concourse.bass as bass
import concourse.mybir as mybir
from concourse.bass_utils import run_bass_kernel_spmd

F32 = mybir.dt.float32
BF16 = mybir.dt.bfloat16
AF = mybir.ActivationFunctionType
ALU = mybir.AluOpType

D = 1024
SEQ = 2048
NCORE = 8
NSEQ_S = 16
DEC = 4
TS = NSEQ_S * DEC
CW = 31
PAST = 16384
DFF = 2816
T = 256
NT = SEQ // T
WBLK = 2048
NWST = 3
NWBF = 3
AHEAD = 2
NCAST = 1
EPS = 1e-6

O_AV, O_AG, O_Q, O_K, O_V, O_G, O_GA, O_GB = 0, 1024, 2048, 3072, 4096, 6144, 8192, 9216

C_ID = 0
C_DECP = C_ID + 128
C_DECS = C_DECP + 512
C_GLP = C_DECS + 256
C_GLS = C_GLP + 1024
C_KDP = C_GLS + 256
C_KDS = C_KDP + 4
C_MCOL = C_KDS + 4
C_MROW = C_MCOL + 16
C_ONES = C_MROW + 128
NCT = C_ONES + 128

V_LNMIX, V_LNFFN, V_LNPLE, V_CB, V_CLG, V_CLB = 0, 8, 16, 24, 32, 40
V_GNG, V_GNB = 48, 64
V_CW = 80
NV = V_CW + 8 * CW


def _log_g():
    h = np.arange(4, dtype=np.float64)
    return np.log1p(-np.exp2(-5.0 - h))


def _make_consts():
    lg = _log_g()
    ct = np.zeros((128, NCT), np.float64)
    ct[:, C_ID:C_ID + 128] = np.eye(128)
    m = np.arange(128)[:, None]
    l = np.arange(128)[None, :]
    for h in range(4):
        dec = np.where(l >= m, np.exp(lg[h] * np.maximum(l - m, 0)), 0.0)
        ct[:, C_DECP + h * 128:C_DECP + (h + 1) * 128] = dec
        ms = np.arange(64)[:, None]
        ls = np.arange(64)[None, :]
        same = (ms // 4) == (ls // 4)
        decs = np.where(same & (ls >= ms), np.exp(lg[h] * np.maximum(ls - ms, 0)), 0.0)
        ct[:64, C_DECS + h * 64:C_DECS + (h + 1) * 64] = decs
        t = np.arange(256)
        ct[:, C_GLP + h * 256:C_GLP + (h + 1) * 256] = np.exp(lg[h] * ((t % 128) + 1.0))[None, :]
        ts = np.arange(64)
        ct[:, C_GLS + h * 64:C_GLS + (h + 1) * 64] = np.exp(lg[h] * ((ts % 4) + 1.0))[None, :]
        ct[:, C_KDP + h] = np.exp(lg[h] * (127.0 - np.arange(128)))
        ct[:64, C_KDS + h] = np.exp(lg[h] * (3.0 - (np.arange(64) % 4)))
    for b in range(16):
        ct[:64, C_MCOL + b] = ((np.arange(64) // 4) == b)
    ct[:, C_MROW + 60:C_MROW + 64] = 1.0
    ct[:, C_ONES:C_ONES + 128] = 1.0 / 1024.0
    half = 128
    inv = (1.0 / (np.float32(10000.0) ** (np.arange(half, dtype=np.float32) / np.float32(half)))).astype(np.float32)
    posp = np.arange(SEQ, dtype=np.float32)
    angp = (posp[None, :] * inv[:, None]).astype(np.float32)
    rotp = np.stack([np.cos(angp), np.sin(angp)], axis=1).astype(np.float32)
    poss = (np.float32(PAST) + (np.arange(TS) % 4).astype(np.float32)).astype(np.float32)
    angs = (poss[None, :] * inv[:, None]).astype(np.float32)
    rots = np.stack([np.cos(angs), np.sin(angs)], axis=1).astype(np.float32)
    return ct.astype(np.float32), rotp, rots


class Buf:
    __slots__ = ("w", "r")

    def __init__(self):
        self.w = None
        self.r = {}


class Eng:
    def __init__(self, name, sync_self):
        self.name = name
        self.sem = None
        self.count = 0
        self.waited = {}
        self.sync_self = sync_self
        self.prog = []


class Slot:
    def __init__(self):
        self.sem = None
        self.count = 0


class Ctx:
    dry = False


def _deps(reads, writes):
    d = {}

    def add(s, v):
        if d.get(s, 0) < v:
            d[s] = v
    for b in reads:
        if b.w is not None:
            add(*b.w)
    for b in writes:
        if b.w is not None:
            add(*b.w)
        for s, v in b.r.items():
            add(s, v)
    return d


def _wait(eng, d):
    for s, v in d.items():
        if s is eng and not eng.sync_self:
            continue
        if eng.waited.get(s, 0) < v:
            eng.prog.append(("w", s, v))
            eng.waited[s] = v


def _record(ev, reads, writes):
    s, v = ev
    for b in reads:
        if b.r.get(s, 0) < v:
            b.r[s] = v
    for b in writes:
        b.w = ev
        b.r = {}


def op(eng, fn, reads=(), writes=(), inc=True):
    if Ctx.dry:
        return
    _wait(eng, _deps(reads, writes))
    eng.prog.append(("o", fn, inc))
    if inc:
        eng.count += 1
        ev = (eng, eng.count)
    else:
        ev = (eng, eng.count + 1)
    _record(ev, reads, writes)


def dma(q, slot, out, in_, reads=(), writes=()):
    if Ctx.dry:
        return
    _wait(q, _deps(reads, writes))
    q.prog.append(("d", out, in_, slot))
    slot.count += 16
    _record((slot, slot.count), reads, writes)


def inherit(new_bufs, old_bufs):
    if Ctx.dry:
        return
    d = {}
    for b in old_bufs:
        if b.w is not None and d.get(b.w[0], 0) < b.w[1]:
            d[b.w[0]] = b.w[1]
        for s, v in b.r.items():
            if d.get(s, 0) < v:
                d[s] = v
    for b in new_bufs:
        for s, v in d.items():
            if b.r.get(s, 0) < v:
                b.r[s] = v


def build_program():
    nc = bass.Bass("TRN2", target_bir_lowering=False)

    def din(name, shape):
        return nc.dram_tensor(name, list(shape), F32, kind="ExternalInput").ap()

    def dout(name, shape):
        return nc.dram_tensor(name, list(shape), F32, kind="ExternalOutput").ap()

    xp_d = din("xp", [SEQ, D])
    xs_d = din("xs", [TS, D])
    stc_d = din("stc", [NSEQ_S, 30, D])
    str_d = din("str", [NSEQ_S, 4, 256, 512])
    pp_d = din("pp", [SEQ, 256])
    ps_d = din("ps", [TS, 256])
    ctab_d = din("ctab", [128, NCT])
    rotp_d = din("rotp", [128, 2, SEQ])
    rots_d = din("rots", [128, 2, TS])
    vecs_d = din("vecs", [128, NV])
    gfin_d = din("gfin", [128, D])

    yp_d = dout("yp", [SEQ, D])
    ys_d = dout("ys", [TS, D])
    ncp_d = dout("ncp", [30, D])
    nrp_d = dout("nrp", [4, 256, 512])
    ncs_d = dout("ncs", [NSEQ_S, 30, D])
    nrs_d = dout("nrs", [NSEQ_S, 4, 256, 512])

    lg = _log_g()
    G128 = [float(np.float32(np.exp(lg[h] * 128.0))) for h in range(4)]
    G4 = [float(np.float32(np.exp(lg[h] * 4.0))) for h in range(4)]

    import contextlib
    es = contextlib.ExitStack()
    with es:
        ARENA_W = 52600
        arena = es.enter_context(nc.sbuf_tensor("arena", [128, ARENA_W], F32))
        psum = [es.enter_context(nc.psum_tensor(f"ps{i}", [128, 512], F32)) for i in range(8)]
        PSB = [Buf() for _ in range(8)]

        class Ar:
            off = 0
            peak = 0

        def alloc(nelem, dtype=F32):
            words = nelem if dtype == F32 else (nelem + 1) // 2
            words = (words + 1) // 2 * 2
            o = Ar.off
            Ar.off += words
            Ar.peak = max(Ar.peak, Ar.off)
            assert Ar.off <= ARENA_W, f"arena overflow {Ar.off}"
            v = arena[:, o:o + words]
            if dtype != F32:
                v = v.bitcast(dtype)[:, 0:nelem]
            else:
                v = v[:, 0:nelem]
            return v

        def r3(ap, a):
            return ap.rearrange("p (a b) -> p a b", a=a)

        ctab = alloc(NCT)
        vecs = alloc(NV)
        gfin = alloc(D)
        identb = alloc(128, BF16)
        rot_l = [alloc(2 * T) for _ in range(2)]
        xs_l = [alloc(2 * D) for _ in range(2)]
        pin_l = [alloc(2 * 256) for _ in range(2)]
        xn_l = [alloc(D), alloc(D)]
        xn = xn_l[0]
        hT_l = [alloc(8 * T, BF16) for _ in range(2)]
        m_a = alloc(8 * T, BF16)
        mixin = alloc(8 * T, BF16)
        small = alloc(96)
        wst = [alloc(WBLK) for _ in range(NWST)]
        wbf = [alloc(WBLK, BF16) for _ in range(NWBF)]
        S32 = alloc(8 * 512)
        Sbf = alloc(8 * 512, BF16)
        utail = alloc(8 * 64)
        utok = xn
        epsc = alloc(2)
        eps5 = alloc(2)
        mhalf = alloc(2)
        uprev = alloc(8 * 32, BF16)
        mark = Ar.off
        qr = alloc(8 * T, BF16)
        kr = alloc(8 * T, BF16)
        qdec = alloc(8 * T, BF16)
        rtmp = alloc(4 * T)
        v_sb = alloc(2 * 2048, BF16)
        kd = alloc(1024, BF16)
        sT = [alloc(128, BF16) for _ in range(4)]
        on = alloc(2048, BF16)
        sgg = alloc(2 * T)
        zT = alloc(16 * T, BF16)
        retc_end = Ar.off
        sg = alloc(2 * T)
        ucat = alloc(8 * 544, BF16)
        c_sb = alloc(8 * T)
        csq = alloc(2 * T)
        lnA = alloc(T)
        lnB = alloc(T)
        mean_sb = alloc(T)
        tmpc = alloc(2 * T)
        ca = alloc(8 * T, BF16)
        sga = alloc(8 * T, BF16)
        scg = alloc(D)
        conv_end = Ar.off
        Ar.off = retc_end
        S0 = [alloc(2 * 512) for _ in range(6)]
        S0b = [alloc(2 * 512, BF16) for _ in range(2)]
        kdx = [alloc(1024, BF16) for _ in range(2)]
        qxb = [alloc(8 * 64, BF16) for _ in range(2)]
        ret_end = Ar.off
        Ar.off = mark
        act = alloc(22 * T, BF16)
        sgate = alloc(2 * T)
        pT = alloc(2 * T, BF16)
        pproj = alloc(2 * D)
        gate_sb = [alloc(512) for _ in range(2)]
        ffn_end = Ar.off
        assert ffn_end <= retc_end, (ffn_end, retc_end)
        print("arena peak words", Ar.peak, "of", ARENA_W)

        ident = ctab[:, C_ID:C_ID + 128]
        ones_s = ctab[:, C_ONES:C_ONES + 128]
        hT3_l = [r3(hT_l[0], 8), r3(hT_l[1], 8)]
        m_a3 = r3(m_a, 8)
        mixin3 = r3(mixin, 8)
        S32_3 = r3(S32, 8)
        Sbf3 = r3(Sbf, 8)
        utail3 = r3(utail, 8)
        uprev3 = r3(uprev, 8)
        sg3 = r3(sg, 2)
        ucat3 = r3(ucat, 8)
        c_sb3 = r3(c_sb, 8)
        csq3 = r3(csq, 2)
        tmpc3 = r3(tmpc, 2)
        ca3 = r3(ca, 8)
        sga3 = r3(sga, 8)
        qr3 = r3(qr, 8)
        kr3 = r3(kr, 8)
        qdec3 = r3(qdec, 8)
        rtmp3 = r3(rtmp, 4)
        v_sb3 = r3(v_sb, 2)
        sgg3 = r3(sgg, 2)
        zT3 = r3(zT, 16)
        act3 = r3(act, 22)
        sgate3 = r3(sgate, 2)
        pT3 = r3(pT, 2)
        pproj3 = r3(pproj, 2)

        def vcol(off, c):
            return vecs[:, off + c:off + c + 1]

        B = {}

        def nb(name, n=None):
            if n is None:
                B[name] = Buf()
            else:
                B[name] = [Buf() for _ in range(n)]
            return B[name]

        b_const = nb("const")
        b_rot_l = [Buf(), Buf()]
        b_xs_l = [[Buf(), Buf()], [Buf(), Buf()]]
        b_pin_l = [Buf(), Buf()]
        b_xn_l = [Buf(), Buf()]
        b_xn = b_xn_l[0]
        b_hT_l = [[Buf() for _ in range(8)], [Buf() for _ in range(8)]]
        b_ma = nb("m_a", 8)
        b_mixin = nb("mixin", 8)
        b_small_l = [Buf() for _ in range(6)]
        b_wst = nb("wst", NWST)
        b_wbf = nb("wbf", NWBF)
        b_S32 = nb("S32", 8)
        b_Sbf = nb("Sbf", 8)
        b_utail = nb("utail")
        b_utok = b_xn
        b_uprev = nb("uprev", 8)
        b_sg = nb("sg", 2)
        b_ucat = nb("ucat", 8)
        b_csb = nb("c_sb", 8)
        b_csq = nb("csq", 2)
        b_ln = nb("ln")
        b_tmpc = nb("tmpc", 2)
        b_ca = nb("ca", 8)
        b_scg = nb("scg")
        b_sga = nb("sga", 8)
        G_CONV = b_sg + b_ucat + b_csb + b_csq + [b_ln] + b_tmpc + b_ca + [b_scg] + b_sga
        b_qr = nb("qr", 8)
        b_kr = nb("kr", 8)
        b_qdec = nb("qdec", 8)
        b_rtmp = nb("rtmp", 4)
        b_v = nb("v", 8)
        b_kd = nb("kd")
        b_sT = nb("sT", 4)
        b_on = nb("on", 4)
        b_sgg = nb("sgg", 2)
        b_zT = nb("zT", 16)
        b_S0 = nb("S0", 6)
        b_S0b = nb("S0b", 2)
        b_kdx = nb("kdx", 2)
        b_qxb = nb("qxb", 2)
        G_RETC = b_qr + b_kr + b_qdec + b_rtmp + b_v + [b_kd] + b_sT + b_on + b_sgg + b_zT
        G_SAMP = b_S0 + b_S0b + b_kdx + b_qxb
        b_act = nb("act", 22)
        b_sgate = nb("sgate", 2)
        b_pT = nb("pT")
        b_pproj = nb("pproj")
        b_gsb = nb("gate_sb", 2)
        G_FFN = b_act + b_sgate + [b_pT, b_pproj] + b_gsb

        PE = Eng("pe", False)
        ACT = Eng("act", True)
        DVE = Eng("dve", True)
        POOL = Eng("pool", True)
        SP = Eng("sp", False)
        slots = {}

        def slot(name):
            if name not in slots:
                slots[name] = Slot()
            return slots[name]

        class PsA:
            i = 0
            held = set()

        def ps_get():
            while True:
                b = PsA.i % 8
                PsA.i += 1
                if b not in PsA.held:
                    return b

        def PSf(b):
            return psum[b][:, :]

        def PSh(b):
            return psum[b][:, :].bitcast(BF16)

        class WS:
            specs = []
            issued = 0
            cons = 0
            nb = 0
            cache = None
            cidx = {}
            wall = None
            nst = 0
            ncast = 0
            dma_issued = 0
            stg = {}

        NWX = NWBF + 2 * NWST
        wbx = list(wbf) + [wst[i][:, h * (WBLK // 2):(h + 1) * (WBLK // 2)].bitcast(BF16) for i in range(NWST) for h in range(2)]
        b_wbx = list(b_wbf) + [Buf() for _ in range(2 * NWST)]
        b_wc = {}

        def wslot(j):
            if j < NCAST * WS.nb:
                return wbf[j % NWBF], b_wbf[j % NWBF]
            return wbx[j % NWX], b_wbx[j % NWX]

        def ws_meta(j):
            name, k0, nk, c0, ncol = WS.specs[j]
            jj = WS.cidx[WS.specs[j]]
            t = j // WS.nb
            ctile = 0 if name == "convD" else jj % NCAST
            return name, k0, nk, c0, ncol, jj, t, ctile

        def ws_is_fp32(j):
            name, k0, nk, c0, ncol, jj, t, ctile = ws_meta(j)
            return not (t > ctile or name == "convD")

        def ws_issue_dma(j):
            name, k0, nk, c0, ncol, jj, t, ctile = ws_meta(j)
            if t > ctile or name == "convD":
                return
            n = nk * ncol
            si = WS.nst % NWST
            WS.nst += 1
            WS.stg[j] = si
            dma(SP, slot(f"wst{si}"), wst[si][:, 0:n], WS.wall[jj, :, 0:n], writes=[b_wst[si]])

        def ws_issue_cast(j):
            name, k0, nk, c0, ncol, jj, t, ctile = ws_meta(j)
            n = nk * ncol
            bi = j % NWBF
            o_ = wbf[bi][:, 0:n]
            if t > ctile:
                dma(SP, slot(f"wlc{bi}"), o_, WS.cache[jj, :, 0:n], reads=[b_wc[jj]], writes=[b_wbf[bi]])
                return
            if name == "convD":
                wcol = vecs[:, V_CW + k0 * CW + c0:V_CW + k0 * CW + c0 + nk]
                op(POOL, lambda h, o_=o_, wcol=wcol, nk=nk: h.tensor_tensor(
                    out=o_.rearrange("p (j q) -> p j q", j=nk), in0=ident.unsqueeze(1).to_broadcast([128, nk, 128]),
                    in1=wcol.unsqueeze(2).to_broadcast([128, nk, 128]), op=ALU.mult),
                   reads=[b_const], writes=[b_wbf[bi]])
            else:
                si = WS.stg.pop(j)
                i_ = wst[si][:, 0:n]
                WS.ncast += 1
                if WS.ncast % 2 == 0:
                    op(ACT, lambda h, o_=o_, i_=i_: h.activation(out=o_, in_=i_, func=AF.Copy),
                       reads=[b_wst[si]], writes=[b_wbf[bi]])
                else:
                    op(DVE, lambda h, o_=o_, i_=i_: h.tensor_copy(out=o_, in_=i_),
                       reads=[b_wst[si]], writes=[b_wbf[bi]])
            if t == ctile:
                b_wc[jj] = Buf()
                dma(ACT, slot(f"wcw{bi}"), WS.cache[jj, :, 0:n], o_, reads=[b_wbf[bi]], writes=[b_wc[jj]])

        def ws_issue_cached(j):
            name, k0, nk, c0, ncol, jj, t, ctile = ws_meta(j)
            n = nk * ncol
            if j == NCAST * WS.nb:
                inherit(b_wbx[NWBF:], b_wst)
            ap_, bf_ = wslot(j)
            dma(SP, slot(f"wl{j % NWX}"), ap_[:, 0:n], WS.cache[jj, :, 0:n], reads=[b_wc[jj]], writes=[bf_])

        def ws_next(name, k0, nk, c0, ncol):
            spec = (name, k0, nk, c0, ncol)
            assert nk * ncol <= WBLK
            i = WS.cons
            WS.cons += 1
            if Ctx.dry:
                WS.specs.append(spec)
                return None, None
            assert WS.specs[i] == spec, (i, WS.specs[i], spec)
            ncr = NCAST * WS.nb
            if i < ncr:
                lim_d = min(ncr, i + NWST + 1)
                lim_c = min(ncr, i + 2)
                while True:
                    if WS.issued < lim_c and WS.dma_issued > WS.issued:
                        ws_issue_cast(WS.issued)
                        WS.issued += 1
                    elif WS.dma_issued < lim_d and (WS.nst - WS.ncast < NWST or not ws_is_fp32(WS.dma_issued)):
                        ws_issue_dma(WS.dma_issued)
                        WS.dma_issued += 1
                    else:
                        break
                assert WS.issued > i
            else:
                lim = min(len(WS.specs), i + NWX - 1)
                while WS.issued < lim:
                    ws_issue_cached(WS.issued)
                    WS.issued += 1
            ap_, bf_ = wslot(i)
            v = ap_[:, 0:nk * ncol].rearrange("p (k c) -> p k c", k=nk)
            return v, bf_

        def mm_group(out_ap, out_buf, terms, first=True, last=True):
            n = len(terms)
            for i, (l_, r_, rd) in enumerate(terms):
                st = first and i == 0
                sp = last and i == n - 1
                op(PE, lambda h, l_=l_, r_=r_, st=st, sp=sp: h.matmul(out_ap, l_, r_, start=st, stop=sp),
                   reads=list(rd), writes=[out_buf], inc=(i == n - 1))

        def transpose(out_ap, out_buf, in_ap, in_bufs, idn, inc=True):
            op(PE, lambda h: h.transpose(out_ap, in_ap, idn), reads=list(in_bufs) + [b_const],
               writes=[out_buf], inc=inc)

        def rms_stats(x_ap, npart, xbufs, si=0):
            o = si * 20
            bs = b_small_l[si]
            sm = small[:npart, o:o + 20]
            op(DVE, lambda h: h.bn_stats(out=sm[:, 0:6], in_=x_ap[:, 0:512]), reads=xbufs, writes=[bs])
            op(DVE, lambda h: h.bn_stats(out=sm[:, 6:12], in_=x_ap[:, 512:1024]), reads=xbufs, writes=[bs])
            op(DVE, lambda h: h.bn_aggr(out=sm[:, 12:14], in_=sm[:, 0:12]), reads=[bs], writes=[bs])
            op(DVE, lambda h: h.scalar_tensor_tensor(out=sm[:, 14:15], in0=sm[:, 12:13], scalar=sm[:, 12:13], in1=sm[:, 13:14],
                                                     op0=ALU.mult, op1=ALU.add), reads=[bs], writes=[bs])
            op(DVE, lambda h: h.tensor_scalar(out=sm[:, 15:16], in0=sm[:, 14:15], scalar1=EPS, scalar2=None, op0=ALU.add),
               reads=[bs], writes=[bs])
            op(POOL, lambda h: h.tensor_tensor(out=sm[:, 16:17], in0=sm[:, 15:16], in1=mhalf[:npart, 0:1], op=ALU.pow),
               reads=[bs, b_const], writes=[bs])
            return sm[:, 16:17], bs

        class Cur:
            xs3 = None
            b_xs = None
            hT3 = None
            b_hT = None
            par = 0
            nxt = None
            normed = False
            pending = []

        def flush_pending():
            for f_ in Cur.pending:
                f_()
            Cur.pending = []

        def norm_to_hT(subs, voff, xs3=None, b_xs=None, hT3=None, b_hT=None, filler=None):
            if xs3 is None:
                xs3, b_xs, hT3, b_hT = Cur.xs3, Cur.b_xs, Cur.hT3, Cur.b_hT
            for s, (t0, npart) in enumerate(subs):
                xa = xs3[:npart, s, :]
                rstd, bs = rms_stats(xa, npart, [b_xs[s]], si=s)
                xnv = xn_l[s]
                op(DVE, lambda h, xa=xa, rstd=rstd, npart=npart, xnv=xnv: h.tensor_scalar(
                    out=xnv[:npart, :], in0=xa, scalar1=rstd, scalar2=None, op0=ALU.mult),
                   reads=[b_xs[s], bs], writes=[b_xn_l[s]])
            if filler is not None:
                filler()
            for s, (t0, npart) in enumerate(subs):
                xnv = xn_l[s]
                for half in range(2):
                    pb = ps_get()
                    for i in range(4):
                        c = half * 4 + i
                        transpose(PSf(pb)[:, i * 128:i * 128 + npart], PSB[pb],
                                  xnv[:npart, c * 128:(c + 1) * 128], [b_xn_l[s]], ident[:npart, :npart], inc=(i == 3))
                    for i in range(4):
                        c = half * 4 + i
                        op(ACT, lambda h, pb=pb, i=i, c=c, t0=t0, npart=npart: h.activation(
                            out=hT3[:, c, t0:t0 + npart], in_=PSf(pb)[:, i * 128:i * 128 + npart],
                            func=AF.Identity, scale=vcol(voff, c), bias=0.0),
                           reads=[PSB[pb], b_const], writes=[b_hT[c]])

        def PS2(pb, TT):
            return psum[pb][:, :].rearrange("p (a t) -> p a t", a=2)[:, :, 0:TT]

        def fm_proj2(name, c0, in3, in_bufs, TT, evac2):
            wv, wb = ws_next(name, 0, 8, c0, 256)
            pb = ps_get()
            if not Ctx.dry:
                for j in range(2):
                    mm_group(PSf(pb)[:, j * 256:j * 256 + TT], PSB[pb],
                             [(wv[:, k, j * 128:(j + 1) * 128], in3[:, k, 0:TT], [wb, in_bufs[k]]) for k in range(8)])
            evac2(pb)

        def fm_proj(name, c0, ncols, in3, in_bufs, nk, TT, evac, kblk=None):
            bc = (WBLK // nk) // 128 * 128
            assert ncols % bc == 0
            oc = 0
            for cb in range(ncols // bc):
                wv, wb = ws_next(name, 0, nk, c0 + cb * bc, bc)
                for j in range(bc // 128):
                    pb = ps_get()
                    if not Ctx.dry:
                        mm_group(PSf(pb)[:, 0:TT], PSB[pb],
                                 [(wv[:, k, j * 128:(j + 1) * 128], in3[:, k, 0:TT], [wb, in_bufs[k]]) for k in range(nk)])
                    evac(oc, pb)
                    oc += 1

        def tm_proj(name, c0, ncols, in3, in_bufs, nk, subs, evac, k_split=4):
            for cg in range(ncols // 512):
                pbs = [ps_get() for _ in subs]
                for pb in pbs:
                    PsA.held.add(pb)
                kb = 0
                nblk = (nk + k_split - 1) // k_split
                for bi in range(nblk):
                    k0 = bi * k_split
                    kk = min(k_split, nk - k0)
                    wv, wb = ws_next(name, k0, kk, c0 + cg * 512, 512)
                    if Ctx.dry:
                        continue
                    for s, (t0, npart) in enumerate(subs):
                        mm_group(PSf(pbs[s])[:npart, :], PSB[pbs[s]],
                                 [(in3[:, k0 + k, t0:t0 + npart], wv[:, k, :], [wb, in_bufs[k0 + k]]) for k in range(kk)],
                                 first=(bi == 0), last=(bi == nblk - 1))
                for s, (t0, npart) in enumerate(subs):
                    evac(cg, s, pbs[s], npart)
                for pb in pbs:
                    PsA.held.discard(pb)

        def load_tile(sample, ti, par):
            xs3 = r3(xs_l[par], 2)
            pin3 = r3(pin_l[par], 2)
            rot3 = r3(rot_l[par], 2)
            b_xs = b_xs_l[par]
            if sample:
                dma(SP, slot(f"xs{par}_0"), xs3[:TS, 0, :], xs_d[:, :], writes=[b_xs[0]])
                dma(SP, slot(f"pin{par}"), pin3[:TS, 0, :], ps_d[:, :], writes=[b_pin_l[par]])
                dma(SP, slot(f"rot{par}"), rot3[:, :, 0:TS], rots_d[:, :, :], writes=[b_rot_l[par]])
            else:
                r0 = ti * T
                for s in range(2):
                    dma(SP, slot(f"xs{par}_{s}"), xs3[:, s, :], xp_d[r0 + s * 128:r0 + (s + 1) * 128, :], writes=[b_xs[s]])
                dma(SP, slot(f"pin{par}"), pin3[:, :, :], pp_d[r0:r0 + T, :].rearrange("(s p) c -> p s c", p=128),
                    writes=[b_pin_l[par]])
                dma(SP, slot(f"rot{par}"), rot3[:, :, 0:T], rotp_d[:, :, r0:r0 + T], writes=[b_rot_l[par]])

        def do_tile(sample, ti):
            TT = TS if sample else T
            subs = [(0, TS)] if sample else [(0, 128), (128, 128)]
            NS = len(subs)
            last_prompt = (not sample) and ti == NT - 1
            first_prompt = (not sample) and ti == 0

            par = Cur.par
            xs3 = r3(xs_l[par], 2)
            pin3 = r3(pin_l[par], 2)
            rot3 = r3(rot_l[par], 2)
            b_xs = b_xs_l[par]
            b_pin = b_pin_l[par]
            b_rot = b_rot_l[par]
            Cur.xs3 = xs3
            Cur.b_xs = b_xs
            hT3 = hT3_l[par]
            b_hT = b_hT_l[par]
            Cur.hT3 = hT3
            Cur.b_hT = b_hT
            cosT = rot3[:, 0, 0:TT]
            sinT = rot3[:, 1, 0:TT]

            if not Cur.normed:
                norm_to_hT(subs, V_LNMIX)

            if sample:
                ucat5 = ucat3.rearrange("p c (b w) -> p c b w", w=34)
            need_tail = sample or last_prompt
            ntail = TS if sample else 32

            def u_dst(c):
                if sample:
                    return ucat5[:, c, :, 30:34]
                return ucat3[:, c, 30:30 + T]

            def p_conv_hist():
                if sample:
                    for g in range(4):
                        dma(SP, slot("scg"), scg[:120, :], stc_d[4 * g:4 * g + 4, :, :].rearrange("b r c -> (b r) c"),
                            writes=[b_scg])
                        for half in range(2):
                            pb = ps_get()
                            for i in range(4):
                                c = half * 4 + i
                                transpose(PSf(pb)[:, i * 128:i * 128 + 120], PSB[pb], scg[:120, c * 128:(c + 1) * 128],
                                          [b_scg], ident[:120, :120], inc=(i == 3))
                            for i in range(4):
                                c = half * 4 + i
                                op(ACT, lambda h, pb=pb, i=i, c=c, g=g: h.activation(
                                    out=ucat5[:, c, 4 * g:4 * g + 4, 0:30],
                                    in_=PSf(pb)[:, i * 128:i * 128 + 120].rearrange("p (b r) -> p b r", r=30),
                                    func=AF.Copy), reads=[PSB[pb]], writes=[b_ucat[c]])
                    dma(SP, slot("ncs0"), ncs_d[:, 0:26, :], stc_d[:, 4:30, :])
                else:
                    for c in range(8):
                        if first_prompt:
                            op(POOL, lambda h, c=c: h.memset(uprev3[:, c, :], 0.0), writes=[b_uprev[c]])
                        op(POOL, lambda h, c=c: h.tensor_copy(out=ucat3[:, c, 0:30], in_=uprev3[:, c, 0:30]),
                           reads=[b_uprev[c]], writes=[b_ucat[c]])

            def g_agav():
                for jb in range(4):
                    def ev_gate(pb, jb=jb):
                        op(ACT, lambda h: h.activation(out=sg3[:, :, 0:TT], in_=PS2(pb, TT), func=AF.Sigmoid),
                           reads=[PSB[pb]], writes=b_sg)
                    fm_proj2("w_in", O_AG + jb * 256, hT3, b_hT, TT, ev_gate)
                    yield

                    def ev_val(pb, jb=jb):
                        c0_ = jb * 2
                        if sample:
                            for i in range(2):
                                c = c0_ + i
                                src = PSf(pb)[:, i * 256:i * 256 + TT].rearrange("p (b t) -> p b t", t=4)
                                s2 = sg3[:, i, 0:TT].rearrange("p (b t) -> p b t", t=4)
                                op(DVE, lambda h, c=c, src=src, s2=s2: h.tensor_tensor(out=u_dst(c), in0=src, in1=s2, op=ALU.mult),
                                   reads=[PSB[pb], b_sg[i]], writes=[b_ucat[c]])
                        else:
                            op(DVE, lambda h: h.tensor_tensor(out=ucat3[:, c0_:c0_ + 2, 30:30 + T], in0=PS2(pb, TT),
                                                             in1=sg3[:, :, 0:TT], op=ALU.mult),
                               reads=[PSB[pb]] + b_sg, writes=[b_ucat[c0_], b_ucat[c0_ + 1]])
                            op(POOL, lambda h: h.tensor_copy(out=uprev3[:, c0_:c0_ + 2, 0:30], in_=ucat3[:, c0_:c0_ + 2, T:T + 30]),
                               reads=[b_ucat[c0_], b_ucat[c0_ + 1]], writes=[b_uprev[c0_], b_uprev[c0_ + 1]])
                        if need_tail:
                            op(DVE, lambda h: h.tensor_tensor(out=utail3[:, c0_:c0_ + 2, 0:ntail], in0=PS2(pb, TT)[:, :, TT - ntail:TT],
                                                             in1=sg3[:, :, TT - ntail:TT], op=ALU.mult),
                               reads=[PSB[pb]] + b_sg, writes=[b_utail])
                    fm_proj2("w_in", O_AV + jb * 256, hT3, b_hT, TT, ev_val)
                    yield

            def p_tail():
                if not need_tail:
                    return
                nrow = TS if sample else 30
                c0t = 0 if sample else 2
                for half in range(2):
                    pb = ps_get()
                    for i in range(4):
                        c = half * 4 + i
                        transpose(PSf(pb)[:nrow, i * 128:(i + 1) * 128], PSB[pb], utail3[:, c, c0t:c0t + nrow],
                                  [b_utail], ident, inc=(i == 3))
                    op(ACT, lambda h, pb=pb, half=half, nrow=nrow: h.activation(
                        out=utok[:nrow, half * 512:(half + 1) * 512], in_=PSf(pb)[:nrow, :], func=AF.Copy),
                       reads=[PSB[pb]], writes=[b_utok])
                if sample:
                    for b in range(NSEQ_S):
                        dma(ACT, slot("ncs1"), ncs_d[b, 26:30, :], utok[4 * b:4 * b + 4, :], reads=[b_utok])
                else:
                    dma(ACT, slot("ncp"), ncp_d[:, :], utok[:30, :], reads=[b_utok])

            st_banks = {}

            def g_conv_chunks():
                pm = ps_get()
                PsA.held.add(pm)
                pe2 = ps_get()
                PsA.held.add(pe2)
                st_banks["pm"] = pm
                st_banks["pe2"] = pe2
                for c in range(8):
                    pb = ps_get()
                    for part, (j0, nj) in enumerate([(0, 16), (16, 15)]):
                        Dv, Db = ws_next("convD", c, nj, j0, 128)
                        if Ctx.dry:
                            continue
                        terms = []
                        for jj in range(nj):
                            j = j0 + jj
                            if sample:
                                rhs = ucat5[:, c, :, j:j + 4]
                            else:
                                rhs = ucat3[:, c, j:j + T]
                            terms.append((Dv[:, jj, :], rhs, [Db, b_ucat[c]]))
                        mm_group(PSf(pb)[:, 0:TT], PSB[pb], terms, first=(part == 0), last=(part == 1))
                        if part == 0 and not sample:
                            yield
                    qi = c % 2
                    op(ACT, lambda h, pb=pb, c=c: h.activation(out=c_sb3[:, c, 0:TT], in_=PSf(pb)[:, 0:TT], func=AF.Identity,
                                                             bias=vcol(V_CB, c), scale=1.0),
                       reads=[PSB[pb], b_const], writes=[b_csb[c]])
                    op(ACT, lambda h, pb=pb, c=c, qi=qi: h.activation(out=csq3[:, qi, 0:TT], in_=PSf(pb)[:, 0:TT], func=AF.Square,
                                                                    bias=vcol(V_CB, c), scale=1.0),
                       reads=[PSB[pb], b_const], writes=[b_csq[qi]])
                    def stats(cc):
                        qq = cc % 2
                        mm_group(PSf(pm)[:, 0:TT], PSB[pm], [(ones_s, c_sb3[:, cc, 0:TT], [b_const, b_csb[cc]])],
                                 first=(cc == 0), last=(cc == 7))
                        mm_group(PSf(pe2)[:, 0:TT], PSB[pe2], [(ones_s, csq3[:, qq, 0:TT], [b_const, b_csq[qq]])],
                                 first=(cc == 0), last=(cc == 7))
                    if c > 0:
                        stats(c - 1)
                    if c == 7:
                        yield
                        stats(7)
                    yield

            def g_ln():
                pm = st_banks["pm"]
                pe2 = st_banks["pe2"]
                op(ACT, lambda h: h.activation(out=mean_sb[:, 0:TT], in_=PSf(pm)[:, 0:TT], func=AF.Copy),
                   reads=[PSB[pm]], writes=[b_ln])
                op(DVE, lambda h: h.tensor_tensor(out=lnB[:, 0:TT], in0=mean_sb[:, 0:TT], in1=mean_sb[:, 0:TT], op=ALU.mult),
                   reads=[b_ln], writes=[b_ln])
                op(DVE, lambda h: h.tensor_tensor(out=lnA[:, 0:TT], in0=PSf(pe2)[:, 0:TT], in1=lnB[:, 0:TT], op=ALU.subtract),
                   reads=[b_ln, PSB[pe2]], writes=[b_ln])
                op(ACT, lambda h: h.activation(out=lnA[:, 0:TT], in_=lnA[:, 0:TT], func=AF.Sqrt, bias=eps5[:, 0:1], scale=1.0),
                   reads=[b_ln, b_const], writes=[b_ln])
                op(DVE, lambda h: h.reciprocal(out=lnA[:, 0:TT], in_=lnA[:, 0:TT]), reads=[b_ln], writes=[b_ln])
                op(DVE, lambda h: h.scalar_tensor_tensor(out=lnB[:, 0:TT], in0=mean_sb[:, 0:TT], scalar=-1.0, in1=lnA[:, 0:TT],
                                                         op0=ALU.mult, op1=ALU.mult), reads=[b_ln], writes=[b_ln])
                PsA.held.discard(pm)
                PsA.held.discard(pe2)
                yield
                for c in range(8):
                    qi = c % 2
                    op(DVE, lambda h, c=c, qi=qi: h.tensor_tensor(out=tmpc3[:, qi, 0:TT], in0=c_sb3[:, c, 0:TT], in1=lnA[:, 0:TT],
                                                                op=ALU.mult), reads=[b_csb[c], b_ln], writes=[b_tmpc[qi]])
                    op(DVE, lambda h, qi=qi: h.tensor_tensor(out=tmpc3[:, qi, 0:TT], in0=tmpc3[:, qi, 0:TT], in1=lnB[:, 0:TT],
                                                           op=ALU.add), reads=[b_tmpc[qi], b_ln], writes=[b_tmpc[qi]])
                    op(ACT, lambda h, c=c, qi=qi: h.activation(out=ca3[:, c, 0:TT], in_=tmpc3[:, qi, 0:TT], func=AF.Silu,
                                                             scale=vcol(V_CLG, c), bias=vcol(V_CLB, c)),
                       reads=[b_tmpc[qi], b_const], writes=[b_ca[c]])
                    yield

            def g_gate_a():
                for jb in range(4):
                    def ev_ga(pb, jb=jb):
                        c0_ = jb * 2
                        op(ACT, lambda h: h.activation(out=sga3[:, c0_:c0_ + 2, 0:TT], in_=PS2(pb, TT), func=AF.Sigmoid),
                           reads=[PSB[pb]], writes=[b_sga[c0_], b_sga[c0_ + 1]])
                    fm_proj2("w_in", O_GA + jb * 256, hT3, b_hT, TT, ev_ga)
                    yield

            def p_conv_out():
                for jb in range(4):
                    def ev_ya(pb, jb=jb):
                        c0_ = jb * 2
                        op(DVE, lambda h: h.tensor_tensor(out=m_a3[:, c0_:c0_ + 2, 0:TT], in0=PS2(pb, TT), in1=sga3[:, c0_:c0_ + 2, 0:TT],
                                                         op=ALU.mult),
                           reads=[PSB[pb], b_sga[c0_], b_sga[c0_ + 1]], writes=[b_ma[c0_], b_ma[c0_ + 1]])
                    fm_proj2("w_co", jb * 256, ca3, b_ca, TT, ev_ya)

            def g_qk():
                for which, (off, dst3, dbufs, scl) in enumerate([(O_Q, qr3, b_qr, 1.0), (O_K, kr3, b_kr, 0.0625)]):
                    for hh in range(4):
                        def ev_rot(pb, hh=hh, dst3=dst3, dbufs=dbufs, scl=scl):
                            x1 = PSf(pb)[:, 0:TT]
                            x2 = PSf(pb)[:, 256:256 + TT]
                            t = [rtmp3[:, i, 0:TT] for i in range(4)]
                            for i, (xx, tab) in enumerate([(x1, cosT), (x2, sinT), (x1, sinT), (x2, cosT)]):
                                op(DVE, lambda h, i=i, xx=xx, tab=tab: h.scalar_tensor_tensor(
                                    out=t[i], in0=xx, scalar=scl, in1=tab, op0=ALU.mult, op1=ALU.mult),
                                   reads=[PSB[pb], b_rot], writes=[b_rtmp[i]])
                            op(POOL, lambda h: h.tensor_tensor(out=dst3[:, 2 * hh, 0:TT], in0=t[0], in1=t[1], op=ALU.subtract),
                               reads=[b_rtmp[0], b_rtmp[1]], writes=[dbufs[2 * hh]])
                            op(POOL, lambda h: h.tensor_tensor(out=dst3[:, 2 * hh + 1, 0:TT], in0=t[2], in1=t[3], op=ALU.add),
                               reads=[b_rtmp[2], b_rtmp[3]], writes=[dbufs[2 * hh + 1]])
                        fm_proj2("w_in", off + hh * 256, hT3, b_hT, TT, ev_rot)
                        if which == 0:
                            gl_off = C_GLS if sample else C_GLP
                            gt = ctab[:, gl_off + hh * TT:gl_off + (hh + 1) * TT]
                            for dc in range(2):
                                c = 2 * hh + dc
                                op(POOL, lambda h, c=c, gt=gt: h.tensor_tensor(out=qdec3[:, c, 0:TT], in0=qr3[:, c, 0:TT], in1=gt,
                                                                             op=ALU.mult),
                                   reads=[b_qr[c], b_const], writes=[b_qdec[c]])
                        yield

            def p_v():
                for hh in range(4):
                    def ev_v(cg, s, pb, npart, hh=hh):
                        op(ACT, lambda h: h.activation(out=v_sb3[:npart, s, hh * 512:(hh + 1) * 512], in_=PSf(pb)[:npart, :],
                                                       func=AF.Copy), reads=[PSB[pb]], writes=[b_v[s * 4 + hh]])
                    tm_proj("w_in", O_V + hh * 512, 512, hT3, b_hT, 8, subs, ev_v)

            def g_retention():
                for s, (t0, npart) in enumerate(subs):
                    yield from retention_prompt(s, t0)

            def g_g():
                for jb in range(8):
                    def ev_g(pb, jb=jb):
                        c0_ = jb * 2
                        op(ACT, lambda h: h.activation(out=sgg3[:, :, 0:TT], in_=PS2(pb, TT), func=AF.Silu),
                           reads=[PSB[pb]], writes=b_sgg)
                        op(DVE, lambda h: h.tensor_tensor(out=zT3[:, c0_:c0_ + 2, 0:TT], in0=zT3[:, c0_:c0_ + 2, 0:TT],
                                                         in1=sgg3[:, :, 0:TT], op=ALU.mult),
                           reads=[b_zT[c0_], b_zT[c0_ + 1]] + b_sgg, writes=[b_zT[c0_], b_zT[c0_ + 1]])
                    fm_proj2("w_in", O_G + jb * 256, hT3, b_hT, TT, ev_g)
                    yield

            def p_yb():
                for jb in range(4):
                    def ev_gb(pb):
                        op(ACT, lambda h: h.activation(out=sgg3[:, :, 0:TT], in_=PS2(pb, TT), func=AF.Sigmoid),
                           reads=[PSB[pb]], writes=b_sgg)
                    fm_proj2("w_in", O_GB + jb * 256, hT3, b_hT, TT, ev_gb)

                    def ev_yb(oc, pb, jb=jb):
                        i = oc % 2
                        c = jb * 2 + i
                        op(DVE, lambda h: h.tensor_tensor(out=sgg3[:, i, 0:TT], in0=PSf(pb)[:, 0:TT], in1=sgg3[:, i, 0:TT],
                                                         op=ALU.mult), reads=[PSB[pb], b_sgg[i]], writes=[b_sgg[i]])
                        op(DVE, lambda h: h.tensor_tensor(out=mixin3[:, c, 0:TT], in0=sgg3[:, i, 0:TT], in1=m_a3[:, c, 0:TT],
                                                         op=ALU.add), reads=[b_sgg[i], b_ma[c]], writes=[b_mixin[c]])
                    fm_proj("w_ro", jb * 256, 256, zT3, b_zT, 16, TT, ev_yb)

            def run(g):
                for _ in g:
                    pass

            def rr(*gens):
                gens = list(gens)
                while gens:
                    for g in list(gens):
                        try:
                            next(g)
                        except StopIteration:
                            gens.remove(g)

            def chain(*gens):
                for g in gens:
                    yield from g

            if sample:
                inherit(G_CONV, G_RETC + G_SAMP + G_FFN)
                p_conv_hist()
                run(g_agav())
                flush_pending()
                p_tail()
                run(g_conv_chunks())
                run(g_gate_a())
                run(g_ln())
                p_conv_out()
                inherit(G_RETC + G_SAMP, G_CONV + G_FFN)
                run(g_qk())
                p_v()
                retention_sample()
                run(g_g())
                p_yb()
            else:
                inherit(G_CONV + G_RETC, G_FFN + G_SAMP)
                p_conv_hist()
                rr(g_agav(), g_qk())
                flush_pending()
                p_tail()
                p_v()
                if first_prompt:
                    for i in range(8):
                        op(POOL, lambda h, i=i: h.memset(S32_3[:, i, :], 0.0), writes=[b_S32[i]])
                        op(POOL, lambda h, i=i: h.memset(Sbf3[:, i, :], 0.0), writes=[b_Sbf[i]])
                rr(g_conv_chunks(), g_retention())
                rr(g_ln(), chain(g_g(), g_gate_a()))
                p_conv_out()
                p_yb()
                if last_prompt:
                    for hh in range(4):
                        dma(ACT, slot("nrp"), nrp_d[hh, :, :].rearrange("(dc p) v -> p dc v", p=128),
                            S32_3[:, 2 * hh:2 * hh + 2, :], reads=[b_S32[2 * hh], b_S32[2 * hh + 1]])

            def ev_res(cg, s, pb, npart):
                xa = xs3[:npart, s, cg * 512:(cg + 1) * 512]
                op(DVE, lambda h: h.tensor_tensor(out=xa, in0=PSf(pb)[:npart, :], in1=xa, op=ALU.add),
                   reads=[PSB[pb], b_xs[s]], writes=[b_xs[s]])
            tm_proj("w_o", 0, D, mixin3, b_mixin, 8, subs, ev_res)

            inherit(G_FFN, G_CONV + G_RETC + G_SAMP)
            if Cur.nxt is not None:
                load_tile(Cur.nxt[0], Cur.nxt[1], 1 - par)
            norm_to_hT(subs, V_LNFFN)
            for jb in range(11):
                def ev_fg(pb):
                    op(ACT, lambda h: h.activation(out=sgate3[:, :, 0:TT], in_=PS2(pb, TT), func=AF.Silu),
                       reads=[PSB[pb]], writes=b_sgate)
                fm_proj2("w_fg", jb * 256, hT3, b_hT, TT, ev_fg)

                def ev_fu(pb, jb=jb):
                    c0_ = jb * 2
                    op(DVE, lambda h: h.tensor_tensor(out=act3[:, c0_:c0_ + 2, 0:TT], in0=PS2(pb, TT), in1=sgate3[:, :, 0:TT],
                                                     op=ALU.mult),
                       reads=[PSB[pb]] + b_sgate, writes=[b_act[c0_], b_act[c0_ + 1]])
                fm_proj2("w_fu", jb * 256, hT3, b_hT, TT, ev_fu)
            tm_proj("w_fd", 0, D, act3, b_act, 22, subs, ev_res)

            def ple_pproj():
                for s, (t0, npart) in enumerate(subs):
                    pb = ps_get()
                    for kc in range(2):
                        transpose(PSf(pb)[:, kc * 128:kc * 128 + npart], PSB[pb], pin3[:npart, s, kc * 128:(kc + 1) * 128],
                                  [b_pin], ident[:npart, :npart], inc=(kc == 1))
                    op(ACT, lambda h, pb=pb, t0=t0, npart=npart: h.activation(
                        out=pT3[:, :, t0:t0 + npart], in_=PSf(pb)[:, 0:256].rearrange("p (k t) -> p k t", k=2)[:, :, 0:npart],
                        func=AF.Copy), reads=[PSB[pb]], writes=[b_pT])

                def ev_pp(cg, s, pb, npart):
                    op(ACT, lambda h: h.activation(out=pproj3[:npart, s, cg * 512:(cg + 1) * 512], in_=PSf(pb)[:npart, :],
                                                   func=AF.Copy), reads=[PSB[pb]], writes=[b_pproj])
                wv, wb = ws_next("w_pp", 0, 2, 0, 1024)
                if not Ctx.dry:
                    for cg in range(2):
                        for s, (t0, npart) in enumerate(subs):
                            pb = ps_get()
                            mm_group(PSf(pb)[:npart, :], PSB[pb],
                                     [(pT3[:, k, t0:t0 + npart], wv[:, k, cg * 512:(cg + 1) * 512], [wb, b_pT]) for k in range(2)])
                            ev_pp(cg, s, pb, npart)

            norm_to_hT(subs, V_LNPLE, filler=ple_pproj)
            if Cur.nxt is not None:
                nsubs = [(0, TS)] if Cur.nxt[0] else [(0, 128), (128, 128)]
                norm_to_hT(nsubs, V_LNMIX, r3(xs_l[1 - par], 2), b_xs_l[1 - par], hT3_l[1 - par], b_hT_l[1 - par])
                Cur.normed = True

            def ev_pg(cg, s, pb, npart):
                gi = (cg * 2 + s) % 2
                ga = gate_sb[gi][:npart, :]
                xa = xs3[:npart, s, cg * 512:(cg + 1) * 512]
                op(ACT, lambda h: h.activation(out=ga, in_=PSf(pb)[:npart, :], func=AF.Sigmoid),
                   reads=[PSB[pb]], writes=[b_gsb[gi]])
                op(DVE, lambda h: h.tensor_tensor(out=ga, in0=ga, in1=pproj3[:npart, s, cg * 512:(cg + 1) * 512], op=ALU.mult),
                   reads=[b_gsb[gi], b_pproj], writes=[b_gsb[gi]])
                op(DVE, lambda h: h.tensor_tensor(out=xa, in0=xa, in1=ga, op=ALU.add),
                   reads=[b_gsb[gi], b_xs[s]], writes=[b_xs[s]])
            tm_proj("w_pg", 0, D, hT3, b_hT, 8, subs, ev_pg)

            for s, (t0, npart) in enumerate(subs):
                xa = xs3[:npart, s, :]
                rstd, bs = rms_stats(xa, npart, [b_xs[s]], si=s)
                op(DVE, lambda h, xa=xa, rstd=rstd, npart=npart, s=s: h.scalar_tensor_tensor(
                    out=xa, in0=xa, scalar=rstd, in1=gfin[:npart, :], op0=ALU.mult, op1=ALU.mult),
                   reads=[b_xs[s], bs, b_const], writes=[b_xs[s]])
                if sample:
                    Cur.pending.append(lambda par=par, xs3=xs3, b_xs=b_xs: dma(
                        ACT, slot(f"yout{par}_0"), ys_d[:, :], xs3[:TS, 0, :], reads=[b_xs[0]]))
                else:
                    r0 = ti * T + s * 128
                    Cur.pending.append(lambda par=par, s=s, r0=r0, xs3=xs3, b_xs=b_xs: dma(
                        ACT, slot(f"yout{par}_{s}"), yp_d[r0:r0 + 128, :], xs3[:, s, :], reads=[b_xs[s]]))

        def kd_build(s_off, npart, kdcol_off):
            pb = ps_get()
            for c in range(8):
                transpose(PSh(pb)[:npart, c * 128:(c + 1) * 128], PSB[pb], kr3[:, c, s_off:s_off + npart], [b_kr[c]],
                          identb, inc=(c == 7))
            for hh in range(4):
                op(ACT, lambda h, hh=hh, pb=pb: h.activation(
                    out=kd[:npart, hh * 256:(hh + 1) * 256], in_=PSh(pb)[:npart, hh * 256:(hh + 1) * 256], func=AF.Identity,
                    scale=ctab[:npart, kdcol_off + hh:kdcol_off + hh + 1], bias=0.0), reads=[PSB[pb], b_const], writes=[b_kd])

        def gn_and_T(pb, hh, npart):
            base = 40 + hh * 12
            bs = b_small_l[2 + hh]
            op(DVE, lambda h: h.bn_stats(out=small[:npart, base:base + 6], in_=PSf(pb)[:npart, :]),
               reads=[PSB[pb]], writes=[bs])
            op(DVE, lambda h: h.bn_aggr(out=small[:npart, base + 6:base + 8], in_=small[:npart, base:base + 6]),
               reads=[bs], writes=[bs])
            op(DVE, lambda h: h.tensor_scalar(out=small[:npart, base + 8:base + 9], in0=small[:npart, base + 7:base + 8],
                                              scalar1=1e-5, scalar2=None, op0=ALU.add), reads=[bs], writes=[bs])
            op(POOL, lambda h: h.tensor_tensor(out=small[:npart, base + 9:base + 10], in0=small[:npart, base + 8:base + 9],
                                               in1=mhalf[:npart, 0:1], op=ALU.pow), reads=[bs, b_const], writes=[bs])
            op(DVE, lambda h: h.tensor_scalar(out=on[:npart, hh * 512:(hh + 1) * 512], in0=PSf(pb)[:npart, :],
                                              scalar1=small[:npart, base + 6:base + 7],
                                              scalar2=small[:npart, base + 9:base + 10],
                                              op0=ALU.subtract, op1=ALU.mult),
               reads=[PSB[pb], bs], writes=[b_on[hh]])

        def on_to_zT(t0, npart):
            for half in range(2):
                pb = ps_get()
                for i in range(8):
                    fc = half * 8 + i
                    transpose(PSh(pb)[:, i * 128:i * 128 + npart], PSB[pb], on[:npart, fc * 128:(fc + 1) * 128],
                              [b_on[fc // 4]], identb[:npart, :npart], inc=(i == 7))
                for i in range(8):
                    fc = half * 8 + i
                    op(ACT, lambda h, pb=pb, i=i, fc=fc: h.activation(
                        out=zT3[:, fc, t0:t0 + npart], in_=PSh(pb)[:, i * 128:i * 128 + npart], func=AF.Identity,
                        scale=vcol(V_GNG, fc), bias=vcol(V_GNB, fc)), reads=[PSB[pb], b_const], writes=[b_zT[fc]])

        def retention_prompt(s, t0):
            kd_build(t0, 128, C_KDP)
            for hh in range(4):
                pb = ps_get()
                mm_group(PSf(pb)[:, 0:128], PSB[pb],
                         [(kr3[:, 2 * hh + dc, t0:t0 + 128], qr3[:, 2 * hh + dc, t0:t0 + 128],
                           [b_kr[2 * hh + dc], b_qr[2 * hh + dc]]) for dc in range(2)])
                op(DVE, lambda h, pb=pb, hh=hh: h.tensor_tensor(
                    out=sT[hh][:, :], in0=PSf(pb)[:, 0:128], in1=ctab[:, C_DECP + hh * 128:C_DECP + (hh + 1) * 128],
                    op=ALU.mult), reads=[PSB[pb], b_const], writes=[b_sT[hh]])
            yield
            for hh in range(4):
                po = ps_get()
                vv = v_sb3[:, s, hh * 512:(hh + 1) * 512]
                terms = [(sT[hh][:, :], vv, [b_sT[hh], b_v[s * 4 + hh]])]
                for dc in range(2):
                    terms.append((qdec3[:, 2 * hh + dc, t0:t0 + 128], Sbf3[:, 2 * hh + dc, :],
                                  [b_qdec[2 * hh + dc], b_Sbf[2 * hh + dc]]))
                mm_group(PSf(po)[:, :], PSB[po], terms)
                gn_and_T(po, hh, 128)
                for dc in range(2):
                    si = 2 * hh + dc
                    pst = ps_get()
                    mm_group(PSf(pst)[:, :], PSB[pst],
                             [(kd[:, hh * 256 + dc * 128:hh * 256 + (dc + 1) * 128], vv, [b_kd, b_v[s * 4 + hh]])])
                    op(DVE, lambda h, si=si, pst=pst, hh=hh: h.scalar_tensor_tensor(
                        out=S32_3[:, si, :], in0=S32_3[:, si, :], scalar=G128[hh], in1=PSf(pst)[:, :],
                        op0=ALU.mult, op1=ALU.add), reads=[PSB[pst], b_S32[si]], writes=[b_S32[si]])
                    op(ACT, lambda h, si=si: h.activation(out=Sbf3[:, si, :], in_=S32_3[:, si, :], func=AF.Copy),
                       reads=[b_S32[si]], writes=[b_Sbf[si]])
                yield
            on_to_zT(t0, 128)
            yield

        def retention_sample():
            kd_build(0, TS, C_KDS)
            po = []
            for hh in range(4):
                pb = ps_get()
                mm_group(PSf(pb)[:TS, 0:TS], PSB[pb],
                         [(kr3[:, 2 * hh + dc, 0:TS], qr3[:, 2 * hh + dc, 0:TS],
                           [b_kr[2 * hh + dc], b_qr[2 * hh + dc]]) for dc in range(2)])
                op(DVE, lambda h, pb=pb, hh=hh: h.tensor_tensor(
                    out=sT[hh][:TS, 0:TS], in0=PSf(pb)[:TS, 0:TS], in1=ctab[:TS, C_DECS + hh * 64:C_DECS + (hh + 1) * 64],
                    op=ALU.mult), reads=[PSB[pb], b_const], writes=[b_sT[hh]])
            for hh in range(4):
                p = ps_get()
                PsA.held.add(p)
                po.append(p)
                mm_group(PSf(p)[:TS, :], PSB[p], [(sT[hh][:TS, 0:TS], v_sb3[:TS, 0, hh * 512:(hh + 1) * 512],
                                                  [b_sT[hh], b_v[hh]])], first=True, last=False)
            units = [(b, hh) for b in range(NSEQ_S) for hh in range(4)]
            NU = len(units)

            def load_unit(u):
                b, hh = units[u]
                i = u % 6
                dma(SP, slot(f"S0_{i}"), S0[i].rearrange("p (dc v) -> p dc v", dc=2),
                    str_d[b, hh, :, :].rearrange("(dc p) v -> p dc v", p=128), writes=[b_S0[i]])
            for u0 in range(3):
                load_unit(u0)
            for u, (b, hh) in enumerate(units):
                if u + 3 < NU:
                    load_unit(u + 3)
                i4 = u % 6
                i2 = u % 2
                if hh == 0:
                    bi = b % 2
                    op(POOL, lambda h, bi=bi, b=b: h.tensor_tensor(
                        out=qxb[bi].rearrange("p (c t) -> p c t", c=8), in0=qdec3[:, :, 0:TS],
                        in1=ctab[:, C_MROW + 60 - 4 * b:C_MROW + 124 - 4 * b].unsqueeze(1).to_broadcast([128, 8, TS]),
                        op=ALU.mult), reads=b_qdec + [b_const], writes=[b_qxb[bi]])
                    op(ACT, lambda h, bi=bi, b=b: h.activation(out=kdx[bi][:TS, :], in_=kd[:TS, :], func=AF.Identity,
                                                             scale=ctab[:TS, C_MCOL + b:C_MCOL + b + 1], bias=0.0),
                       reads=[b_kd, b_const], writes=[b_kdx[bi]])
                bi = b % 2
                qx3 = qxb[bi].rearrange("p (c t) -> p c t", c=8)
                S0v = S0[i4].rearrange("p (dc v) -> p dc v", dc=2)
                S0bv = S0b[i2].rearrange("p (dc v) -> p dc v", dc=2)
                op(ACT, lambda h, i2=i2, i4=i4: h.activation(out=S0b[i2][:, :], in_=S0[i4][:, :], func=AF.Copy),
                   reads=[b_S0[i4]], writes=[b_S0b[i2]])
                lastu = (b == NSEQ_S - 1)
                mm_group(PSf(po[hh])[:TS, :], PSB[po[hh]],
                         [(qx3[:, 2 * hh + dc, :], S0bv[:, dc, :], [b_qxb[bi], b_S0b[i2]]) for dc in range(2)],
                         first=False, last=lastu)
                for dc in range(2):
                    pst = ps_get()
                    mm_group(PSf(pst)[:, :], PSB[pst],
                             [(kdx[bi][:TS, hh * 256 + dc * 128:hh * 256 + (dc + 1) * 128],
                               v_sb3[:TS, 0, hh * 512:(hh + 1) * 512], [b_kdx[bi], b_v[hh]])])
                    op(DVE, lambda h, dc=dc, pst=pst, hh=hh, S0v=S0v: h.scalar_tensor_tensor(
                        out=S0v[:, dc, :], in0=S0v[:, dc, :], scalar=G4[hh], in1=PSf(pst)[:, :],
                        op0=ALU.mult, op1=ALU.add), reads=[PSB[pst], b_S0[i4]], writes=[b_S0[i4]])
                dma(SP, slot(f"S1_{i4}"), nrs_d[b, hh, :, :].rearrange("(dc p) v -> p dc v", p=128), S0v,
                    reads=[b_S0[i4]])
            for hh in range(4):
                gn_and_T(po[hh], hh, TS)
                PsA.held.discard(po[hh])
            on_to_zT(0, TS)

        def emit_all():
            PsA.i = 0
            PsA.held = set()
            WS.cons = 0
            Cur.normed = False
            dma(SP, slot("const"), ctab[:, :], ctab_d[:, :], writes=[b_const])
            dma(SP, slot("const"), vecs[:, :], vecs_d[:, :], writes=[b_const])
            dma(SP, slot("const"), gfin[:, :], gfin_d[:, :], writes=[b_const])
            op(DVE, lambda h: h.tensor_copy(out=identb[:, :], in_=ident), reads=[b_const], writes=[b_const])
            op(DVE, lambda h: h.memset(epsc[:, :], EPS), writes=[b_const])
            op(DVE, lambda h: h.memset(eps5[:, :], 1e-5), writes=[b_const])
            op(DVE, lambda h: h.memset(mhalf[:, :], -0.5), writes=[b_const])
            order = [(False, ti) for ti in range(NT)] + [(True, 0)]
            load_tile(order[0][0], order[0][1], 0)
            Cur.pending = []
            for n_, (smp, ti) in enumerate(order):
                Cur.par = n_ % 2
                Cur.nxt = order[n_ + 1] if n_ + 1 < len(order) else None
                do_tile(smp, ti)
            flush_pending()

        Ctx.dry = True
        emit_all()
        Ctx.dry = False
        WS.nb = len(WS.specs) // (NT + 1)
        assert WS.nb * (NT + 1) == len(WS.specs)
        WS.cache = nc.dram_tensor("wcache", [WS.nb, 128, WBLK], BF16).ap()
        WS.wall = din("wall", [WS.nb, 128, WBLK])
        WS.cidx = {sp: i for i, sp in enumerate(WS.specs[:WS.nb])}
        assert len(WS.cidx) == WS.nb
        for t_ in range(NT + 1):
            assert sorted(WS.specs[t_ * WS.nb:(t_ + 1) * WS.nb]) == sorted(WS.specs[:WS.nb])
        emit_all()
        assert WS.cons == len(WS.specs)

        for sl in slots.values():
            SP.prog.append(("w", sl, sl.count))

        engs = [PE, ACT, DVE, POOL]
        for e in engs:
            e.sem = es.enter_context(nc.semaphore(f"s_{e.name}"))
        for name, sl in slots.items():
            sl.sem = es.enter_context(nc.semaphore(f"d_{name}"))

        def replay(eng, h):
            fuse = eng in (ACT, DVE)
            pend = []
            for it in eng.prog:
                if it[0] == "w":
                    if fuse:
                        pend.append((it[1].sem, it[2]))
                    else:
                        h.wait_ge(it[1].sem, it[2])
                    continue
                for sm_, v_ in pend[:-1]:
                    h.wait_ge(sm_, v_)
                if it[0] == "o":
                    ins = it[1](h)
                    if pend:
                        ins._wait_ge(pend[-1][0], pend[-1][1])
                    if it[2]:
                        ins.then_inc(eng.sem, 1)
                else:
                    if pend:
                        h.wait_ge(pend[-1][0], pend[-1][1])
                    h.dma_start(out=it[1], in_=it[2]).then_inc(it[3].sem, 16)
                pend = []
            for sm_, v_ in pend:
                h.wait_ge(sm_, v_)

        with nc.Block() as block:
            @block.sync
            def _(h):
                replay(SP, h)

            @block.tensor
            def _(h):
                replay(PE, h)

            @block.scalar
            def _(h):
                replay(ACT, h)

            @block.vector
            def _(h):
                replay(DVE, h)

            @block.gpsimd
            def _(h):
                replay(POOL, h)
        print("instr counts:", {e.name: len(e.prog) for e in engs + [SP]})
    nc._ws_specs = list(WS.specs[:WS.nb])
    return nc


_CACHE = {}


def kernel(**inputs):
    f = lambda a: np.ascontiguousarray(np.asarray(a, dtype=np.float32))
    x_prompt = f(inputs["x_prompt"])
    x_sample = f(inputs["x_sample"])
    state_conv = f(inputs["state_conv"])[0]
    state_ret = f(inputs["state_ret"])[0]
    p_prompt = f(inputs["p_prompt"])[0]
    p_sample = f(inputs["p_sample"])[0]

    if "nc" not in _CACHE:
        _CACHE["nc"] = build_program()
        _CACHE["consts"] = _make_consts()
    nc = _CACHE["nc"]
    ctab, rotp, rots = _CACHE["consts"]

    def cols(v, n):
        return np.ascontiguousarray(np.asarray(v, np.float32).reshape(n, 128).T)

    vecs = np.zeros((128, NV), np.float32)
    vecs[:, V_LNMIX:V_LNMIX + 8] = cols(inputs["ln_mix_g"][0], 8)
    vecs[:, V_LNFFN:V_LNFFN + 8] = cols(inputs["ln_ffn_g"][0], 8)
    vecs[:, V_LNPLE:V_LNPLE + 8] = cols(inputs["ln_ple_g"][0], 8)
    vecs[:, V_CB:V_CB + 8] = cols(inputs["conv_b"][0], 8)
    vecs[:, V_CLG:V_CLG + 8] = cols(inputs["conv_ln_g"][0], 8)
    vecs[:, V_CLB:V_CLB + 8] = cols(inputs["conv_ln_b"][0], 8)
    vecs[:, V_GNG:V_GNG + 16] = cols(inputs["ret_gn_g"][0], 16)
    vecs[:, V_GNB:V_GNB + 16] = cols(inputs["ret_gn_b"][0], 16)
    cw = np.asarray(inputs["conv_w"], np.float32)[0]
    vecs[:, V_CW:V_CW + 8 * CW] = cw.reshape(CW, 8, 128).transpose(2, 1, 0).reshape(128, 8 * CW)
    gfin = np.ascontiguousarray(np.broadcast_to(np.asarray(inputs["ln_final_g"], np.float32)[None, :], (128, D)))

    Wh = {
        "w_in": f(inputs["w_in"])[0], "w_co": f(inputs["w_conv_out"])[0], "w_ro": f(inputs["w_ret_out"])[0],
        "w_o": f(inputs["w_o"])[0], "w_fg": f(inputs["w_ffn_gate"])[0], "w_fu": f(inputs["w_ffn_up"])[0],
        "w_fd": f(inputs["w_ffn_down"])[0], "w_pg": f(inputs["w_ple_gate"])[0], "w_pp": f(inputs["w_ple_proj"])[0],
    }
    specs = nc._ws_specs
    wall = np.zeros((len(specs), 128, WBLK), np.float32)
    for jj, (name, k0, nk, c0, ncol) in enumerate(specs):
        if name == "convD":
            continue
        blk = Wh[name][k0 * 128:(k0 + nk) * 128, c0:c0 + ncol]
        wall[jj, :, :nk * ncol] = blk.reshape(nk, 128, ncol).transpose(1, 0, 2).reshape(128, nk * ncol)
    shared = {"wall": wall, "ctab": ctab, "rotp": rotp, "rots": rots, "vecs": vecs, "gfin": gfin}
    in_maps = []
    for i in range(NCORE):
        m = dict(shared)
        m["xp"] = x_prompt[i]
        m["xs"] = x_sample[i * NSEQ_S:(i + 1) * NSEQ_S].reshape(TS, D)
        m["stc"] = state_conv[i * NSEQ_S:(i + 1) * NSEQ_S]
        m["str"] = state_ret[i * NSEQ_S:(i + 1) * NSEQ_S]
        m["pp"] = p_prompt[i]
        m["ps"] = p_sample[i * NSEQ_S:(i + 1) * NSEQ_S].reshape(TS, 256)
        in_maps.append(m)
    res = run_bass_kernel_spmd(nc, in_maps, core_ids=list(range(NCORE)))
    R = res.results
    y_prompt = np.stack([R[i]["yp"] for i in range(NCORE)], 0).astype(np.float32)
    y_sample = np.concatenate([R[i]["ys"].reshape(NSEQ_S, DEC, D) for i in range(NCORE)], 0).astype(np.float32)
    ncp = np.stack([R[i]["ncp"] for i in range(NCORE)], 0)[None].astype(np.float32)
    nrp = np.stack([R[i]["nrp"] for i in range(NCORE)], 0)[None].astype(np.float32)
    ncs = np.concatenate([R[i]["ncs"] for i in range(NCORE)], 0)[None].astype(np.float32)
    nrs = np.concatenate([R[i]["nrs"] for i in range(NCORE)], 0)[None].astype(np.float32)
    return (y_prompt, y_sample, ncp, nrp, ncs, nrs)
```

```python
import numpy as np
import concourse.bass as bass
import concourse.mybir as mybir
from concourse.bass_utils import run_bass_kernel_spmd

F32 = mybir.dt.float32
BF16 = mybir.dt.bfloat16
AF = mybir.ActivationFunctionType
ALU = mybir.AluOpType

D = 1024
SEQ = 2048
NCORE = 8
NSEQ_S = 16
DEC = 4
TS = NSEQ_S * DEC
CW = 31
PAST = 16384
DFF = 2816
T = 256
NT = SEQ // T
WBLK = 2048
NWST = 3
NWBF = 3
AHEAD = 2
NCAST = 1
EPS = 1e-6

O_AV, O_AG, O_Q, O_K, O_V, O_G, O_GA, O_GB = 0, 1024, 2048, 3072, 4096, 6144, 8192, 9216

C_ID = 0
C_DECP = C_ID + 128
C_DECS = C_DECP + 512
C_GLP = C_DECS + 256
C_GLS = C_GLP + 1024
C_KDP = C_GLS + 256
C_KDS = C_KDP + 4
C_MCOL = C_KDS + 4
C_MROW = C_MCOL + 16
C_ONES = C_MROW + 128
NCT = C_ONES + 128

V_LNMIX, V_LNFFN, V_LNPLE, V_CB, V_CLG, V_CLB = 0, 8, 16, 24, 32, 40
V_GNG, V_GNB = 48, 64
V_CW = 80
NV = V_CW + 8 * CW


def _log_g():
    h = np.arange(4, dtype=np.float64)
    return np.log1p(-np.exp2(-5.0 - h))


def _make_consts():
    lg = _log_g()
    ct = np.zeros((128, NCT), np.float64)
    ct[:, C_ID:C_ID + 128] = np.eye(128)
    m = np.arange(128)[:, None]
    l = np.arange(128)[None, :]
    for h in range(4):
        dec = np.where(l >= m, np.exp(lg[h] * np.maximum(l - m, 0)), 0.0)
        ct[:, C_DECP + h * 128:C_DECP + (h + 1) * 128] = dec
        ms = np.arange(64)[:, None]
        ls = np.arange(64)[None, :]
        same = (ms // 4) == (ls // 4)
        decs = np.where(same & (ls >= ms), np.exp(lg[h] * np.maximum(ls - ms, 0)), 0.0)
        ct[:64, C_DECS + h * 64:C_DECS + (h + 1) * 64] = decs
        t = np.arange(256)
        ct[:, C_GLP + h * 256:C_GLP + (h + 1) * 256] = np.exp(lg[h] * ((t % 128) + 1.0))[None, :]
        ts = np.arange(64)
        ct[:, C_GLS + h * 64:C_GLS + (h + 1) * 64] = np.exp(lg[h] * ((ts % 4) + 1.0))[None, :]
        ct[:, C_KDP + h] = np.exp(lg[h] * (127.0 - np.arange(128)))
        ct[:64, C_KDS + h] = np.exp(lg[h] * (3.0 - (np.arange(64) % 4)))
    for b in range(16):
        ct[:64, C_MCOL + b] = ((np.arange(64) // 4) == b)
    ct[:, C_MROW + 60:C_MROW + 64] = 1.0
    ct[:, C_ONES:C_ONES + 128] = 1.0 / 1024.0
    half = 128
    inv = (1.0 / (np.float32(10000.0) ** (np.arange(half, dtype=np.float32) / np.float32(half)))).astype(np.float32)
    posp = np.arange(SEQ, dtype=np.float32)
    angp = (posp[None, :] * inv[:, None]).astype(np.float32)
    rotp = np.stack([np.cos(angp), np.sin(angp)], axis=1).astype(np.float32)
    poss = (np.float32(PAST) + (np.arange(TS) % 4).astype(np.float32)).astype(np.float32)
    angs = (poss[None, :] * inv[:, None]).astype(np.float32)
    rots = np.stack([np.cos(angs), np.sin(angs)], axis=1).astype(np.float32)
    return ct.astype(np.float32), rotp, rots


class Buf:
    __slots__ = ("w", "r")

    def __init__(self):
        self.w = None
        self.r = {}


class Eng:
    def __init__(self, name, sync_self):
        self.name = name
        self.sem = None
        self.count = 0
        self.waited = {}
        self.sync_self = sync_self
        self.prog = []


class Slot:
    def __init__(self):
        self.sem = None
        self.count = 0


class Ctx:
    dry = False


def _deps(reads, writes):
    d = {}

    def add(s, v):
        if d.get(s, 0) < v:
            d[s] = v
    for b in reads:
        if b.w is not None:
            add(*b.w)
    for b in writes:
        if b.w is not None:
            add(*b.w)
        for s, v in b.r.items():
            add(s, v)
    return d


def _wait(eng, d):
    for s, v in d.items():
        if s is eng and not eng.sync_self:
            continue
        if eng.waited.get(s, 0) < v:
            eng.prog.append(("w", s, v))
            eng.waited[s] = v


def _record(ev, reads, writes):
    s, v = ev
    for b in reads:
        if b.r.get(s, 0) < v:
            b.r[s] = v
    for b in writes:
        b.w = ev
        b.r = {}


def op(eng, fn, reads=(), writes=(), inc=True):
    if Ctx.dry:
        return
    _wait(eng, _deps(reads, writes))
    eng.prog.append(("o", fn, inc))
    if inc:
        eng.count += 1
        ev = (eng, eng.count)
    else:
        ev = (eng, eng.count + 1)
    _record(ev, reads, writes)


def dma(q, slot, out, in_, reads=(), writes=()):
    if Ctx.dry:
        return
    _wait(q, _deps(reads, writes))
    q.prog.append(("d", out, in_, slot))
    slot.count += 16
    _record((slot, slot.count), reads, writes)


def inherit(new_bufs, old_bufs):
    if Ctx.dry:
        return
    d = {}
    for b in old_bufs:
        if b.w is not None and d.get(b.w[0], 0) < b.w[1]:
            d[b.w[0]] = b.w[1]
        for s, v in b.r.items():
            if d.get(s, 0) < v:
                d[s] = v
    for b in new_bufs:
        for s, v in d.items():
            if b.r.get(s, 0) < v:
                b.r[s] = v


def build_program():
    nc = bass.Bass("TRN2", target_bir_lowering=False)

    def din(name, shape):
        return nc.dram_tensor(name, list(shape), F32, kind="ExternalInput").ap()

    def dout(name, shape):
        return nc.dram_tensor(name, list(shape), F32, kind="ExternalOutput").ap()

    xp_d = din("xp", [SEQ, D])
    xs_d = din("xs", [TS, D])
    stc_d = din("stc", [NSEQ_S, 30, D])
    str_d = din("str", [NSEQ_S, 4, 256, 512])
    pp_d = din("pp", [SEQ, 256])
    ps_d = din("ps", [TS, 256])
    ctab_d = din("ctab", [128, NCT])
    rotp_d = din("rotp", [128, 2, SEQ])
    rots_d = din("rots", [128, 2, TS])
    vecs_d = din("vecs", [128, NV])
    gfin_d = din("gfin", [128, D])

    yp_d = dout("yp", [SEQ, D])
    ys_d = dout("ys", [TS, D])
    ncp_d = dout("ncp", [30, D])
    nrp_d = dout("nrp", [4, 256, 512])
    ncs_d = dout("ncs", [NSEQ_S, 30, D])
    nrs_d = dout("nrs", [NSEQ_S, 4, 256, 512])

    lg = _log_g()
    G128 = [float(np.float32(np.exp(lg[h] * 128.0))) for h in range(4)]
    G4 = [float(np.float32(np.exp(lg[h] * 4.0))) for h in range(4)]

    import contextlib
    es = contextlib.ExitStack()
    with es:
        ARENA_W = 52600
        arena = es.enter_context(nc.sbuf_tensor("arena", [128, ARENA_W], F32))
        psum = [es.enter_context(nc.psum_tensor(f"ps{i}", [128, 512], F32)) for i in range(8)]
        PSB = [Buf() for _ in range(8)]

        class Ar:
            off = 0
            peak = 0

        def alloc(nelem, dtype=F32):
            words = nelem if dtype == F32 else (nelem + 1) // 2
            words = (words + 1) // 2 * 2
            o = Ar.off
            Ar.off += words
            Ar.peak = max(Ar.peak, Ar.off)
            assert Ar.off <= ARENA_W, f"arena overflow {Ar.off}"
            v = arena[:, o:o + words]
            if dtype != F32:
                v = v.bitcast(dtype)[:, 0:nelem]
            else:
                v = v[:, 0:nelem]
            return v

        def r3(ap, a):
            return ap.rearrange("p (a b) -> p a b", a=a)

        ctab = alloc(NCT)
        vecs = alloc(NV)
        gfin = alloc(D)
        identb = alloc(128, BF16)
        rot_l = [alloc(2 * T) for _ in range(2)]
        xs_l = [alloc(2 * D) for _ in range(2)]
        pin_l = [alloc(2 * 256) for _ in range(2)]
        xn_l = [alloc(D), alloc(D)]
        xn = xn_l[0]
        hT_l = [alloc(8 * T, BF16) for _ in range(2)]
        m_a = alloc(8 * T, BF16)
        mixin = alloc(8 * T, BF16)
        small = alloc(96)
        wst = [alloc(WBLK) for _ in range(NWST)]
        wbf = [alloc(WBLK, BF16) for _ in range(NWBF)]
        S32 = alloc(8 * 512)
        Sbf = alloc(8 * 512, BF16)
        utail = alloc(8 * 64)
        utok = xn
        epsc = alloc(2)
        eps5 = alloc(2)
        mhalf = alloc(2)
        uprev = alloc(8 * 32, BF16)
        mark = Ar.off
        qr = alloc(8 * T, BF16)
        kr = alloc(8 * T, BF16)
        qdec = alloc(8 * T, BF16)
        rtmp = alloc(4 * T)
        v_sb = alloc(2 * 2048, BF16)
        kd = alloc(1024, BF16)
        sT = [alloc(128, BF16) for _ in range(4)]
        on = alloc(2048, BF16)
        sgg = alloc(2 * T)
        zT = alloc(16 * T, BF16)
        retc_end = Ar.off
        sg = alloc(2 * T)
        ucat = alloc(8 * 544, BF16)
        c_sb = alloc(8 * T)
        csq = alloc(2 * T)
        lnA = alloc(T)
        lnB = alloc(T)
        mean_sb = alloc(T)
        tmpc = alloc(2 * T)
        ca = alloc(8 * T, BF16)
        sga = alloc(8 * T, BF16)
        scg = alloc(D)
        conv_end = Ar.off
        Ar.off = retc_end
        S0 = [alloc(2 * 512) for _ in range(6)]
        S0b = [alloc(2 * 512, BF16) for _ in range(2)]
        kdx = [alloc(1024, BF16) for _ in range(2)]
        qxb = [alloc(8 * 64, BF16) for _ in range(2)]
        ret_end = Ar.off
        Ar.off = mark
        act = alloc(22 * T, BF16)
        sgate = alloc(2 * T)
        pT = alloc(2 * T, BF16)
        pproj = alloc(2 * D)
        gate_sb = [alloc(512) for _ in range(2)]
        ffn_end = Ar.off
        assert ffn_end <= retc_end, (ffn_end, retc_end)
        print("arena peak words", Ar.peak, "of", ARENA_W)

        ident = ctab[:, C_ID:C_ID + 128]
        ones_s = ctab[:, C_ONES:C_ONES + 128]
        hT3_l = [r3(hT_l[0], 8), r3(hT_l[1], 8)]
        m_a3 = r3(m_a, 8)
        mixin3 = r3(mixin, 8)
        S32_3 = r3(S32, 8)
        Sbf3 = r3(Sbf, 8)
        utail3 = r3(utail, 8)
        uprev3 = r3(uprev, 8)
        sg3 = r3(sg, 2)
        ucat3 = r3(ucat, 8)
        c_sb3 = r3(c_sb, 8)
        csq3 = r3(csq, 2)
        tmpc3 = r3(tmpc, 2)
        ca3 = r3(ca, 8)
        sga3 = r3(sga, 8)
        qr3 = r3(qr, 8)
        kr3 = r3(kr, 8)
        qdec3 = r3(qdec, 8)
        rtmp3 = r3(rtmp, 4)
        v_sb3 = r3(v_sb, 2)
        sgg3 = r3(sgg, 2)
        zT3 = r3(zT, 16)
        act3 = r3(act, 22)
        sgate3 = r3(sgate, 2)
        pT3 = r3(pT, 2)
        pproj3 = r3(pproj, 2)

        def vcol(off, c):
            return vecs[:, off + c:off + c + 1]

        B = {}

        def nb(name, n=None):
            if n is None:
                B[name] = Buf()
            else:
                B[name] = [Buf() for _ in range(n)]
            return B[name]

        b_const = nb("const")
        b_rot_l = [Buf(), Buf()]
        b_xs_l = [[Buf(), Buf()], [Buf(), Buf()]]
        b_pin_l = [Buf(), Buf()]
        b_xn_l = [Buf(), Buf()]
        b_xn = b_xn_l[0]
        b_hT_l = [[Buf() for _ in range(8)], [Buf() for _ in range(8)]]
        b_ma = nb("m_a", 8)
        b_mixin = nb("mixin", 8)
        b_small_l = [Buf() for _ in range(6)]
        b_wst = nb("wst", NWST)
        b_wbf = nb("wbf", NWBF)
        b_S32 = nb("S32", 8)
        b_Sbf = nb("Sbf", 8)
        b_utail = nb("utail")
        b_utok = b_xn
        b_uprev = nb("uprev", 8)
        b_sg = nb("sg", 2)
        b_ucat = nb("ucat", 8)
        b_csb = nb("c_sb", 8)
        b_csq = nb("csq", 2)
        b_ln = nb("ln")
        b_tmpc = nb("tmpc", 2)
        b_ca = nb("ca", 8)
        b_scg = nb("scg")
        b_sga = nb("sga", 8)
        G_CONV = b_sg + b_ucat + b_csb + b_csq + [b_ln] + b_tmpc + b_ca + [b_scg] + b_sga
        b_qr = nb("qr", 8)
        b_kr = nb("kr", 8)
        b_qdec = nb("qdec", 8)
        b_rtmp = nb("rtmp", 4)
        b_v = nb("v", 8)
        b_kd = nb("kd")
        b_sT = nb("sT", 4)
        b_on = nb("on", 4)
        b_sgg = nb("sgg", 2)
        b_zT = nb("zT", 16)
        b_S0 = nb("S0", 6)
        b_S0b = nb("S0b", 2)
        b_kdx = nb("kdx", 2)
        b_qxb = nb("qxb", 2)
        G_RETC = b_qr + b_kr + b_qdec + b_rtmp + b_v + [b_kd] + b_sT + b_on + b_sgg + b_zT
        G_SAMP = b_S0 + b_S0b + b_kdx + b_qxb
        b_act = nb("act", 22)
        b_sgate = nb("sgate", 2)
        b_pT = nb("pT")
        b_pproj = nb("pproj")
        b_gsb = nb("gate_sb", 2)
        G_FFN = b_act + b_sgate + [b_pT, b_pproj] + b_gsb

        PE = Eng("pe", False)
        ACT = Eng("act", True)
        DVE = Eng("dve", True)
        POOL = Eng("pool", True)
        SP = Eng("sp", False)
        slots = {}

        def slot(name):
            if name not in slots:
                slots[name] = Slot()
            return slots[name]

        class PsA:
            i = 0
            held = set()

        def ps_get():
            while True:
                b = PsA.i % 8
                PsA.i += 1
                if b not in PsA.held:
                    return b

        def PSf(b):
            return psum[b][:, :]

        def PSh(b):
            return psum[b][:, :].bitcast(BF16)

        class WS:
            specs = []
            issued = 0
            cons = 0
            nb = 0
            cache = None
            cidx = {}
            wall = None
            nst = 0
            ncast = 0
            dma_issued = 0
            stg = {}

        NWX = NWBF + 2 * NWST
        wbx = list(wbf) + [wst[i][:, h * (WBLK // 2):(h + 1) * (WBLK // 2)].bitcast(BF16) for i in range(NWST) for h in range(2)]
        b_wbx = list(b_wbf) + [Buf() for _ in range(2 * NWST)]
        b_wc = {}

        def wslot(j):
            if j < NCAST * WS.nb:
                return wbf[j % NWBF], b_wbf[j % NWBF]
            return wbx[j % NWX], b_wbx[j % NWX]

        def ws_meta(j):
            name, k0, nk, c0, ncol = WS.specs[j]
            jj = WS.cidx[WS.specs[j]]
            t = j // WS.nb
            ctile = 0 if name == "convD" else jj % NCAST
            return name, k0, nk, c0, ncol, jj, t, ctile

        def ws_is_fp32(j):
            name, k0, nk, c0, ncol, jj, t, ctile = ws_meta(j)
            return not (t > ctile or name == "convD")

        def ws_issue_dma(j):
            name, k0, nk, c0, ncol, jj, t, ctile = ws_meta(j)
            if t > ctile or name == "convD":
                return
            n = nk * ncol
            si = WS.nst % NWST
            WS.nst += 1
            WS.stg[j] = si
            dma(SP, slot(f"wst{si}"), wst[si][:, 0:n], WS.wall[jj, :, 0:n], writes=[b_wst[si]])

        def ws_issue_cast(j):
            name, k0, nk, c0, ncol, jj, t, ctile = ws_meta(j)
            n = nk * ncol
            bi = j % NWBF
            o_ = wbf[bi][:, 0:n]
            if t > ctile:
                dma(SP, slot(f"wlc{bi}"), o_, WS.cache[jj, :, 0:n], reads=[b_wc[jj]], writes=[b_wbf[bi]])
                return
            if name == "convD":
                wcol = vecs[:, V_CW + k0 * CW + c0:V_CW + k0 * CW + c0 + nk]
                op(POOL, lambda h, o_=o_, wcol=wcol, nk=nk: h.tensor_tensor(
                    out=o_.rearrange("p (j q) -> p j q", j=nk), in0=ident.unsqueeze(1).to_broadcast([128, nk, 128]),
                    in1=wcol.unsqueeze(2).to_broadcast([128, nk, 128]), op=ALU.mult),
                   reads=[b_const], writes=[b_wbf[bi]])
            else:
                si = WS.stg.pop(j)
                i_ = wst[si][:, 0:n]
                WS.ncast += 1
                if WS.ncast % 2 == 0:
                    op(ACT, lambda h, o_=o_, i_=i_: h.activation(out=o_, in_=i_, func=AF.Copy),
                       reads=[b_wst[si]], writes=[b_wbf[bi]])
                else:
                    op(DVE, lambda h, o_=o_, i_=i_: h.tensor_copy(out=o_, in_=i_),
                       reads=[b_wst[si]], writes=[b_wbf[bi]])
            if t == ctile:
                b_wc[jj] = Buf()
                dma(ACT, slot(f"wcw{bi}"), WS.cache[jj, :, 0:n], o_, reads=[b_wbf[bi]], writes=[b_wc[jj]])

        def ws_issue_cached(j):
            name, k0, nk, c0, ncol, jj, t, ctile = ws_meta(j)
            n = nk * ncol
            if j == NCAST * WS.nb:
                inherit(b_wbx[NWBF:], b_wst)
            ap_, bf_ = wslot(j)
            dma(SP, slot(f"wl{j % NWX}"), ap_[:, 0:n], WS.cache[jj, :, 0:n], reads=[b_wc[jj]], writes=[bf_])

        def ws_next(name, k0, nk, c0, ncol):
            spec = (name, k0, nk, c0, ncol)
            assert nk * ncol <= WBLK
            i = WS.cons
            WS.cons += 1
            if Ctx.dry:
                WS.specs.append(spec)
                return None, None
            assert WS.specs[i] == spec, (i, WS.specs[i], spec)
            ncr = NCAST * WS.nb
            if i < ncr:
                lim_d = min(ncr, i + NWST + 1)
                lim_c = min(ncr, i + 2)
                while True:
                    if WS.issued < lim_c and WS.dma_issued > WS.issued:
                        ws_issue_cast(WS.issued)
                        WS.issued += 1
                    elif WS.dma_issued < lim_d and (WS.nst - WS.ncast < NWST or not ws_is_fp32(WS.dma_issued)):
                        ws_issue_dma(WS.dma_issued)
                        WS.dma_issued += 1
                    else:
                        break
                assert WS.issued > i
            else:
                lim = min(len(WS.specs), i + NWX - 1)
                while WS.issued < lim:
                    ws_issue_cached(WS.issued)
                    WS.issued += 1
            ap_, bf_ = wslot(i)
            v = ap_[:, 0:nk * ncol].rearrange("p (k c) -> p k c", k=nk)
            return v, bf_

        def mm_group(out_ap, out_buf, terms, first=True, last=True):
            n = len(terms)
            for i, (l_, r_, rd) in enumerate(terms):
                st = first and i == 0
                sp = last and i == n - 1
                op(PE, lambda h, l_=l_, r_=r_, st=st, sp=sp: h.matmul(out_ap, l_, r_, start=st, stop=sp),
                   reads=list(rd), writes=[out_buf], inc=(i == n - 1))

        def transpose(out_ap, out_buf, in_ap, in_bufs, idn, inc=True):
            op(PE, lambda h: h.transpose(out_ap, in_ap, idn), reads=list(in_bufs) + [b_const],
               writes=[out_buf], inc=inc)

        def rms_stats(x_ap, npart, xbufs, si=0):
            o = si * 20
            bs = b_small_l[si]
            sm = small[:npart, o:o + 20]
            op(DVE, lambda h: h.bn_stats(out=sm[:, 0:6], in_=x_ap[:, 0:512]), reads=xbufs, writes=[bs])
            op(DVE, lambda h: h.bn_stats(out=sm[:, 6:12], in_=x_ap[:, 512:1024]), reads=xbufs, writes=[bs])
            op(DVE, lambda h: h.bn_aggr(out=sm[:, 12:14], in_=sm[:, 0:12]), reads=[bs], writes=[bs])
            op(DVE, lambda h: h.scalar_tensor_tensor(out=sm[:, 14:15], in0=sm[:, 12:13], scalar=sm[:, 12:13], in1=sm[:, 13:14],
                                                     op0=ALU.mult, op1=ALU.add), reads=[bs], writes=[bs])
            op(DVE, lambda h: h.tensor_scalar(out=sm[:, 15:16], in0=sm[:, 14:15], scalar1=EPS, scalar2=None, op0=ALU.add),
               reads=[bs], writes=[bs])
            op(POOL, lambda h: h.tensor_tensor(out=sm[:, 16:17], in0=sm[:, 15:16], in1=mhalf[:npart, 0:1], op=ALU.pow),
               reads=[bs, b_const], writes=[bs])
            return sm[:, 16:17], bs

        class Cur:
            xs3 = None
            b_xs = None
            hT3 = None
            b_hT = None
            par = 0
            nxt = None
            normed = False
            pending = []

        def flush_pending():
            for f_ in Cur.pending:
                f_()
            Cur.pending = []

        def norm_to_hT(subs, voff, xs3=None, b_xs=None, hT3=None, b_hT=None, filler=None):
            if xs3 is None:
                xs3, b_xs, hT3, b_hT = Cur.xs3, Cur.b_xs, Cur.hT3, Cur.b_hT
            for s, (t0, npart) in enumerate(subs):
                xa = xs3[:npart, s, :]
                rstd, bs = rms_stats(xa, npart, [b_xs[s]], si=s)
                xnv = xn_l[s]
                op(DVE, lambda h, xa=xa, rstd=rstd, npart=npart, xnv=xnv: h.tensor_scalar(
                    out=xnv[:npart, :], in0=xa, scalar1=rstd, scalar2=None, op0=ALU.mult),
                   reads=[b_xs[s], bs], writes=[b_xn_l[s]])
            if filler is not None:
                filler()
            for s, (t0, npart) in enumerate(subs):
                xnv = xn_l[s]
                for half in range(2):
                    pb = ps_get()
                    for i in range(4):
                        c = half * 4 + i
                        transpose(PSf(pb)[:, i * 128:i * 128 + npart], PSB[pb],
                                  xnv[:npart, c * 128:(c + 1) * 128], [b_xn_l[s]], ident[:npart, :npart], inc=(i == 3))
                    for i in range(4):
                        c = half * 4 + i
                        op(ACT, lambda h, pb=pb, i=i, c=c, t0=t0, npart=npart: h.activation(
                            out=hT3[:, c, t0:t0 + npart], in_=PSf(pb)[:, i * 128:i * 128 + npart],
                            func=AF.Identity, scale=vcol(voff, c), bias=0.0),
                           reads=[PSB[pb], b_const], writes=[b_hT[c]])

        def PS2(pb, TT):
            return psum[pb][:, :].rearrange("p (a t) -> p a t", a=2)[:, :, 0:TT]

        def fm_proj2(name, c0, in3, in_bufs, TT, evac2):
            wv, wb = ws_next(name, 0, 8, c0, 256)
            pb = ps_get()
            if not Ctx.dry:
                for j in range(2):
                    mm_group(PSf(pb)[:, j * 256:j * 256 + TT], PSB[pb],
                             [(wv[:, k, j * 128:(j + 1) * 128], in3[:, k, 0:TT], [wb, in_bufs[k]]) for k in range(8)])
            evac2(pb)

        def fm_proj(name, c0, ncols, in3, in_bufs, nk, TT, evac, kblk=None):
            bc = (WBLK // nk) // 128 * 128
            assert ncols % bc == 0
            oc = 0
            for cb in range(ncols // bc):
                wv, wb = ws_next(name, 0, nk, c0 + cb * bc, bc)
                for j in range(bc // 128):
                    pb = ps_get()
                    if not Ctx.dry:
                        mm_group(PSf(pb)[:, 0:TT], PSB[pb],
                                 [(wv[:, k, j * 128:(j + 1) * 128], in3[:, k, 0:TT], [wb, in_bufs[k]]) for k in range(nk)])
                    evac(oc, pb)
                    oc += 1

        def tm_proj(name, c0, ncols, in3, in_bufs, nk, subs, evac, k_split=4):
            for cg in range(ncols // 512):
                pbs = [ps_get() for _ in subs]
                for pb in pbs:
                    PsA.held.add(pb)
                kb = 0
                nblk = (nk + k_split - 1) // k_split
                for bi in range(nblk):
                    k0 = bi * k_split
                    kk = min(k_split, nk - k0)
                    wv, wb = ws_next(name, k0, kk, c0 + cg * 512, 512)
                    if Ctx.dry:
                        continue
                    for s, (t0, npart) in enumerate(subs):
                        mm_group(PSf(pbs[s])[:npart, :], PSB[pbs[s]],
                                 [(in3[:, k0 + k, t0:t0 + npart], wv[:, k, :], [wb, in_bufs[k0 + k]]) for k in range(kk)],
                                 first=(bi == 0), last=(bi == nblk - 1))
                for s, (t0, npart) in enumerate(subs):
                    evac(cg, s, pbs[s], npart)
                for pb in pbs:
                    PsA.held.discard(pb)

        def load_tile(sample, ti, par):
            xs3 = r3(xs_l[par], 2)
            pin3 = r3(pin_l[par], 2)
            rot3 = r3(rot_l[par], 2)
            b_xs = b_xs_l[par]
            if sample:
                dma(SP, slot(f"xs{par}_0"), xs3[:TS, 0, :], xs_d[:, :], writes=[b_xs[0]])
                dma(SP, slot(f"pin{par}"), pin3[:TS, 0, :], ps_d[:, :], writes=[b_pin_l[par]])
                dma(SP, slot(f"rot{par}"), rot3[:, :, 0:TS], rots_d[:, :, :], writes=[b_rot_l[par]])
            else:
                r0 = ti * T
                for s in range(2):
                    dma(SP, slot(f"xs{par}_{s}"), xs3[:, s, :], xp_d[r0 + s * 128:r0 + (s + 1) * 128, :], writes=[b_xs[s]])
                dma(SP, slot(f"pin{par}"), pin3[:, :, :], pp_d[r0:r0 + T, :].rearrange("(s p) c -> p s c", p=128),
                    writes=[b_pin_l[par]])
                dma(SP, slot(f"rot{par}"), rot3[:, :, 0:T], rotp_d[:, :, r0:r0 + T], writes=[b_rot_l[par]])

        def do_tile(sample, ti):
            TT = TS if sample else T
            subs = [(0, TS)] if sample else [(0, 128), (128, 128)]
            NS = len(subs)
            last_prompt = (not sample) and ti == NT - 1
            first_prompt = (not sample) and ti == 0

            par = Cur.par
            xs3 = r3(xs_l[par], 2)
            pin3 = r3(pin_l[par], 2)
            rot3 = r3(rot_l[par], 2)
            b_xs = b_xs_l[par]
            b_pin = b_pin_l[par]
            b_rot = b_rot_l[par]
            Cur.xs3 = xs3
            Cur.b_xs = b_xs
            hT3 = hT3_l[par]
            b_hT = b_hT_l[par]
            Cur.hT3 = hT3
            Cur.b_hT = b_hT
            cosT = rot3[:, 0, 0:TT]
            sinT = rot3[:, 1, 0:TT]

            if not Cur.normed:
                norm_to_hT(subs, V_LNMIX)

            if sample:
                ucat5 = ucat3.rearrange("p c (b w) -> p c b w", w=34)
            need_tail = sample or last_prompt
            ntail = TS if sample else 32

            def u_dst(c):
                if sample:
                    return ucat5[:, c, :, 30:34]
                return ucat3[:, c, 30:30 + T]

            def p_conv_hist():
                if sample:
                    for g in range(4):
                        dma(SP, slot("scg"), scg[:120, :], stc_d[4 * g:4 * g + 4, :, :].rearrange("b r c -> (b r) c"),
                            writes=[b_scg])
                        for half in range(2):
                            pb = ps_get()
                            for i in range(4):
                                c = half * 4 + i
                                transpose(PSf(pb)[:, i * 128:i * 128 + 120], PSB[pb], scg[:120, c * 128:(c + 1) * 128],
                                          [b_scg], ident[:120, :120], inc=(i == 3))
                            for i in range(4):
                                c = half * 4 + i
                                op(ACT, lambda h, pb=pb, i=i, c=c, g=g: h.activation(
                                    out=ucat5[:, c, 4 * g:4 * g + 4, 0:30],
                                    in_=PSf(pb)[:, i * 128:i * 128 + 120].rearrange("p (b r) -> p b r", r=30),
                                    func=AF.Copy), reads=[PSB[pb]], writes=[b_ucat[c]])
                    dma(SP, slot("ncs0"), ncs_d[:, 0:26, :], stc_d[:, 4:30, :])
                else:
                    for c in range(8):
                        if first_prompt:
                            op(POOL, lambda h, c=c: h.memset(uprev3[:, c, :], 0.0), writes=[b_uprev[c]])
                        op(POOL, lambda h, c=c: h.tensor_copy(out=ucat3[:, c, 0:30], in_=uprev3[:, c, 0:30]),
                           reads=[b_uprev[c]], writes=[b_ucat[c]])

            def g_agav():
                for jb in range(4):
                    def ev_gate(pb, jb=jb):
                        op(ACT, lambda h: h.activation(out=sg3[:, :, 0:TT], in_=PS2(pb, TT), func=AF.Sigmoid),
                           reads=[PSB[pb]], writes=b_sg)
                    fm_proj2("w_in", O_AG + jb * 256, hT3, b_hT, TT, ev_gate)
                    yield

                    def ev_val(pb, jb=jb):
                        c0_ = jb * 2
                        if sample:
                            for i in range(2):
                                c = c0_ + i
                                src = PSf(pb)[:, i * 256:i * 256 + TT].rearrange("p (b t) -> p b t", t=4)
                                s2 = sg3[:, i, 0:TT].rearrange("p (b t) -> p b t", t=4)
                                op(DVE, lambda h, c=c, src=src, s2=s2: h.tensor_tensor(out=u_dst(c), in0=src, in1=s2, op=ALU.mult),
                                   reads=[PSB[pb], b_sg[i]], writes=[b_ucat[c]])
                        else:
                            op(DVE, lambda h: h.tensor_tensor(out=ucat3[:, c0_:c0_ + 2, 30:30 + T], in0=PS2(pb, TT),
                                                             in1=sg3[:, :, 0:TT], op=ALU.mult),
                               reads=[PSB[pb]] + b_sg, writes=[b_ucat[c0_], b_ucat[c0_ + 1]])
                            op(POOL, lambda h: h.tensor_copy(out=uprev3[:, c0_:c0_ + 2, 0:30], in_=ucat3[:, c0_:c0_ + 2, T:T + 30]),
                               reads=[b_ucat[c0_], b_ucat[c0_ + 1]], writes=[b_uprev[c0_], b_uprev[c0_ + 1]])
                        if need_tail:
                            op(DVE, lambda h: h.tensor_tensor(out=utail3[:, c0_:c0_ + 2, 0:ntail], in0=PS2(pb, TT)[:, :, TT - ntail:TT],
                                                             in1=sg3[:, :, TT - ntail:TT], op=ALU.mult),
                               reads=[PSB[pb]] + b_sg, writes=[b_utail])
                    fm_proj2("w_in", O_AV + jb * 256, hT3, b_hT, TT, ev_val)
                    yield

            def p_tail():
                if not need_tail:
                    return
                nrow = TS if sample else 30
                c0t = 0 if sample else 2
                for half in range(2):
                    pb = ps_get()
                    for i in range(4):
                        c = half * 4 + i
                        transpose(PSf(pb)[:nrow, i * 128:(i + 1) * 128], PSB[pb], utail3[:, c, c0t:c0t + nrow],
                                  [b_utail], ident, inc=(i == 3))
                    op(ACT, lambda h, pb=pb, half=half, nrow=nrow: h.activation(
                        out=utok[:nrow, half * 512:(half + 1) * 512], in_=PSf(pb)[:nrow, :], func=AF.Copy),
                       reads=[PSB[pb]], writes=[b_utok])
                if sample:
                    for b in range(NSEQ_S):
                        dma(ACT, slot("ncs1"), ncs_d[b, 26:30, :], utok[4 * b:4 * b + 4, :], reads=[b_utok])
                else:
                    dma(ACT, slot("ncp"), ncp_d[:, :], utok[:30, :], reads=[b_utok])

            st_banks = {}

            def g_conv_chunks():
                pm = ps_get()
                PsA.held.add(pm)
                pe2 = ps_get()
                PsA.held.add(pe2)
                st_banks["pm"] = pm
                st_banks["pe2"] = pe2
                for c in range(8):
                    pb = ps_get()
                    for part, (j0, nj) in enumerate([(0, 16), (16, 15)]):
                        Dv, Db = ws_next("convD", c, nj, j0, 128)
                        if Ctx.dry:
                            continue
                        terms = []
                        for jj in range(nj):
                            j = j0 + jj
                            if sample:
                                rhs = ucat5[:, c, :, j:j + 4]
                            else:
                                rhs = ucat3[:, c, j:j + T]
                            terms.append((Dv[:, jj, :], rhs, [Db, b_ucat[c]]))
                        mm_group(PSf(pb)[:, 0:TT], PSB[pb], terms, first=(part == 0), last=(part == 1))
                        if part == 0 and not sample:
                            yield
                    qi = c % 2
                    op(ACT, lambda h, pb=pb, c=c: h.activation(out=c_sb3[:, c, 0:TT], in_=PSf(pb)[:, 0:TT], func=AF.Identity,
                                                             bias=vcol(V_CB, c), scale=1.0),
                       reads=[PSB[pb], b_const], writes=[b_csb[c]])
                    op(ACT, lambda h, pb=pb, c=c, qi=qi: h.activation(out=csq3[:, qi, 0:TT], in_=PSf(pb)[:, 0:TT], func=AF.Square,
                                                                    bias=vcol(V_CB, c), scale=1.0),
                       reads=[PSB[pb], b_const], writes=[b_csq[qi]])
                    def stats(cc):
                        qq = cc % 2
                        mm_group(PSf(pm)[:, 0:TT], PSB[pm], [(ones_s, c_sb3[:, cc, 0:TT], [b_const, b_csb[cc]])],
                                 first=(cc == 0), last=(cc == 7))
                        mm_group(PSf(pe2)[:, 0:TT], PSB[pe2], [(ones_s, csq3[:, qq, 0:TT], [b_const, b_csq[qq]])],
                                 first=(cc == 0), last=(cc == 7))
                    if c > 0:
                        stats(c - 1)
                    if c == 7:
                        yield
                        stats(7)
                    yield

            def g_ln():
                pm = st_banks["pm"]
                pe2 = st_banks["pe2"]
                op(ACT, lambda h: h.activation(out=mean_sb[:, 0:TT], in_=PSf(pm)[:, 0:TT], func=AF.Copy),
                   reads=[PSB[pm]], writes=[b_ln])
                op(DVE, lambda h: h.tensor_tensor(out=lnB[:, 0:TT], in0=mean_sb[:, 0:TT], in1=mean_sb[:, 0:TT], op=ALU.mult),
                   reads=[b_ln], writes=[b_ln])
                op(DVE, lambda h: h.tensor_tensor(out=lnA[:, 0:TT], in0=PSf(pe2)[:, 0:TT], in1=lnB[:, 0:TT], op=ALU.subtract),
                   reads=[b_ln, PSB[pe2]], writes=[b_ln])
                op(ACT, lambda h: h.activation(out=lnA[:, 0:TT], in_=lnA[:, 0:TT], func=AF.Sqrt, bias=eps5[:, 0:1], scale=1.0),
                   reads=[b_ln, b_const], writes=[b_ln])
                op(DVE, lambda h: h.reciprocal(out=lnA[:, 0:TT], in_=lnA[:, 0:TT]), reads=[b_ln], writes=[b_ln])
                op(DVE, lambda h: h.scalar_tensor_tensor(out=lnB[:, 0:TT], in0=mean_sb[:, 0:TT], scalar=-1.0, in1=lnA[:, 0:TT],
                                                         op0=ALU.mult, op1=ALU.mult), reads=[b_ln], writes=[b_ln])
                PsA.held.discard(pm)
                PsA.held.discard(pe2)
                yield
                for c in range(8):
                    qi = c % 2
                    op(DVE, lambda h, c=c, qi=qi: h.tensor_tensor(out=tmpc3[:, qi, 0:TT], in0=c_sb3[:, c, 0:TT], in1=lnA[:, 0:TT],
                                                                op=ALU.mult), reads=[b_csb[c], b_ln], writes=[b_tmpc[qi]])
                    op(DVE, lambda h, qi=qi: h.tensor_tensor(out=tmpc3[:, qi, 0:TT], in0=tmpc3[:, qi, 0:TT], in1=lnB[:, 0:TT],
                                                           op=ALU.add), reads=[b_tmpc[qi], b_ln], writes=[b_tmpc[qi]])
                    op(ACT, lambda h, c=c, qi=qi: h.activation(out=ca3[:, c, 0:TT], in_=tmpc3[:, qi, 0:TT], func=AF.Silu,
                                                             scale=vcol(V_CLG, c), bias=vcol(V_CLB, c)),
                       reads=[b_tmpc[qi], b_const], writes=[b_ca[c]])
                    yield

            def g_gate_a():
                for jb in range(4):
                    def ev_ga(pb, jb=jb):
                        c0_ = jb * 2
                        op(ACT, lambda h: h.activation(out=sga3[:, c0_:c0_ + 2, 0:TT], in_=PS2(pb, TT), func=AF.Sigmoid),
                           reads=[PSB[pb]], writes=[b_sga[c0_], b_sga[c0_ + 1]])
                    fm_proj2("w_in", O_GA + jb * 256, hT3, b_hT, TT, ev_ga)
                    yield

            def p_conv_out():
                for jb in range(4):
                    def ev_ya(pb, jb=jb):
                        c0_ = jb * 2
                        op(DVE, lambda h: h.tensor_tensor(out=m_a3[:, c0_:c0_ + 2, 0:TT], in0=PS2(pb, TT), in1=sga3[:, c0_:c0_ + 2, 0:TT],
                                                         op=ALU.mult),
                           reads=[PSB[pb], b_sga[c0_], b_sga[c0_ + 1]], writes=[b_ma[c0_], b_ma[c0_ + 1]])
                    fm_proj2("w_co", jb * 256, ca3, b_ca, TT, ev_ya)

            def g_qk():
                for which, (off, dst3, dbufs, scl) in enumerate([(O_Q, qr3, b_qr, 1.0), (O_K, kr3, b_kr, 0.0625)]):
                    for hh in range(4):
                        def ev_rot(pb, hh=hh, dst3=dst3, dbufs=dbufs, scl=scl):
                            x1 = PSf(pb)[:, 0:TT]
                            x2 = PSf(pb)[:, 256:256 + TT]
                            t = [rtmp3[:, i, 0:TT] for i in range(4)]
                            for i, (xx, tab) in enumerate([(x1, cosT), (x2, sinT), (x1, sinT), (x2, cosT)]):
                                op(DVE, lambda h, i=i, xx=xx, tab=tab: h.scalar_tensor_tensor(
                                    out=t[i], in0=xx, scalar=scl, in1=tab, op0=ALU.mult, op1=ALU.mult),
                                   reads=[PSB[pb], b_rot], writes=[b_rtmp[i]])
                            op(POOL, lambda h: h.tensor_tensor(out=dst3[:, 2 * hh, 0:TT], in0=t[0], in1=t[1], op=ALU.subtract),
                               reads=[b_rtmp[0], b_rtmp[1]], writes=[dbufs[2 * hh]])
                            op(POOL, lambda h: h.tensor_tensor(out=dst3[:, 2 * hh + 1, 0:TT], in0=t[2], in1=t[3], op=ALU.add),
                               reads=[b_rtmp[2], b_rtmp[3]], writes=[dbufs[2 * hh + 1]])
                        fm_proj2("w_in", off + hh * 256, hT3, b_hT, TT, ev_rot)
                        if which == 0:
                            gl_off = C_GLS if sample else C_GLP
                            gt = ctab[:, gl_off + hh * TT:gl_off + (hh + 1) * TT]
                            for dc in range(2):
                                c = 2 * hh + dc
                                op(POOL, lambda h, c=c, gt=gt: h.tensor_tensor(out=qdec3[:, c, 0:TT], in0=qr3[:, c, 0:TT], in1=gt,
                                                                             op=ALU.mult),
                                   reads=[b_qr[c], b_const], writes=[b_qdec[c]])
                        yield

            def p_v():
                for hh in range(4):
                    def ev_v(cg, s, pb, npart, hh=hh):
                        op(ACT, lambda h: h.activation(out=v_sb3[:npart, s, hh * 512:(hh + 1) * 512], in_=PSf(pb)[:npart, :],
                                                       func=AF.Copy), reads=[PSB[pb]], writes=[b_v[s * 4 + hh]])
                    tm_proj("w_in", O_V + hh * 512, 512, hT3, b_hT, 8, subs, ev_v)

            def g_retention():
                for s, (t0, npart) in enumerate(subs):
                    yield from retention_prompt(s, t0)

            def g_g():
                for jb in range(8):
                    def ev_g(pb, jb=jb):
                        c0_ = jb * 2
                        op(ACT, lambda h: h.activation(out=sgg3[:, :, 0:TT], in_=PS2(pb, TT), func=AF.Silu),
                           reads=[PSB[pb]], writes=b_sgg)
                        op(DVE, lambda h: h.tensor_tensor(out=zT3[:, c0_:c0_ + 2, 0:TT], in0=zT3[:, c0_:c0_ + 2, 0:TT],
                                                         in1=sgg3[:, :, 0:TT], op=ALU.mult),
                           reads=[b_zT[c0_], b_zT[c0_ + 1]] + b_sgg, writes=[b_zT[c0_], b_zT[c0_ + 1]])
                    fm_proj2("w_in", O_G + jb * 256, hT3, b_hT, TT, ev_g)
                    yield

            def p_yb():
                for jb in range(4):
                    def ev_gb(pb):
                        op(ACT, lambda h: h.activation(out=sgg3[:, :, 0:TT], in_=PS2(pb, TT), func=AF.Sigmoid),
                           reads=[PSB[pb]], writes=b_sgg)
                    fm_proj2("w_in", O_GB + jb * 256, hT3, b_hT, TT, ev_gb)

                    def ev_yb(oc, pb, jb=jb):
                        i = oc % 2
                        c = jb * 2 + i
                        op(DVE, lambda h: h.tensor_tensor(out=sgg3[:, i, 0:TT], in0=PSf(pb)[:, 0:TT], in1=sgg3[:, i, 0:TT],
                                                         op=ALU.mult), reads=[PSB[pb], b_sgg[i]], writes=[b_sgg[i]])
                        op(DVE, lambda h: h.tensor_tensor(out=mixin3[:, c, 0:TT], in0=sgg3[:, i, 0:TT], in1=m_a3[:, c, 0:TT],
                                                         op=ALU.add), reads=[b_sgg[i], b_ma[c]], writes=[b_mixin[c]])
                    fm_proj("w_ro", jb * 256, 256, zT3, b_zT, 16, TT, ev_yb)

            def run(g):
                for _ in g:
                    pass

            def rr(*gens):
                gens = list(gens)
                while gens:
                    for g in list(gens):
                        try:
                            next(g)
                        except StopIteration:
                            gens.remove(g)

            def chain(*gens):
                for g in gens:
                    yield from g

            if sample:
                inherit(G_CONV, G_RETC + G_SAMP + G_FFN)
                p_conv_hist()
                run(g_agav())
                flush_pending()
                p_tail()
                run(g_conv_chunks())
                run(g_gate_a())
                run(g_ln())
                p_conv_out()
                inherit(G_RETC + G_SAMP, G_CONV + G_FFN)
                run(g_qk())
                p_v()
                retention_sample()
                run(g_g())
                p_yb()
            else:
                inherit(G_CONV + G_RETC, G_FFN + G_SAMP)
                p_conv_hist()
                rr(g_agav(), g_qk())
                flush_pending()
                p_tail()
                p_v()
                if first_prompt:
                    for i in range(8):
                        op(POOL, lambda h, i=i: h.memset(S32_3[:, i, :], 0.0), writes=[b_S32[i]])
                        op(POOL, lambda h, i=i: h.memset(Sbf3[:, i, :], 0.0), writes=[b_Sbf[i]])
                rr(g_conv_chunks(), g_retention())
                rr(g_ln(), chain(g_g(), g_gate_a()))
                p_conv_out()
                p_yb()
                if last_prompt:
                    for hh in range(4):
                        dma(ACT, slot("nrp"), nrp_d[hh, :, :].rearrange("(dc p) v -> p dc v", p=128),
                            S32_3[:, 2 * hh:2 * hh + 2, :], reads=[b_S32[2 * hh], b_S32[2 * hh + 1]])

            def ev_res(cg, s, pb, npart):
                xa = xs3[:npart, s, cg * 512:(cg + 1) * 512]
                op(DVE, lambda h: h.tensor_tensor(out=xa, in0=PSf(pb)[:npart, :], in1=xa, op=ALU.add),
                   reads=[PSB[pb], b_xs[s]], writes=[b_xs[s]])
            tm_proj("w_o", 0, D, mixin3, b_mixin, 8, subs, ev_res)

            inherit(G_FFN, G_CONV + G_RETC + G_SAMP)
            if Cur.nxt is not None:
                load_tile(Cur.nxt[0], Cur.nxt[1], 1 - par)
            norm_to_hT(subs, V_LNFFN)
            for jb in range(11):
                def ev_fg(pb):
                    op(ACT, lambda h: h.activation(out=sgate3[:, :, 0:TT], in_=PS2(pb, TT), func=AF.Silu),
                       reads=[PSB[pb]], writes=b_sgate)
                fm_proj2("w_fg", jb * 256, hT3, b_hT, TT, ev_fg)

                def ev_fu(pb, jb=jb):
                    c0_ = jb * 2
                    op(DVE, lambda h: h.tensor_tensor(out=act3[:, c0_:c0_ + 2, 0:TT], in0=PS2(pb, TT), in1=sgate3[:, :, 0:TT],
                                                     op=ALU.mult),
                       reads=[PSB[pb]] + b_sgate, writes=[b_act[c0_], b_act[c0_ + 1]])
                fm_proj2("w_fu", jb * 256, hT3, b_hT, TT, ev_fu)
            tm_proj("w_fd", 0, D, act3, b_act, 22, subs, ev_res)

            def ple_pproj():
                for s, (t0, npart) in enumerate(subs):
                    pb = ps_get()
                    for kc in range(2):
                        transpose(PSf(pb)[:, kc * 128:kc * 128 + npart], PSB[pb], pin3[:npart, s, kc * 128:(kc + 1) * 128],
                                  [b_pin], ident[:npart, :npart], inc=(kc == 1))
                    op(ACT, lambda h, pb=pb, t0=t0, npart=npart: h.activation(
                        out=pT3[:, :, t0:t0 + npart], in_=PSf(pb)[:, 0:256].rearrange("p (k t) -> p k t", k=2)[:, :, 0:npart],
                        func=AF.Copy), reads=[PSB[pb]], writes=[b_pT])

                def ev_pp(cg, s, pb, npart):
                    op(ACT, lambda h: h.activation(out=pproj3[:npart, s, cg * 512:(cg + 1) * 512], in_=PSf(pb)[:npart, :],
                                                   func=AF.Copy), reads=[PSB[pb]], writes=[b_pproj])
                wv, wb = ws_next("w_pp", 0, 2, 0, 1024)
                if not Ctx.dry:
                    for cg in range(2):
                        for s, (t0, npart) in enumerate(subs):
                            pb = ps_get()
                            mm_group(PSf(pb)[:npart, :], PSB[pb],
                                     [(pT3[:, k, t0:t0 + npart], wv[:, k, cg * 512:(cg + 1) * 512], [wb, b_pT]) for k in range(2)])
                            ev_pp(cg, s, pb, npart)

            norm_to_hT(subs, V_LNPLE, filler=ple_pproj)
            if Cur.nxt is not None:
                nsubs = [(0, TS)] if Cur.nxt[0] else [(0, 128), (128, 128)]
                norm_to_hT(nsubs, V_LNMIX, r3(xs_l[1 - par], 2), b_xs_l[1 - par], hT3_l[1 - par], b_hT_l[1 - par])
                Cur.normed = True

            def ev_pg(cg, s, pb, npart):
                gi = (cg * 2 + s) % 2
                ga = gate_sb[gi][:npart, :]
                xa = xs3[:npart, s, cg * 512:(cg + 1) * 512]
                op(ACT, lambda h: h.activation(out=ga, in_=PSf(pb)[:npart, :], func=AF.Sigmoid),
                   reads=[PSB[pb]], writes=[b_gsb[gi]])
                op(DVE, lambda h: h.tensor_tensor(out=ga, in0=ga, in1=pproj3[:npart, s, cg * 512:(cg + 1) * 512], op=ALU.mult),
                   reads=[b_gsb[gi], b_pproj], writes=[b_gsb[gi]])
                op(DVE, lambda h: h.tensor_tensor(out=xa, in0=xa, in1=ga, op=ALU.add),
                   reads=[b_gsb[gi], b_xs[s]], writes=[b_xs[s]])
            tm_proj("w_pg", 0, D, hT3, b_hT, 8, subs, ev_pg)

            for s, (t0, npart) in enumerate(subs):
                xa = xs3[:npart, s, :]
                rstd, bs = rms_stats(xa, npart, [b_xs[s]], si=s)
                op(DVE, lambda h, xa=xa, rstd=rstd, npart=npart, s=s: h.scalar_tensor_tensor(
                    out=xa, in0=xa, scalar=rstd, in1=gfin[:npart, :], op0=ALU.mult, op1=ALU.mult),
                   reads=[b_xs[s], bs, b_const], writes=[b_xs[s]])
                if sample:
                    Cur.pending.append(lambda par=par, xs3=xs3, b_xs=b_xs: dma(
                        ACT, slot(f"yout{par}_0"), ys_d[:, :], xs3[:TS, 0, :], reads=[b_xs[0]]))
                else:
                    r0 = ti * T + s * 128
                    Cur.pending.append(lambda par=par, s=s, r0=r0, xs3=xs3, b_xs=b_xs: dma(
                        ACT, slot(f"yout{par}_{s}"), yp_d[r0:r0 + 128, :], xs3[:, s, :], reads=[b_xs[s]]))

        def kd_build(s_off, npart, kdcol_off):
            pb = ps_get()
            for c in range(8):
                transpose(PSh(pb)[:npart, c * 128:(c + 1) * 128], PSB[pb], kr3[:, c, s_off:s_off + npart], [b_kr[c]],
                          identb, inc=(c == 7))
            for hh in range(4):
                op(ACT, lambda h, hh=hh, pb=pb: h.activation(
                    out=kd[:npart, hh * 256:(hh + 1) * 256], in_=PSh(pb)[:npart, hh * 256:(hh + 1) * 256], func=AF.Identity,
                    scale=ctab[:npart, kdcol_off + hh:kdcol_off + hh + 1], bias=0.0), reads=[PSB[pb], b_const], writes=[b_kd])

        def gn_and_T(pb, hh, npart):
            base = 40 + hh * 12
            bs = b_small_l[2 + hh]
            op(DVE, lambda h: h.bn_stats(out=small[:npart, base:base + 6], in_=PSf(pb)[:npart, :]),
               reads=[PSB[pb]], writes=[bs])
            op(DVE, lambda h: h.bn_aggr(out=small[:npart, base + 6:base + 8], in_=small[:npart, base:base + 6]),
               reads=[bs], writes=[bs])
            op(DVE, lambda h: h.tensor_scalar(out=small[:npart, base + 8:base + 9], in0=small[:npart, base + 7:base + 8],
                                              scalar1=1e-5, scalar2=None, op0=ALU.add), reads=[bs], writes=[bs])
            op(POOL, lambda h: h.tensor_tensor(out=small[:npart, base + 9:base + 10], in0=small[:npart, base + 8:base + 9],
                                               in1=mhalf[:npart, 0:1], op=ALU.pow), reads=[bs, b_const], writes=[bs])
            op(DVE, lambda h: h.tensor_scalar(out=on[:npart, hh * 512:(hh + 1) * 512], in0=PSf(pb)[:npart, :],
                                              scalar1=small[:npart, base + 6:base + 7],
                                              scalar2=small[:npart, base + 9:base + 10],
                                              op0=ALU.subtract, op1=ALU.mult),
               reads=[PSB[pb], bs], writes=[b_on[hh]])

        def on_to_zT(t0, npart):
            for half in range(2):
                pb = ps_get()
                for i in range(8):
                    fc = half * 8 + i
                    transpose(PSh(pb)[:, i * 128:i * 128 + npart], PSB[pb], on[:npart, fc * 128:(fc + 1) * 128],
                              [b_on[fc // 4]], identb[:npart, :npart], inc=(i == 7))
                for i in range(8):
                    fc = half * 8 + i
                    op(ACT, lambda h, pb=pb, i=i, fc=fc: h.activation(
                        out=zT3[:, fc, t0:t0 + npart], in_=PSh(pb)[:, i * 128:i * 128 + npart], func=AF.Identity,
                        scale=vcol(V_GNG, fc), bias=vcol(V_GNB, fc)), reads=[PSB[pb], b_const], writes=[b_zT[fc]])

        def retention_prompt(s, t0):
            kd_build(t0, 128, C_KDP)
            for hh in range(4):
                pb = ps_get()
                mm_group(PSf(pb)[:, 0:128], PSB[pb],
                         [(kr3[:, 2 * hh + dc, t0:t0 + 128], qr3[:, 2 * hh + dc, t0:t0 + 128],
                           [b_kr[2 * hh + dc], b_qr[2 * hh + dc]]) for dc in range(2)])
                op(DVE, lambda h, pb=pb, hh=hh: h.tensor_tensor(
                    out=sT[hh][:, :], in0=PSf(pb)[:, 0:128], in1=ctab[:, C_DECP + hh * 128:C_DECP + (hh + 1) * 128],
                    op=ALU.mult), reads=[PSB[pb], b_const], writes=[b_sT[hh]])
            yield
            for hh in range(4):
                po = ps_get()
                vv = v_sb3[:, s, hh * 512:(hh + 1) * 512]
                terms = [(sT[hh][:, :], vv, [b_sT[hh], b_v[s * 4 + hh]])]
                for dc in range(2):
                    terms.append((qdec3[:, 2 * hh + dc, t0:t0 + 128], Sbf3[:, 2 * hh + dc, :],
                                  [b_qdec[2 * hh + dc], b_Sbf[2 * hh + dc]]))
                mm_group(PSf(po)[:, :], PSB[po], terms)
                gn_and_T(po, hh, 128)
                for dc in range(2):
                    si = 2 * hh + dc
                    pst = ps_get()
                    mm_group(PSf(pst)[:, :], PSB[pst],
                             [(kd[:, hh * 256 + dc * 128:hh * 256 + (dc + 1) * 128], vv, [b_kd, b_v[s * 4 + hh]])])
                    op(DVE, lambda h, si=si, pst=pst, hh=hh: h.scalar_tensor_tensor(
                        out=S32_3[:, si, :], in0=S32_3[:, si, :], scalar=G128[hh], in1=PSf(pst)[:, :],
                        op0=ALU.mult, op1=ALU.add), reads=[PSB[pst], b_S32[si]], writes=[b_S32[si]])
                    op(ACT, lambda h, si=si: h.activation(out=Sbf3[:, si, :], in_=S32_3[:, si, :], func=AF.Copy),
                       reads=[b_S32[si]], writes=[b_Sbf[si]])
                yield
            on_to_zT(t0, 128)
            yield

        def retention_sample():
            kd_build(0, TS, C_KDS)
            po = []
            for hh in range(4):
                pb = ps_get()
                mm_group(PSf(pb)[:TS, 0:TS], PSB[pb],
                         [(kr3[:, 2 * hh + dc, 0:TS], qr3[:, 2 * hh + dc, 0:TS],
                           [b_kr[2 * hh + dc], b_qr[2 * hh + dc]]) for dc in range(2)])
                op(DVE, lambda h, pb=pb, hh=hh: h.tensor_tensor(
                    out=sT[hh][:TS, 0:TS], in0=PSf(pb)[:TS, 0:TS], in1=ctab[:TS, C_DECS + hh * 64:C_DECS + (hh + 1) * 64],
                    op=ALU.mult), reads=[PSB[pb], b_const], writes=[b_sT[hh]])
            for hh in range(4):
                p = ps_get()
                PsA.held.add(p)
                po.append(p)
                mm_group(PSf(p)[:TS, :], PSB[p], [(sT[hh][:TS, 0:TS], v_sb3[:TS, 0, hh * 512:(hh + 1) * 512],
                                                  [b_sT[hh], b_v[hh]])], first=True, last=False)
            units = [(b, hh) for b in range(NSEQ_S) for hh in range(4)]
            NU = len(units)

            def load_unit(u):
                b, hh = units[u]
                i = u % 6
                dma(SP, slot(f"S0_{i}"), S0[i].rearrange("p (dc v) -> p dc v", dc=2),
                    str_d[b, hh, :, :].rearrange("(dc p) v -> p dc v", p=128), writes=[b_S0[i]])
            for u0 in range(3):
                load_unit(u0)
            for u, (b, hh) in enumerate(units):
                if u + 3 < NU:
                    load_unit(u + 3)
                i4 = u % 6
                i2 = u % 2
                if hh == 0:
                    bi = b % 2
                    op(POOL, lambda h, bi=bi, b=b: h.tensor_tensor(
                        out=qxb[bi].rearrange("p (c t) -> p c t", c=8), in0=qdec3[:, :, 0:TS],
                        in1=ctab[:, C_MROW + 60 - 4 * b:C_MROW + 124 - 4 * b].unsqueeze(1).to_broadcast([128, 8, TS]),
                        op=ALU.mult), reads=b_qdec + [b_const], writes=[b_qxb[bi]])
                    op(ACT, lambda h, bi=bi, b=b: h.activation(out=kdx[bi][:TS, :], in_=kd[:TS, :], func=AF.Identity,
                                                             scale=ctab[:TS, C_MCOL + b:C_MCOL + b + 1], bias=0.0),
                       reads=[b_kd, b_const], writes=[b_kdx[bi]])
                bi = b % 2
                qx3 = qxb[bi].rearrange("p (c t) -> p c t", c=8)
                S0v = S0[i4].rearrange("p (dc v) -> p dc v", dc=2)
                S0bv = S0b[i2].rearrange("p (dc v) -> p dc v", dc=2)
                op(ACT, lambda h, i2=i2, i4=i4: h.activation(out=S0b[i2][:, :], in_=S0[i4][:, :], func=AF.Copy),
                   reads=[b_S0[i4]], writes=[b_S0b[i2]])
                lastu = (b == NSEQ_S - 1)
                mm_group(PSf(po[hh])[:TS, :], PSB[po[hh]],
                         [(qx3[:, 2 * hh + dc, :], S0bv[:, dc, :], [b_qxb[bi], b_S0b[i2]]) for dc in range(2)],
                         first=False, last=lastu)
                for dc in range(2):
                    pst = ps_get()
                    mm_group(PSf(pst)[:, :], PSB[pst],
                             [(kdx[bi][:TS, hh * 256 + dc * 128:hh * 256 + (dc + 1) * 128],
                               v_sb3[:TS, 0, hh * 512:(hh + 1) * 512], [b_kdx[bi], b_v[hh]])])
                    op(DVE, lambda h, dc=dc, pst=pst, hh=hh, S0v=S0v: h.scalar_tensor_tensor(
                        out=S0v[:, dc, :], in0=S0v[:, dc, :], scalar=G4[hh], in1=PSf(pst)[:, :],
                        op0=ALU.mult, op1=ALU.add), reads=[PSB[pst], b_S0[i4]], writes=[b_S0[i4]])
                dma(SP, slot(f"S1_{i4}"), nrs_d[b, hh, :, :].rearrange("(dc p) v -> p dc v", p=128), S0v,
                    reads=[b_S0[i4]])
            for hh in range(4):
                gn_and_T(po[hh], hh, TS)
                PsA.held.discard(po[hh])
            on_to_zT(0, TS)

        def emit_all():
            PsA.i = 0
            PsA.held = set()
            WS.cons = 0
            Cur.normed = False
            dma(SP, slot("const"), ctab[:, :], ctab_d[:, :], writes=[b_const])
            dma(SP, slot("const"), vecs[:, :], vecs_d[:, :], writes=[b_const])
            dma(SP, slot("const"), gfin[:, :], gfin_d[:, :], writes=[b_const])
            op(DVE, lambda h: h.tensor_copy(out=identb[:, :], in_=ident), reads=[b_const], writes=[b_const])
            op(DVE, lambda h: h.memset(epsc[:, :], EPS), writes=[b_const])
            op(DVE, lambda h: h.memset(eps5[:, :], 1e-5), writes=[b_const])
            op(DVE, lambda h: h.memset(mhalf[:, :], -0.5), writes=[b_const])
            order = [(False, ti) for ti in range(NT)] + [(True, 0)]
            load_tile(order[0][0], order[0][1], 0)
            Cur.pending = []
            for n_, (smp, ti) in enumerate(order):
                Cur.par = n_ % 2
                Cur.nxt = order[n_ + 1] if n_ + 1 < len(order) else None
                do_tile(smp, ti)
            flush_pending()

        Ctx.dry = True
        emit_all()
        Ctx.dry = False
        WS.nb = len(WS.specs) // (NT + 1)
        assert WS.nb * (NT + 1) == len(WS.specs)
        WS.cache = nc.dram_tensor("wcache", [WS.nb, 128, WBLK], BF16).ap()
        WS.wall = din("wall", [WS.nb, 128, WBLK])
        WS.cidx = {sp: i for i, sp in enumerate(WS.specs[:WS.nb])}
        assert len(WS.cidx) == WS.nb
        for t_ in range(NT + 1):
            assert sorted(WS.specs[t_ * WS.nb:(t_ + 1) * WS.nb]) == sorted(WS.specs[:WS.nb])
        emit_all()
        assert WS.cons == len(WS.specs)

        for sl in slots.values():
            SP.prog.append(("w", sl, sl.count))

        engs = [PE, ACT, DVE, POOL]
        for e in engs:
            e.sem = es.enter_context(nc.semaphore(f"s_{e.name}"))
        for name, sl in slots.items():
            sl.sem = es.enter_context(nc.semaphore(f"d_{name}"))

        def replay(eng, h):
            fuse = eng in (ACT, DVE, POOL, SP)
            pend = []
            for it in eng.prog:
                if it[0] == "w":
                    if fuse:
                        pend.append((it[1].sem, it[2]))
                    else:
                        h.wait_ge(it[1].sem, it[2])
                    continue
                for sm_, v_ in pend[:-1]:
                    h.wait_ge(sm_, v_)
                if it[0] == "o":
                    ins = it[1](h)
                    if pend:
                        ins._wait_ge(pend[-1][0], pend[-1][1])
                    if it[2]:
                        ins.then_inc(eng.sem, 1)
                else:
                    ins = h.dma_start(out=it[1], in_=it[2])
                    if pend:
                        ins._wait_ge(pend[-1][0], pend[-1][1])
                    ins.then_inc(it[3].sem, 16)
                pend = []
            for sm_, v_ in pend:
                h.wait_ge(sm_, v_)

        with nc.Block() as block:
            @block.sync
            def _(h):
                replay(SP, h)

            @block.tensor
            def _(h):
                replay(PE, h)

            @block.scalar
            def _(h):
                replay(ACT, h)

            @block.vector
            def _(h):
                replay(DVE, h)

            @block.gpsimd
            def _(h):
                replay(POOL, h)
        print("instr counts:", {e.name: len(e.prog) for e in engs + [SP]})
    nc._ws_specs = list(WS.specs[:WS.nb])
    return nc


_CACHE = {}


def kernel(**inputs):
    f = lambda a: np.ascontiguousarray(np.asarray(a, dtype=np.float32))
    x_prompt = f(inputs["x_prompt"])
    x_sample = f(inputs["x_sample"])
    state_conv = f(inputs["state_conv"])[0]
    state_ret = f(inputs["state_ret"])[0]
    p_prompt = f(inputs["p_prompt"])[0]
    p_sample = f(inputs["p_sample"])[0]

    if "nc" not in _CACHE:
        _CACHE["nc"] = build_program()
        _CACHE["consts"] = _make_consts()
    nc = _CACHE["nc"]
    ctab, rotp, rots = _CACHE["consts"]

    def cols(v, n):
        return np.ascontiguousarray(np.asarray(v, np.float32).reshape(n, 128).T)

    vecs = np.zeros((128, NV), np.float32)
    vecs[:, V_LNMIX:V_LNMIX + 8] = cols(inputs["ln_mix_g"][0], 8)
    vecs[:, V_LNFFN:V_LNFFN + 8] = cols(inputs["ln_ffn_g"][0], 8)
    vecs[:, V_LNPLE:V_LNPLE + 8] = cols(inputs["ln_ple_g"][0], 8)
    vecs[:, V_CB:V_CB + 8] = cols(inputs["conv_b"][0], 8)
    vecs[:, V_CLG:V_CLG + 8] = cols(inputs["conv_ln_g"][0], 8)
    vecs[:, V_CLB:V_CLB + 8] = cols(inputs["conv_ln_b"][0], 8)
    vecs[:, V_GNG:V_GNG + 16] = cols(inputs["ret_gn_g"][0], 16)
    vecs[:, V_GNB:V_GNB + 16] = cols(inputs["ret_gn_b"][0], 16)
    cw = np.asarray(inputs["conv_w"], np.float32)[0]
    vecs[:, V_CW:V_CW + 8 * CW] = cw.reshape(CW, 8, 128).transpose(2, 1, 0).reshape(128, 8 * CW)
    gfin = np.ascontiguousarray(np.broadcast_to(np.asarray(inputs["ln_final_g"], np.float32)[None, :], (128, D)))

    Wh = {
        "w_in": f(inputs["w_in"])[0], "w_co": f(inputs["w_conv_out"])[0], "w_ro": f(inputs["w_ret_out"])[0],
        "w_o": f(inputs["w_o"])[0], "w_fg": f(inputs["w_ffn_gate"])[0], "w_fu": f(inputs["w_ffn_up"])[0],
        "w_fd": f(inputs["w_ffn_down"])[0], "w_pg": f(inputs["w_ple_gate"])[0], "w_pp": f(inputs["w_ple_proj"])[0],
    }
    specs = nc._ws_specs
    wall = np.zeros((len(specs), 128, WBLK), np.float32)
    for jj, (name, k0, nk, c0, ncol) in enumerate(specs):
        if name == "convD":
            continue
        blk = Wh[name][k0 * 128:(k0 + nk) * 128, c0:c0 + ncol]
        wall[jj, :, :nk * ncol] = blk.reshape(nk, 128, ncol).transpose(1, 0, 2).reshape(128, nk * ncol)
    shared = {"wall": wall, "ctab": ctab, "rotp": rotp, "rots": rots, "vecs": vecs, "gfin": gfin}
    in_maps = []
    for i in range(NCORE):
        m = dict(shared)
        m["xp"] = x_prompt[i]
        m["xs"] = x_sample[i * NSEQ_S:(i + 1) * NSEQ_S].reshape(TS, D)
        m["stc"] = state_conv[i * NSEQ_S:(i + 1) * NSEQ_S]
        m["str"] = state_ret[i * NSEQ_S:(i + 1) * NSEQ_S]
        m["pp"] = p_prompt[i]
        m["ps"] = p_sample[i * NSEQ_S:(i + 1) * NSEQ_S].reshape(TS, 256)
        in_maps.append(m)
    res = run_bass_kernel_spmd(nc, in_maps, core_ids=list(range(NCORE)))
    R = res.results
    y_prompt = np.stack([R[i]["yp"] for i in range(NCORE)], 0).astype(np.float32)
    y_sample = np.concatenate([R[i]["ys"].reshape(NSEQ_S, DEC, D) for i in range(NCORE)], 0).astype(np.float32)
    ncp = np.stack([R[i]["ncp"] for i in range(NCORE)], 0)[None].astype(np.float32)
    nrp = np.stack([R[i]["nrp"] for i in range(NCORE)], 0)[None].astype(np.float32)
    ncs = np.concatenate([R[i]["ncs"] for i in range(NCORE)], 0)[None].astype(np.float32)
    nrs = np.concatenate([R[i]["nrs"] for i in range(NCORE)], 0)[None].astype(np.float32)
    return (y_prompt, y_sample, ncp, nrp, ncs, nrs)
```

```python
import numpy as np
import concourse.bass as bass
import concourse.mybir as mybir
from concourse.bass_utils import run_bass_kernel_spmd

F32 = mybir.dt.float32
BF16 = mybir.dt.bfloat16
AF = mybir.ActivationFunctionType
ALU = mybir.AluOpType

D = 1024
SEQ = 2048
NCORE = 8
NSEQ_S = 16
DEC = 4
TS = NSEQ_S * DEC
CW = 31
PAST = 16384
DFF = 2816
T = 256
NT = SEQ // T
WBLK = 2048
NWST = 3
NWBF = 3
AHEAD = 2
NCAST = 1
EPS = 1e-6

O_AV, O_AG, O_Q, O_K, O_V, O_G, O_GA, O_GB = 0, 1024, 2048, 3072, 4096, 6144, 8192, 9216

C_ID = 0
C_DECP = C_ID + 128
C_DECS = C_DECP + 512
C_GLP = C_DECS + 256
C_GLS = C_GLP + 1024
C_KDP = C_GLS + 256
C_KDS = C_KDP + 4
C_MCOL = C_KDS + 4
C_MROW = C_MCOL + 16
C_ONES = C_MROW + 128
NCT = C_ONES + 128

V_LNMIX, V_LNFFN, V_LNPLE, V_CB, V_CLG, V_CLB = 0, 8, 16, 24, 32, 40
V_GNG, V_GNB = 48, 64
V_CW = 80
NV = V_CW + 8 * CW


def _log_g():
    h = np.arange(4, dtype=np.float64)
    return np.log1p(-np.exp2(-5.0 - h))


def _make_consts():
    lg = _log_g()
    ct = np.zeros((128, NCT), np.float64)
    ct[:, C_ID:C_ID + 128] = np.eye(128)
    m = np.arange(128)[:, None]
    l = np.arange(128)[None, :]
    for h in range(4):
        dec = np.where(l >= m, np.exp(lg[h] * np.maximum(l - m, 0)), 0.0)
        ct[:, C_DECP + h * 128:C_DECP + (h + 1) * 128] = dec
        ms = np.arange(64)[:, None]
        ls = np.arange(64)[None, :]
        same = (ms // 4) == (ls // 4)
        decs = np.where(same & (ls >= ms), np.exp(lg[h] * np.maximum(ls - ms, 0)), 0.0)
        ct[:64, C_DECS + h * 64:C_DECS + (h + 1) * 64] = decs
        t = np.arange(256)
        ct[:, C_GLP + h * 256:C_GLP + (h + 1) * 256] = np.exp(lg[h] * ((t % 128) + 1.0))[None, :]
        ts = np.arange(64)
        ct[:, C_GLS + h * 64:C_GLS + (h + 1) * 64] = np.exp(lg[h] * ((ts % 4) + 1.0))[None, :]
        ct[:, C_KDP + h] = np.exp(lg[h] * (127.0 - np.arange(128)))
        ct[:64, C_KDS + h] = np.exp(lg[h] * (3.0 - (np.arange(64) % 4)))
    for b in range(16):
        ct[:64, C_MCOL + b] = ((np.arange(64) // 4) == b)
    ct[:, C_MROW + 60:C_MROW + 64] = 1.0
    ct[:, C_ONES:C_ONES + 128] = 1.0 / 1024.0
    half = 128
    inv = (1.0 / (np.float32(10000.0) ** (np.arange(half, dtype=np.float32) / np.float32(half)))).astype(np.float32)
    posp = np.arange(SEQ, dtype=np.float32)
    angp = (posp[None, :] * inv[:, None]).astype(np.float32)
    rotp = np.stack([np.cos(angp), np.sin(angp)], axis=1).astype(np.float32)
    poss = (np.float32(PAST) + (np.arange(TS) % 4).astype(np.float32)).astype(np.float32)
    angs = (poss[None, :] * inv[:, None]).astype(np.float32)
    rots = np.stack([np.cos(angs), np.sin(angs)], axis=1).astype(np.float32)
    return ct.astype(np.float32), rotp, rots


class Buf:
    __slots__ = ("w", "r")

    def __init__(self):
        self.w = None
        self.r = {}


class Eng:
    def __init__(self, name, sync_self):
        self.name = name
        self.sem = None
        self.count = 0
        self.waited = {}
        self.sync_self = sync_self
        self.prog = []


class Slot:
    def __init__(self):
        self.sem = None
        self.count = 0


class Ctx:
    dry = False


def _deps(reads, writes):
    d = {}

    def add(s, v):
        if d.get(s, 0) < v:
            d[s] = v
    for b in reads:
        if b.w is not None:
            add(*b.w)
    for b in writes:
        if b.w is not None:
            add(*b.w)
        for s, v in b.r.items():
            add(s, v)
    return d


def _wait(eng, d):
    for s, v in d.items():
        if s is eng and not eng.sync_self:
            continue
        if eng.waited.get(s, 0) < v:
            eng.prog.append(("w", s, v))
            eng.waited[s] = v


def _record(ev, reads, writes):
    s, v = ev
    for b in reads:
        if b.r.get(s, 0) < v:
            b.r[s] = v
    for b in writes:
        b.w = ev
        b.r = {}


def op(eng, fn, reads=(), writes=(), inc=True):
    if Ctx.dry:
        return
    _wait(eng, _deps(reads, writes))
    eng.prog.append(("o", fn, inc))
    if inc:
        eng.count += 1
        ev = (eng, eng.count)
    else:
        ev = (eng, eng.count + 1)
    _record(ev, reads, writes)


def dma(q, slot, out, in_, reads=(), writes=()):
    if Ctx.dry:
        return
    _wait(q, _deps(reads, writes))
    q.prog.append(("d", out, in_, slot))
    slot.count += 16
    _record((slot, slot.count), reads, writes)


def inherit(new_bufs, old_bufs):
    if Ctx.dry:
        return
    d = {}
    for b in old_bufs:
        if b.w is not None and d.get(b.w[0], 0) < b.w[1]:
            d[b.w[0]] = b.w[1]
        for s, v in b.r.items():
            if d.get(s, 0) < v:
                d[s] = v
    for b in new_bufs:
        for s, v in d.items():
            if b.r.get(s, 0) < v:
                b.r[s] = v


def build_program():
    nc = bass.Bass("TRN2", target_bir_lowering=False)

    def din(name, shape):
        return nc.dram_tensor(name, list(shape), F32, kind="ExternalInput").ap()

    def dout(name, shape):
        return nc.dram_tensor(name, list(shape), F32, kind="ExternalOutput").ap()

    xp_d = din("xp", [SEQ, D])
    xs_d = din("xs", [TS, D])
    stc_d = din("stc", [NSEQ_S, 30, D])
    str_d = din("str", [NSEQ_S, 4, 256, 512])
    pp_d = din("pp", [SEQ, 256])
    ps_d = din("ps", [TS, 256])
    ctab_d = din("ctab", [128, NCT])
    rotp_d = din("rotp", [128, 2, SEQ])
    rots_d = din("rots", [128, 2, TS])
    vecs_d = din("vecs", [128, NV])
    gfin_d = din("gfin", [128, D])

    yp_d = dout("yp", [SEQ, D])
    ys_d = dout("ys", [TS, D])
    ncp_d = dout("ncp", [30, D])
    nrp_d = dout("nrp", [4, 256, 512])
    ncs_d = dout("ncs", [NSEQ_S, 30, D])
    nrs_d = dout("nrs", [NSEQ_S, 4, 256, 512])

    lg = _log_g()
    G128 = [float(np.float32(np.exp(lg[h] * 128.0))) for h in range(4)]
    G4 = [float(np.float32(np.exp(lg[h] * 4.0))) for h in range(4)]

    import contextlib
    es = contextlib.ExitStack()
    with es:
        ARENA_W = 52600
        arena = es.enter_context(nc.sbuf_tensor("arena", [128, ARENA_W], F32))
        psum = [es.enter_context(nc.psum_tensor(f"ps{i}", [128, 512], F32)) for i in range(8)]
        PSB = [Buf() for _ in range(8)]

        class Ar:
            off = 0
            peak = 0

        def alloc(nelem, dtype=F32):
            words = nelem if dtype == F32 else (nelem + 1) // 2
            words = (words + 1) // 2 * 2
            o = Ar.off
            Ar.off += words
            Ar.peak = max(Ar.peak, Ar.off)
            assert Ar.off <= ARENA_W, f"arena overflow {Ar.off}"
            v = arena[:, o:o + words]
            if dtype != F32:
                v = v.bitcast(dtype)[:, 0:nelem]
            else:
                v = v[:, 0:nelem]
            return v

        def r3(ap, a):
            return ap.rearrange("p (a b) -> p a b", a=a)

        ctab = alloc(NCT)
        vecs = alloc(NV)
        gfin = alloc(D)
        identb = alloc(128, BF16)
        rot_l = [alloc(2 * T) for _ in range(2)]
        xs_l = [alloc(2 * D) for _ in range(2)]
        pin_l = [alloc(2 * 256) for _ in range(2)]
        xn_l = [alloc(D), alloc(D)]
        xn = xn_l[0]
        hT_l = [alloc(8 * T, BF16) for _ in range(2)]
        m_a = alloc(8 * T, BF16)
        mixin = alloc(8 * T, BF16)
        small = alloc(96)
        wst = [alloc(WBLK) for _ in range(NWST)]
        wbf = [alloc(WBLK, BF16) for _ in range(NWBF)]
        S32 = alloc(8 * 512)
        Sbf = alloc(8 * 512, BF16)
        utail = alloc(8 * 64)
        utok = xn
        epsc = alloc(2)
        eps5 = alloc(2)
        mhalf = alloc(2)
        uprev = alloc(8 * 32, BF16)
        mark = Ar.off
        qr = alloc(8 * T, BF16)
        kr = alloc(8 * T, BF16)
        qdec = alloc(8 * T, BF16)
        rtmp = alloc(4 * T)
        v_sb = alloc(2 * 2048, BF16)
        kd = alloc(1024, BF16)
        sT = [alloc(128, BF16) for _ in range(4)]
        on = alloc(2048, BF16)
        sgg = alloc(2 * T)
        zT = alloc(16 * T, BF16)
        retc_end = Ar.off
        sg = alloc(2 * T)
        ucat = alloc(8 * 544, BF16)
        c_sb = alloc(8 * T)
        csq = alloc(2 * T)
        lnA = alloc(T)
        lnB = alloc(T)
        mean_sb = alloc(T)
        tmpc = alloc(2 * T)
        ca = alloc(8 * T, BF16)
        sga = alloc(8 * T, BF16)
        scg = alloc(D)
        conv_end = Ar.off
        Ar.off = retc_end
        S0 = [alloc(2 * 512) for _ in range(6)]
        S0b = [alloc(2 * 512, BF16) for _ in range(2)]
        kdx = [alloc(1024, BF16) for _ in range(2)]
        qxb = [alloc(8 * 64, BF16) for _ in range(2)]
        ret_end = Ar.off
        Ar.off = mark
        act = alloc(22 * T, BF16)
        sgate = alloc(2 * T)
        pT = alloc(2 * T, BF16)
        pproj = alloc(2 * D)
        gate_sb = [alloc(512) for _ in range(2)]
        ffn_end = Ar.off
        assert ffn_end <= retc_end, (ffn_end, retc_end)
        print("arena peak words", Ar.peak, "of", ARENA_W)

        ident = ctab[:, C_ID:C_ID + 128]
        ones_s = ctab[:, C_ONES:C_ONES + 128]
        hT3_l = [r3(hT_l[0], 8), r3(hT_l[1], 8)]
        m_a3 = r3(m_a, 8)
        mixin3 = r3(mixin, 8)
        S32_3 = r3(S32, 8)
        Sbf3 = r3(Sbf, 8)
        utail3 = r3(utail, 8)
        uprev3 = r3(uprev, 8)
        sg3 = r3(sg, 2)
        ucat3 = r3(ucat, 8)
        c_sb3 = r3(c_sb, 8)
        csq3 = r3(csq, 2)
        tmpc3 = r3(tmpc, 2)
        ca3 = r3(ca, 8)
        sga3 = r3(sga, 8)
        qr3 = r3(qr, 8)
        kr3 = r3(kr, 8)
        qdec3 = r3(qdec, 8)
        rtmp3 = r3(rtmp, 4)
        v_sb3 = r3(v_sb, 2)
        sgg3 = r3(sgg, 2)
        zT3 = r3(zT, 16)
        act3 = r3(act, 22)
        sgate3 = r3(sgate, 2)
        pT3 = r3(pT, 2)
        pproj3 = r3(pproj, 2)

        def vcol(off, c):
            return vecs[:, off + c:off + c + 1]

        B = {}

        def nb(name, n=None):
            if n is None:
                B[name] = Buf()
            else:
                B[name] = [Buf() for _ in range(n)]
            return B[name]

        b_const = nb("const")
        b_rot_l = [Buf(), Buf()]
        b_xs_l = [[Buf(), Buf()], [Buf(), Buf()]]
        b_pin_l = [Buf(), Buf()]
        b_xn_l = [Buf(), Buf()]
        b_xn = b_xn_l[0]
        b_hT_l = [[Buf() for _ in range(8)], [Buf() for _ in range(8)]]
        b_ma = nb("m_a", 8)
        b_mixin = nb("mixin", 8)
        b_small_l = [Buf() for _ in range(6)]
        b_wst = nb("wst", NWST)
        b_wbf = nb("wbf", NWBF)
        b_S32 = nb("S32", 8)
        b_Sbf = nb("Sbf", 8)
        b_utail = nb("utail")
        b_utok = b_xn
        b_uprev = nb("uprev", 8)
        b_sg = nb("sg", 2)
        b_ucat = nb("ucat", 8)
        b_csb = nb("c_sb", 8)
        b_csq = nb("csq", 2)
        b_ln = nb("ln")
        b_tmpc = nb("tmpc", 2)
        b_ca = nb("ca", 8)
        b_scg = nb("scg")
        b_sga = nb("sga", 8)
        G_CONV = b_sg + b_ucat + b_csb + b_csq + [b_ln] + b_tmpc + b_ca + [b_scg] + b_sga
        b_qr = nb("qr", 8)
        b_kr = nb("kr", 8)
        b_qdec = nb("qdec", 8)
        b_rtmp = nb("rtmp", 4)
        b_v = nb("v", 8)
        b_kd = nb("kd")
        b_sT = nb("sT", 4)
        b_on = nb("on", 4)
        b_sgg = nb("sgg", 2)
        b_zT = nb("zT", 16)
        b_S0 = nb("S0", 6)
        b_S0b = nb("S0b", 2)
        b_kdx = nb("kdx", 2)
        b_qxb = nb("qxb", 2)
        G_RETC = b_qr + b_kr + b_qdec + b_rtmp + b_v + [b_kd] + b_sT + b_on + b_sgg + b_zT
        G_SAMP = b_S0 + b_S0b + b_kdx + b_qxb
        b_act = nb("act", 22)
        b_sgate = nb("sgate", 2)
        b_pT = nb("pT")
        b_pproj = nb("pproj")
        b_gsb = nb("gate_sb", 2)
        G_FFN = b_act + b_sgate + [b_pT, b_pproj] + b_gsb

        PE = Eng("pe", False)
        ACT = Eng("act", True)
        DVE = Eng("dve", True)
        POOL = Eng("pool", True)
        SP = Eng("sp", False)
        slots = {}

        def slot(name):
            if name not in slots:
                slots[name] = Slot()
            return slots[name]

        class PsA:
            i = 0
            held = set()

        def ps_get():
            while True:
                b = PsA.i % 8
                PsA.i += 1
                if b not in PsA.held:
                    return b

        def PSf(b):
            return psum[b][:, :]

        def PSh(b):
            return psum[b][:, :].bitcast(BF16)

        class WS:
            specs = []
            issued = 0
            cons = 0
            nb = 0
            cache = None
            cidx = {}
            wall = None
            nst = 0
            ncast = 0
            dma_issued = 0
            stg = {}

        NWX = NWBF + 2 * NWST
        wbx = list(wbf) + [wst[i][:, h * (WBLK // 2):(h + 1) * (WBLK // 2)].bitcast(BF16) for i in range(NWST) for h in range(2)]
        b_wbx = list(b_wbf) + [Buf() for _ in range(2 * NWST)]
        b_wc = {}

        def wslot(j):
            if j < NCAST * WS.nb:
                return wbf[j % NWBF], b_wbf[j % NWBF]
            return wbx[j % NWX], b_wbx[j % NWX]

        def ws_meta(j):
            name, k0, nk, c0, ncol = WS.specs[j]
            jj = WS.cidx[WS.specs[j]]
            t = j // WS.nb
            ctile = 0 if name == "convD" else jj % NCAST
            return name, k0, nk, c0, ncol, jj, t, ctile

        def ws_is_fp32(j):
            name, k0, nk, c0, ncol, jj, t, ctile = ws_meta(j)
            return not (t > ctile or name == "convD")

        def ws_issue_dma(j):
            name, k0, nk, c0, ncol, jj, t, ctile = ws_meta(j)
            if t > ctile or name == "convD":
                return
            n = nk * ncol
            si = WS.nst % NWST
            WS.nst += 1
            WS.stg[j] = si
            dma(SP, slot(f"wst{si}"), wst[si][:, 0:n], WS.wall[jj, :, 0:n], writes=[b_wst[si]])

        def ws_issue_cast(j):
            name, k0, nk, c0, ncol, jj, t, ctile = ws_meta(j)
            n = nk * ncol
            bi = j % NWBF
            o_ = wbf[bi][:, 0:n]
            if t > ctile:
                dma(SP, slot(f"wlc{bi}"), o_, WS.cache[jj, :, 0:n], reads=[b_wc[jj]], writes=[b_wbf[bi]])
                return
            if name == "convD":
                wcol = vecs[:, V_CW + k0 * CW + c0:V_CW + k0 * CW + c0 + nk]
                op(POOL, lambda h, o_=o_, wcol=wcol, nk=nk: h.tensor_tensor(
                    out=o_.rearrange("p (j q) -> p j q", j=nk), in0=ident.unsqueeze(1).to_broadcast([128, nk, 128]),
                    in1=wcol.unsqueeze(2).to_broadcast([128, nk, 128]), op=ALU.mult),
                   reads=[b_const], writes=[b_wbf[bi]])
            else:
                si = WS.stg.pop(j)
                i_ = wst[si][:, 0:n]
                WS.ncast += 1
                if WS.ncast % 2 == 0:
                    op(ACT, lambda h, o_=o_, i_=i_: h.activation(out=o_, in_=i_, func=AF.Copy),
                       reads=[b_wst[si]], writes=[b_wbf[bi]])
                else:
                    op(DVE, lambda h, o_=o_, i_=i_: h.tensor_copy(out=o_, in_=i_),
                       reads=[b_wst[si]], writes=[b_wbf[bi]])
            if t == ctile:
                b_wc[jj] = Buf()
                dma(ACT, slot(f"wcw{bi}"), WS.cache[jj, :, 0:n], o_, reads=[b_wbf[bi]], writes=[b_wc[jj]])

        def ws_issue_cached(j):
            name, k0, nk, c0, ncol, jj, t, ctile = ws_meta(j)
            n = nk * ncol
            if j == NCAST * WS.nb:
                inherit(b_wbx[NWBF:], b_wst)
            ap_, bf_ = wslot(j)
            dma(SP, slot(f"wl{j % NWX}"), ap_[:, 0:n], WS.cache[jj, :, 0:n], reads=[b_wc[jj]], writes=[bf_])

        def ws_next(name, k0, nk, c0, ncol):
            spec = (name, k0, nk, c0, ncol)
            assert nk * ncol <= WBLK
            i = WS.cons
            WS.cons += 1
            if Ctx.dry:
                WS.specs.append(spec)
                return None, None
            assert WS.specs[i] == spec, (i, WS.specs[i], spec)
            ncr = NCAST * WS.nb
            if i < ncr:
                lim_d = min(ncr, i + NWST + 1)
                lim_c = min(ncr, i + 2)
                while True:
                    if WS.issued < lim_c and WS.dma_issued > WS.issued:
                        ws_issue_cast(WS.issued)
                        WS.issued += 1
                    elif WS.dma_issued < lim_d and (WS.nst - WS.ncast < NWST or not ws_is_fp32(WS.dma_issued)):
                        ws_issue_dma(WS.dma_issued)
                        WS.dma_issued += 1
                    else:
                        break
                assert WS.issued > i
            else:
                lim = min(len(WS.specs), i + NWX - 1)
                while WS.issued < lim:
                    ws_issue_cached(WS.issued)
                    WS.issued += 1
            ap_, bf_ = wslot(i)
            v = ap_[:, 0:nk * ncol].rearrange("p (k c) -> p k c", k=nk)
            return v, bf_

        def mm_group(out_ap, out_buf, terms, first=True, last=True):
            n = len(terms)
            for i, (l_, r_, rd) in enumerate(terms):
                st = first and i == 0
                sp = last and i == n - 1
                op(PE, lambda h, l_=l_, r_=r_, st=st, sp=sp: h.matmul(out_ap, l_, r_, start=st, stop=sp),
                   reads=list(rd), writes=[out_buf], inc=(i == n - 1))

        def transpose(out_ap, out_buf, in_ap, in_bufs, idn, inc=True):
            op(PE, lambda h: h.transpose(out_ap, in_ap, idn), reads=list(in_bufs) + [b_const],
               writes=[out_buf], inc=inc)

        def rms_stats(x_ap, npart, xbufs, si=0):
            o = si * 20
            bs = b_small_l[si]
            sm = small[:npart, o:o + 20]
            op(DVE, lambda h: h.bn_stats(out=sm[:, 0:6], in_=x_ap[:, 0:512]), reads=xbufs, writes=[bs])
            op(DVE, lambda h: h.bn_stats(out=sm[:, 6:12], in_=x_ap[:, 512:1024]), reads=xbufs, writes=[bs])
            op(DVE, lambda h: h.bn_aggr(out=sm[:, 12:14], in_=sm[:, 0:12]), reads=[bs], writes=[bs])
            op(DVE, lambda h: h.scalar_tensor_tensor(out=sm[:, 14:15], in0=sm[:, 12:13], scalar=sm[:, 12:13], in1=sm[:, 13:14],
                                                     op0=ALU.mult, op1=ALU.add), reads=[bs], writes=[bs])
            op(DVE, lambda h: h.tensor_scalar(out=sm[:, 15:16], in0=sm[:, 14:15], scalar1=EPS, scalar2=None, op0=ALU.add),
               reads=[bs], writes=[bs])
            op(POOL, lambda h: h.tensor_tensor(out=sm[:, 16:17], in0=sm[:, 15:16], in1=mhalf[:npart, 0:1], op=ALU.pow),
               reads=[bs, b_const], writes=[bs])
            return sm[:, 16:17], bs

        class Cur:
            xs3 = None
            b_xs = None
            hT3 = None
            b_hT = None
            par = 0
            nxt = None
            normed = False
            pending = []

        def flush_pending():
            for f_ in Cur.pending:
                f_()
            Cur.pending = []

        def norm_to_hT(subs, voff, xs3=None, b_xs=None, hT3=None, b_hT=None, filler=None):
            if xs3 is None:
                xs3, b_xs, hT3, b_hT = Cur.xs3, Cur.b_xs, Cur.hT3, Cur.b_hT
            for s, (t0, npart) in enumerate(subs):
                xa = xs3[:npart, s, :]
                rstd, bs = rms_stats(xa, npart, [b_xs[s]], si=s)
                xnv = xn_l[s]
                op(DVE, lambda h, xa=xa, rstd=rstd, npart=npart, xnv=xnv: h.tensor_scalar(
                    out=xnv[:npart, :], in0=xa, scalar1=rstd, scalar2=None, op0=ALU.mult),
                   reads=[b_xs[s], bs], writes=[b_xn_l[s]])
            if filler is not None:
                filler()
            for s, (t0, npart) in enumerate(subs):
                xnv = xn_l[s]
                for half in range(2):
                    pb = ps_get()
                    for i in range(4):
                        c = half * 4 + i
                        transpose(PSf(pb)[:, i * 128:i * 128 + npart], PSB[pb],
                                  xnv[:npart, c * 128:(c + 1) * 128], [b_xn_l[s]], ident[:npart, :npart], inc=(i == 3))
                    for i in range(4):
                        c = half * 4 + i
                        op(ACT, lambda h, pb=pb, i=i, c=c, t0=t0, npart=npart: h.activation(
                            out=hT3[:, c, t0:t0 + npart], in_=PSf(pb)[:, i * 128:i * 128 + npart],
                            func=AF.Identity, scale=vcol(voff, c), bias=0.0),
                           reads=[PSB[pb], b_const], writes=[b_hT[c]])

        def PS2(pb, TT):
            return psum[pb][:, :].rearrange("p (a t) -> p a t", a=2)[:, :, 0:TT]

        def fm_proj2(name, c0, in3, in_bufs, TT, evac2):
            wv, wb = ws_next(name, 0, 8, c0, 256)
            pb = ps_get()
            if not Ctx.dry:
                for j in range(2):
                    mm_group(PSf(pb)[:, j * 256:j * 256 + TT], PSB[pb],
                             [(wv[:, k, j * 128:(j + 1) * 128], in3[:, k, 0:TT], [wb, in_bufs[k]]) for k in range(8)])
            evac2(pb)

        def fm_proj(name, c0, ncols, in3, in_bufs, nk, TT, evac, kblk=None):
            bc = (WBLK // nk) // 128 * 128
            assert ncols % bc == 0
            oc = 0
            for cb in range(ncols // bc):
                wv, wb = ws_next(name, 0, nk, c0 + cb * bc, bc)
                for j in range(bc // 128):
                    pb = ps_get()
                    if not Ctx.dry:
                        mm_group(PSf(pb)[:, 0:TT], PSB[pb],
                                 [(wv[:, k, j * 128:(j + 1) * 128], in3[:, k, 0:TT], [wb, in_bufs[k]]) for k in range(nk)])
                    evac(oc, pb)
                    oc += 1

        def tm_proj(name, c0, ncols, in3, in_bufs, nk, subs, evac, k_split=4):
            for cg in range(ncols // 512):
                pbs = [ps_get() for _ in subs]
                for pb in pbs:
                    PsA.held.add(pb)
                kb = 0
                nblk = (nk + k_split - 1) // k_split
                for bi in range(nblk):
                    k0 = bi * k_split
                    kk = min(k_split, nk - k0)
                    wv, wb = ws_next(name, k0, kk, c0 + cg * 512, 512)
                    if Ctx.dry:
                        continue
                    for s, (t0, npart) in enumerate(subs):
                        mm_group(PSf(pbs[s])[:npart, :], PSB[pbs[s]],
                                 [(in3[:, k0 + k, t0:t0 + npart], wv[:, k, :], [wb, in_bufs[k0 + k]]) for k in range(kk)],
                                 first=(bi == 0), last=(bi == nblk - 1))
                for s, (t0, npart) in enumerate(subs):
                    evac(cg, s, pbs[s], npart)
                for pb in pbs:
                    PsA.held.discard(pb)

        def load_tile(sample, ti, par):
            xs3 = r3(xs_l[par], 2)
            pin3 = r3(pin_l[par], 2)
            rot3 = r3(rot_l[par], 2)
            b_xs = b_xs_l[par]
            if sample:
                dma(SP, slot(f"xs{par}_0"), xs3[:TS, 0, :], xs_d[:, :], writes=[b_xs[0]])
                dma(SP, slot(f"pin{par}"), pin3[:TS, 0, :], ps_d[:, :], writes=[b_pin_l[par]])
                dma(SP, slot(f"rot{par}"), rot3[:, :, 0:TS], rots_d[:, :, :], writes=[b_rot_l[par]])
            else:
                r0 = ti * T
                for s in range(2):
                    dma(SP, slot(f"xs{par}_{s}"), xs3[:, s, :], xp_d[r0 + s * 128:r0 + (s + 1) * 128, :], writes=[b_xs[s]])
                dma(SP, slot(f"pin{par}"), pin3[:, :, :], pp_d[r0:r0 + T, :].rearrange("(s p) c -> p s c", p=128),
                    writes=[b_pin_l[par]])
                dma(SP, slot(f"rot{par}"), rot3[:, :, 0:T], rotp_d[:, :, r0:r0 + T], writes=[b_rot_l[par]])

        def do_tile(sample, ti):
            TT = TS if sample else T
            subs = [(0, TS)] if sample else [(0, 128), (128, 128)]
            NS = len(subs)
            last_prompt = (not sample) and ti == NT - 1
            first_prompt = (not sample) and ti == 0

            par = Cur.par
            xs3 = r3(xs_l[par], 2)
            pin3 = r3(pin_l[par], 2)
            rot3 = r3(rot_l[par], 2)
            b_xs = b_xs_l[par]
            b_pin = b_pin_l[par]
            b_rot = b_rot_l[par]
            Cur.xs3 = xs3
            Cur.b_xs = b_xs
            hT3 = hT3_l[par]
            b_hT = b_hT_l[par]
            Cur.hT3 = hT3
            Cur.b_hT = b_hT
            cosT = rot3[:, 0, 0:TT]
            sinT = rot3[:, 1, 0:TT]

            if not Cur.normed:
                norm_to_hT(subs, V_LNMIX)

            if sample:
                ucat5 = ucat3.rearrange("p c (b w) -> p c b w", w=34)
            need_tail = sample or last_prompt
            ntail = TS if sample else 32

            def u_dst(c):
                if sample:
                    return ucat5[:, c, :, 30:34]
                return ucat3[:, c, 30:30 + T]

            def p_conv_hist():
                if sample:
                    for g in range(4):
                        dma(SP, slot("scg"), scg[:120, :], stc_d[4 * g:4 * g + 4, :, :].rearrange("b r c -> (b r) c"),
                            writes=[b_scg])
                        for half in range(2):
                            pb = ps_get()
                            for i in range(4):
                                c = half * 4 + i
                                transpose(PSf(pb)[:, i * 128:i * 128 + 120], PSB[pb], scg[:120, c * 128:(c + 1) * 128],
                                          [b_scg], ident[:120, :120], inc=(i == 3))
                            for i in range(4):
                                c = half * 4 + i
                                op(ACT, lambda h, pb=pb, i=i, c=c, g=g: h.activation(
                                    out=ucat5[:, c, 4 * g:4 * g + 4, 0:30],
                                    in_=PSf(pb)[:, i * 128:i * 128 + 120].rearrange("p (b r) -> p b r", r=30),
                                    func=AF.Copy), reads=[PSB[pb]], writes=[b_ucat[c]])
                    dma(SP, slot("ncs0"), ncs_d[:, 0:26, :], stc_d[:, 4:30, :])
                else:
                    for c in range(8):
                        if first_prompt:
                            op(POOL, lambda h, c=c: h.memset(uprev3[:, c, :], 0.0), writes=[b_uprev[c]])
                        op(POOL, lambda h, c=c: h.tensor_copy(out=ucat3[:, c, 0:30], in_=uprev3[:, c, 0:30]),
                           reads=[b_uprev[c]], writes=[b_ucat[c]])

            def g_agav():
                for jb in range(4):
                    def ev_gate(pb, jb=jb):
                        op(ACT, lambda h: h.activation(out=sg3[:, :, 0:TT], in_=PS2(pb, TT), func=AF.Sigmoid),
                           reads=[PSB[pb]], writes=b_sg)
                    fm_proj2("w_in", O_AG + jb * 256, hT3, b_hT, TT, ev_gate)
                    yield

                    def ev_val(pb, jb=jb):
                        c0_ = jb * 2
                        if sample:
                            for i in range(2):
                                c = c0_ + i
                                src = PSf(pb)[:, i * 256:i * 256 + TT].rearrange("p (b t) -> p b t", t=4)
                                s2 = sg3[:, i, 0:TT].rearrange("p (b t) -> p b t", t=4)
                                op(DVE, lambda h, c=c, src=src, s2=s2: h.tensor_tensor(out=u_dst(c), in0=src, in1=s2, op=ALU.mult),
                                   reads=[PSB[pb], b_sg[i]], writes=[b_ucat[c]])
                        else:
                            op(DVE, lambda h: h.tensor_tensor(out=ucat3[:, c0_:c0_ + 2, 30:30 + T], in0=PS2(pb, TT),
                                                             in1=sg3[:, :, 0:TT], op=ALU.mult),
                               reads=[PSB[pb]] + b_sg, writes=[b_ucat[c0_], b_ucat[c0_ + 1]])
                            op(POOL, lambda h: h.tensor_copy(out=uprev3[:, c0_:c0_ + 2, 0:30], in_=ucat3[:, c0_:c0_ + 2, T:T + 30]),
                               reads=[b_ucat[c0_], b_ucat[c0_ + 1]], writes=[b_uprev[c0_], b_uprev[c0_ + 1]])
                        if need_tail:
                            op(DVE, lambda h: h.tensor_tensor(out=utail3[:, c0_:c0_ + 2, 0:ntail], in0=PS2(pb, TT)[:, :, TT - ntail:TT],
                                                             in1=sg3[:, :, TT - ntail:TT], op=ALU.mult),
                               reads=[PSB[pb]] + b_sg, writes=[b_utail])
                    fm_proj2("w_in", O_AV + jb * 256, hT3, b_hT, TT, ev_val)
                    yield

            def p_tail():
                if not need_tail:
                    return
                nrow = TS if sample else 30
                c0t = 0 if sample else 2
                for half in range(2):
                    pb = ps_get()
                    for i in range(4):
                        c = half * 4 + i
                        transpose(PSf(pb)[:nrow, i * 128:(i + 1) * 128], PSB[pb], utail3[:, c, c0t:c0t + nrow],
                                  [b_utail], ident, inc=(i == 3))
                    op(ACT, lambda h, pb=pb, half=half, nrow=nrow: h.activation(
                        out=utok[:nrow, half * 512:(half + 1) * 512], in_=PSf(pb)[:nrow, :], func=AF.Copy),
                       reads=[PSB[pb]], writes=[b_utok])
                if sample:
                    for b in range(NSEQ_S):
                        dma(ACT, slot("ncs1"), ncs_d[b, 26:30, :], utok[4 * b:4 * b + 4, :], reads=[b_utok])
                else:
                    dma(ACT, slot("ncp"), ncp_d[:, :], utok[:30, :], reads=[b_utok])

            st_banks = {}

            def g_conv_chunks():
                pm = ps_get()
                PsA.held.add(pm)
                pe2 = ps_get()
                PsA.held.add(pe2)
                st_banks["pm"] = pm
                st_banks["pe2"] = pe2
                for c in range(8):
                    pb = ps_get()
                    for part, (j0, nj) in enumerate([(0, 16), (16, 15)]):
                        Dv, Db = ws_next("convD", c, nj, j0, 128)
                        if Ctx.dry:
                            continue
                        terms = []
                        for jj in range(nj):
                            j = j0 + jj
                            if sample:
                                rhs = ucat5[:, c, :, j:j + 4]
                            else:
                                rhs = ucat3[:, c, j:j + T]
                            terms.append((Dv[:, jj, :], rhs, [Db, b_ucat[c]]))
                        mm_group(PSf(pb)[:, 0:TT], PSB[pb], terms, first=(part == 0), last=(part == 1))
                        if part == 0 and not sample:
                            yield
                    qi = c % 2
                    op(ACT, lambda h, pb=pb, c=c: h.activation(out=c_sb3[:, c, 0:TT], in_=PSf(pb)[:, 0:TT], func=AF.Identity,
                                                             bias=vcol(V_CB, c), scale=1.0),
                       reads=[PSB[pb], b_const], writes=[b_csb[c]])
                    op(ACT, lambda h, pb=pb, c=c, qi=qi: h.activation(out=csq3[:, qi, 0:TT], in_=PSf(pb)[:, 0:TT], func=AF.Square,
                                                                    bias=vcol(V_CB, c), scale=1.0),
                       reads=[PSB[pb], b_const], writes=[b_csq[qi]])
                    def stats(cc):
                        qq = cc % 2
                        mm_group(PSf(pm)[:, 0:TT], PSB[pm], [(ones_s, c_sb3[:, cc, 0:TT], [b_const, b_csb[cc]])],
                                 first=(cc == 0), last=(cc == 7))
                        mm_group(PSf(pe2)[:, 0:TT], PSB[pe2], [(ones_s, csq3[:, qq, 0:TT], [b_const, b_csq[qq]])],
                                 first=(cc == 0), last=(cc == 7))
                    if c > 0:
                        stats(c - 1)
                    if c == 7:
                        yield
                        stats(7)
                    yield

            def g_ln():
                pm = st_banks["pm"]
                pe2 = st_banks["pe2"]
                op(ACT, lambda h: h.activation(out=mean_sb[:, 0:TT], in_=PSf(pm)[:, 0:TT], func=AF.Copy),
                   reads=[PSB[pm]], writes=[b_ln])
                op(DVE, lambda h: h.tensor_tensor(out=lnB[:, 0:TT], in0=mean_sb[:, 0:TT], in1=mean_sb[:, 0:TT], op=ALU.mult),
                   reads=[b_ln], writes=[b_ln])
                op(DVE, lambda h: h.tensor_tensor(out=lnA[:, 0:TT], in0=PSf(pe2)[:, 0:TT], in1=lnB[:, 0:TT], op=ALU.subtract),
                   reads=[b_ln, PSB[pe2]], writes=[b_ln])
                op(ACT, lambda h: h.activation(out=lnA[:, 0:TT], in_=lnA[:, 0:TT], func=AF.Sqrt, bias=eps5[:, 0:1], scale=1.0),
                   reads=[b_ln, b_const], writes=[b_ln])
                op(DVE, lambda h: h.reciprocal(out=lnA[:, 0:TT], in_=lnA[:, 0:TT]), reads=[b_ln], writes=[b_ln])
                op(DVE, lambda h: h.scalar_tensor_tensor(out=lnB[:, 0:TT], in0=mean_sb[:, 0:TT], scalar=-1.0, in1=lnA[:, 0:TT],
                                                         op0=ALU.mult, op1=ALU.mult), reads=[b_ln], writes=[b_ln])
                PsA.held.discard(pm)
                PsA.held.discard(pe2)
                yield
                for c in range(8):
                    qi = c % 2
                    op(DVE, lambda h, c=c, qi=qi: h.tensor_tensor(out=tmpc3[:, qi, 0:TT], in0=c_sb3[:, c, 0:TT], in1=lnA[:, 0:TT],
                                                                op=ALU.mult), reads=[b_csb[c], b_ln], writes=[b_tmpc[qi]])
                    op(DVE, lambda h, qi=qi: h.tensor_tensor(out=tmpc3[:, qi, 0:TT], in0=tmpc3[:, qi, 0:TT], in1=lnB[:, 0:TT],
                                                           op=ALU.add), reads=[b_tmpc[qi], b_ln], writes=[b_tmpc[qi]])
                    op(ACT, lambda h, c=c, qi=qi: h.activation(out=ca3[:, c, 0:TT], in_=tmpc3[:, qi, 0:TT], func=AF.Silu,
                                                             scale=vcol(V_CLG, c), bias=vcol(V_CLB, c)),
                       reads=[b_tmpc[qi], b_const], writes=[b_ca[c]])
                    yield

            def g_gate_a():
                for jb in range(4):
                    def ev_ga(pb, jb=jb):
                        c0_ = jb * 2
                        op(ACT, lambda h: h.activation(out=sga3[:, c0_:c0_ + 2, 0:TT], in_=PS2(pb, TT), func=AF.Sigmoid),
                           reads=[PSB[pb]], writes=[b_sga[c0_], b_sga[c0_ + 1]])
                    fm_proj2("w_in", O_GA + jb * 256, hT3, b_hT, TT, ev_ga)
                    yield

            def p_conv_out():
                for jb in range(4):
                    def ev_ya(pb, jb=jb):
                        c0_ = jb * 2
                        op(DVE, lambda h: h.tensor_tensor(out=m_a3[:, c0_:c0_ + 2, 0:TT], in0=PS2(pb, TT), in1=sga3[:, c0_:c0_ + 2, 0:TT],
                                                         op=ALU.mult),
                           reads=[PSB[pb], b_sga[c0_], b_sga[c0_ + 1]], writes=[b_ma[c0_], b_ma[c0_ + 1]])
                    fm_proj2("w_co", jb * 256, ca3, b_ca, TT, ev_ya)

            def g_qk():
                for which, (off, dst3, dbufs, scl) in enumerate([(O_Q, qr3, b_qr, 1.0), (O_K, kr3, b_kr, 0.0625)]):
                    for hh in range(4):
                        def ev_rot(pb, hh=hh, dst3=dst3, dbufs=dbufs, scl=scl):
                            x1 = PSf(pb)[:, 0:TT]
                            x2 = PSf(pb)[:, 256:256 + TT]
                            t = [rtmp3[:, i, 0:TT] for i in range(4)]
                            for i, (xx, tab) in enumerate([(x1, cosT), (x2, sinT), (x1, sinT), (x2, cosT)]):
                                op(DVE, lambda h, i=i, xx=xx, tab=tab: h.scalar_tensor_tensor(
                                    out=t[i], in0=xx, scalar=scl, in1=tab, op0=ALU.mult, op1=ALU.mult),
                                   reads=[PSB[pb], b_rot], writes=[b_rtmp[i]])
                            op(POOL, lambda h: h.tensor_tensor(out=dst3[:, 2 * hh, 0:TT], in0=t[0], in1=t[1], op=ALU.subtract),
                               reads=[b_rtmp[0], b_rtmp[1]], writes=[dbufs[2 * hh]])
                            op(POOL, lambda h: h.tensor_tensor(out=dst3[:, 2 * hh + 1, 0:TT], in0=t[2], in1=t[3], op=ALU.add),
                               reads=[b_rtmp[2], b_rtmp[3]], writes=[dbufs[2 * hh + 1]])
                        fm_proj2("w_in", off + hh * 256, hT3, b_hT, TT, ev_rot)
                        if which == 0:
                            gl_off = C_GLS if sample else C_GLP
                            gt = ctab[:, gl_off + hh * TT:gl_off + (hh + 1) * TT]
                            for dc in range(2):
                                c = 2 * hh + dc
                                op(POOL, lambda h, c=c, gt=gt: h.tensor_tensor(out=qdec3[:, c, 0:TT], in0=qr3[:, c, 0:TT], in1=gt,
                                                                             op=ALU.mult),
                                   reads=[b_qr[c], b_const], writes=[b_qdec[c]])
                        yield

            def p_v():
                for hh in range(4):
                    def ev_v(cg, s, pb, npart, hh=hh):
                        op(ACT, lambda h: h.activation(out=v_sb3[:npart, s, hh * 512:(hh + 1) * 512], in_=PSf(pb)[:npart, :],
                                                       func=AF.Copy), reads=[PSB[pb]], writes=[b_v[s * 4 + hh]])
                    tm_proj("w_in", O_V + hh * 512, 512, hT3, b_hT, 8, subs, ev_v)

            def g_retention():
                for s, (t0, npart) in enumerate(subs):
                    yield from retention_prompt(s, t0)

            def g_g():
                for jb in range(8):
                    def ev_g(pb, jb=jb):
                        c0_ = jb * 2
                        op(ACT, lambda h: h.activation(out=sgg3[:, :, 0:TT], in_=PS2(pb, TT), func=AF.Silu),
                           reads=[PSB[pb]], writes=b_sgg)
                        op(DVE, lambda h: h.tensor_tensor(out=zT3[:, c0_:c0_ + 2, 0:TT], in0=zT3[:, c0_:c0_ + 2, 0:TT],
                                                         in1=sgg3[:, :, 0:TT], op=ALU.mult),
                           reads=[b_zT[c0_], b_zT[c0_ + 1]] + b_sgg, writes=[b_zT[c0_], b_zT[c0_ + 1]])
                    fm_proj2("w_in", O_G + jb * 256, hT3, b_hT, TT, ev_g)
                    yield

            def p_yb():
                for jb in range(4):
                    def ev_gb(pb):
                        op(ACT, lambda h: h.activation(out=sgg3[:, :, 0:TT], in_=PS2(pb, TT), func=AF.Sigmoid),
                           reads=[PSB[pb]], writes=b_sgg)
                    fm_proj2("w_in", O_GB + jb * 256, hT3, b_hT, TT, ev_gb)

                    def ev_yb(oc, pb, jb=jb):
                        i = oc % 2
                        c = jb * 2 + i
                        op(DVE, lambda h: h.tensor_tensor(out=sgg3[:, i, 0:TT], in0=PSf(pb)[:, 0:TT], in1=sgg3[:, i, 0:TT],
                                                         op=ALU.mult), reads=[PSB[pb], b_sgg[i]], writes=[b_sgg[i]])
                        op(DVE, lambda h: h.tensor_tensor(out=mixin3[:, c, 0:TT], in0=sgg3[:, i, 0:TT], in1=m_a3[:, c, 0:TT],
                                                         op=ALU.add), reads=[b_sgg[i], b_ma[c]], writes=[b_mixin[c]])
                    fm_proj("w_ro", jb * 256, 256, zT3, b_zT, 16, TT, ev_yb)

            def run(g):
                for _ in g:
                    pass

            def rr(*gens):
                gens = list(gens)
                while gens:
                    for g in list(gens):
                        try:
                            next(g)
                        except StopIteration:
                            gens.remove(g)

            def chain(*gens):
                for g in gens:
                    yield from g

            if sample:
                inherit(G_CONV, G_RETC + G_SAMP + G_FFN)
                p_conv_hist()
                run(g_agav())
                flush_pending()
                p_tail()
                run(g_conv_chunks())
                run(g_gate_a())
                run(g_ln())
                p_conv_out()
                inherit(G_RETC + G_SAMP, G_CONV + G_FFN)
                run(g_qk())
                p_v()
                retention_sample()
                run(g_g())
                p_yb()
            else:
                inherit(G_CONV + G_RETC, G_FFN + G_SAMP)
                p_conv_hist()
                rr(g_agav(), g_qk())
                flush_pending()
                p_tail()
                p_v()
                if first_prompt:
                    for i in range(8):
                        op(POOL, lambda h, i=i: h.memset(S32_3[:, i, :], 0.0), writes=[b_S32[i]])
                        op(POOL, lambda h, i=i: h.memset(Sbf3[:, i, :], 0.0), writes=[b_Sbf[i]])
                rr(g_conv_chunks(), g_retention())
                rr(g_ln(), chain(g_g(), g_gate_a()))
                p_conv_out()
                p_yb()
                if last_prompt:
                    for hh in range(4):
                        dma(ACT, slot("nrp"), nrp_d[hh, :, :].rearrange("(dc p) v -> p dc v", p=128),
                            S32_3[:, 2 * hh:2 * hh + 2, :], reads=[b_S32[2 * hh], b_S32[2 * hh + 1]])

            def ev_res(cg, s, pb, npart):
                xa = xs3[:npart, s, cg * 512:(cg + 1) * 512]
                op(DVE, lambda h: h.tensor_tensor(out=xa, in0=PSf(pb)[:npart, :], in1=xa, op=ALU.add),
                   reads=[PSB[pb], b_xs[s]], writes=[b_xs[s]])
            tm_proj("w_o", 0, D, mixin3, b_mixin, 8, subs, ev_res)

            inherit(G_FFN, G_CONV + G_RETC + G_SAMP)
            if Cur.nxt is not None:
                load_tile(Cur.nxt[0], Cur.nxt[1], 1 - par)
            norm_to_hT(subs, V_LNFFN)
            for jb in range(11):
                def ev_fg(pb):
                    op(ACT, lambda h: h.activation(out=sgate3[:, :, 0:TT], in_=PS2(pb, TT), func=AF.Silu),
                       reads=[PSB[pb]], writes=b_sgate)
                fm_proj2("w_fg", jb * 256, hT3, b_hT, TT, ev_fg)

                def ev_fu(pb, jb=jb):
                    c0_ = jb * 2
                    op(DVE, lambda h: h.tensor_tensor(out=act3[:, c0_:c0_ + 2, 0:TT], in0=PS2(pb, TT), in1=sgate3[:, :, 0:TT],
                                                     op=ALU.mult),
                       reads=[PSB[pb]] + b_sgate, writes=[b_act[c0_], b_act[c0_ + 1]])
                fm_proj2("w_fu", jb * 256, hT3, b_hT, TT, ev_fu)
            tm_proj("w_fd", 0, D, act3, b_act, 22, subs, ev_res)

            def ple_pproj():
                for s, (t0, npart) in enumerate(subs):
                    pb = ps_get()
                    for kc in range(2):
                        transpose(PSf(pb)[:, kc * 128:kc * 128 + npart], PSB[pb], pin3[:npart, s, kc * 128:(kc + 1) * 128],
                                  [b_pin], ident[:npart, :npart], inc=(kc == 1))
                    op(ACT, lambda h, pb=pb, t0=t0, npart=npart: h.activation(
                        out=pT3[:, :, t0:t0 + npart], in_=PSf(pb)[:, 0:256].rearrange("p (k t) -> p k t", k=2)[:, :, 0:npart],
                        func=AF.Copy), reads=[PSB[pb]], writes=[b_pT])

                def ev_pp(cg, s, pb, npart):
                    op(ACT, lambda h: h.activation(out=pproj3[:npart, s, cg * 512:(cg + 1) * 512], in_=PSf(pb)[:npart, :],
                                                   func=AF.Copy), reads=[PSB[pb]], writes=[b_pproj])
                wv, wb = ws_next("w_pp", 0, 2, 0, 1024)
                if not Ctx.dry:
                    for cg in range(2):
                        for s, (t0, npart) in enumerate(subs):
                            pb = ps_get()
                            mm_group(PSf(pb)[:npart, :], PSB[pb],
                                     [(pT3[:, k, t0:t0 + npart], wv[:, k, cg * 512:(cg + 1) * 512], [wb, b_pT]) for k in range(2)])
                            ev_pp(cg, s, pb, npart)

            norm_to_hT(subs, V_LNPLE, filler=ple_pproj)
            if Cur.nxt is not None:
                nsubs = [(0, TS)] if Cur.nxt[0] else [(0, 128), (128, 128)]
                norm_to_hT(nsubs, V_LNMIX, r3(xs_l[1 - par], 2), b_xs_l[1 - par], hT3_l[1 - par], b_hT_l[1 - par])
                Cur.normed = True

            def ev_pg(cg, s, pb, npart):
                gi = (cg * 2 + s) % 2
                ga = gate_sb[gi][:npart, :]
                xa = xs3[:npart, s, cg * 512:(cg + 1) * 512]
                op(ACT, lambda h: h.activation(out=ga, in_=PSf(pb)[:npart, :], func=AF.Sigmoid),
                   reads=[PSB[pb]], writes=[b_gsb[gi]])
                op(DVE, lambda h: h.tensor_tensor(out=ga, in0=ga, in1=pproj3[:npart, s, cg * 512:(cg + 1) * 512], op=ALU.mult),
                   reads=[b_gsb[gi], b_pproj], writes=[b_gsb[gi]])
                op(DVE, lambda h: h.tensor_tensor(out=xa, in0=xa, in1=ga, op=ALU.add),
                   reads=[b_gsb[gi], b_xs[s]], writes=[b_xs[s]])
            tm_proj("w_pg", 0, D, hT3, b_hT, 8, subs, ev_pg)

            for s, (t0, npart) in enumerate(subs):
                xa = xs3[:npart, s, :]
                rstd, bs = rms_stats(xa, npart, [b_xs[s]], si=s)
                op(DVE, lambda h, xa=xa, rstd=rstd, npart=npart, s=s: h.scalar_tensor_tensor(
                    out=xa, in0=xa, scalar=rstd, in1=gfin[:npart, :], op0=ALU.mult, op1=ALU.mult),
                   reads=[b_xs[s], bs, b_const], writes=[b_xs[s]])
                if sample:
                    Cur.pending.append(lambda par=par, xs3=xs3, b_xs=b_xs: dma(
                        ACT, slot(f"yout{par}_0"), ys_d[:, :], xs3[:TS, 0, :], reads=[b_xs[0]]))
                else:
                    r0 = ti * T + s * 128
                    Cur.pending.append(lambda par=par, s=s, r0=r0, xs3=xs3, b_xs=b_xs: dma(
                        ACT, slot(f"yout{par}_{s}"), yp_d[r0:r0 + 128, :], xs3[:, s, :], reads=[b_xs[s]]))

        def kd_build(s_off, npart, kdcol_off):
            pb = ps_get()
            for c in range(8):
                transpose(PSh(pb)[:npart, c * 128:(c + 1) * 128], PSB[pb], kr3[:, c, s_off:s_off + npart], [b_kr[c]],
                          identb, inc=(c == 7))
            for hh in range(4):
                op(ACT, lambda h, hh=hh, pb=pb: h.activation(
                    out=kd[:npart, hh * 256:(hh + 1) * 256], in_=PSh(pb)[:npart, hh * 256:(hh + 1) * 256], func=AF.Identity,
                    scale=ctab[:npart, kdcol_off + hh:kdcol_off + hh + 1], bias=0.0), reads=[PSB[pb], b_const], writes=[b_kd])

        def gn_and_T(pb, hh, npart):
            base = 40 + hh * 12
            bs = b_small_l[2 + hh]
            op(DVE, lambda h: h.bn_stats(out=small[:npart, base:base + 6], in_=PSf(pb)[:npart, :]),
               reads=[PSB[pb]], writes=[bs])
            op(DVE, lambda h: h.bn_aggr(out=small[:npart, base + 6:base + 8], in_=small[:npart, base:base + 6]),
               reads=[bs], writes=[bs])
            op(ACT, lambda h: h.activation(out=small[:npart, base + 8:base + 9], in_=small[:npart, base + 7:base + 8],
                                           func=AF.Sqrt, bias=eps5[:npart, 0:1], scale=1.0),
               reads=[bs, b_const], writes=[bs])
            op(DVE, lambda h: h.reciprocal(out=small[:npart, base + 9:base + 10], in_=small[:npart, base + 8:base + 9]),
               reads=[bs], writes=[bs])
            op(DVE, lambda h: h.tensor_scalar(out=on[:npart, hh * 512:(hh + 1) * 512], in0=PSf(pb)[:npart, :],
                                              scalar1=small[:npart, base + 6:base + 7],
                                              scalar2=small[:npart, base + 9:base + 10],
                                              op0=ALU.subtract, op1=ALU.mult),
               reads=[PSB[pb], bs], writes=[b_on[hh]])

        def on_to_zT(t0, npart):
            for half in range(2):
                pb = ps_get()
                for i in range(8):
                    fc = half * 8 + i
                    transpose(PSh(pb)[:, i * 128:i * 128 + npart], PSB[pb], on[:npart, fc * 128:(fc + 1) * 128],
                              [b_on[fc // 4]], identb[:npart, :npart], inc=(i == 7))
                for i in range(8):
                    fc = half * 8 + i
                    op(ACT, lambda h, pb=pb, i=i, fc=fc: h.activation(
                        out=zT3[:, fc, t0:t0 + npart], in_=PSh(pb)[:, i * 128:i * 128 + npart], func=AF.Identity,
                        scale=vcol(V_GNG, fc), bias=vcol(V_GNB, fc)), reads=[PSB[pb], b_const], writes=[b_zT[fc]])

        def retention_prompt(s, t0):
            kd_build(t0, 128, C_KDP)
            for hh in range(4):
                pb = ps_get()
                mm_group(PSf(pb)[:, 0:128], PSB[pb],
                         [(kr3[:, 2 * hh + dc, t0:t0 + 128], qr3[:, 2 * hh + dc, t0:t0 + 128],
                           [b_kr[2 * hh + dc], b_qr[2 * hh + dc]]) for dc in range(2)])
                op(DVE, lambda h, pb=pb, hh=hh: h.tensor_tensor(
                    out=sT[hh][:, :], in0=PSf(pb)[:, 0:128], in1=ctab[:, C_DECP + hh * 128:C_DECP + (hh + 1) * 128],
                    op=ALU.mult), reads=[PSB[pb], b_const], writes=[b_sT[hh]])
            yield
            for hh in range(4):
                po = ps_get()
                vv = v_sb3[:, s, hh * 512:(hh + 1) * 512]
                terms = [(sT[hh][:, :], vv, [b_sT[hh], b_v[s * 4 + hh]])]
                for dc in range(2):
                    terms.append((qdec3[:, 2 * hh + dc, t0:t0 + 128], Sbf3[:, 2 * hh + dc, :],
                                  [b_qdec[2 * hh + dc], b_Sbf[2 * hh + dc]]))
                mm_group(PSf(po)[:, :], PSB[po], terms)
                gn_and_T(po, hh, 128)
                for dc in range(2):
                    si = 2 * hh + dc
                    pst = ps_get()
                    mm_group(PSf(pst)[:, :], PSB[pst],
                             [(kd[:, hh * 256 + dc * 128:hh * 256 + (dc + 1) * 128], vv, [b_kd, b_v[s * 4 + hh]])])
                    op(DVE, lambda h, si=si, pst=pst, hh=hh: h.scalar_tensor_tensor(
                        out=S32_3[:, si, :], in0=S32_3[:, si, :], scalar=G128[hh], in1=PSf(pst)[:, :],
                        op0=ALU.mult, op1=ALU.add), reads=[PSB[pst], b_S32[si]], writes=[b_S32[si]])
                    op(ACT, lambda h, si=si: h.activation(out=Sbf3[:, si, :], in_=S32_3[:, si, :], func=AF.Copy),
                       reads=[b_S32[si]], writes=[b_Sbf[si]])
                yield
            on_to_zT(t0, 128)
            yield

        def retention_sample():
            kd_build(0, TS, C_KDS)
            po = []
            for hh in range(4):
                pb = ps_get()
                mm_group(PSf(pb)[:TS, 0:TS], PSB[pb],
                         [(kr3[:, 2 * hh + dc, 0:TS], qr3[:, 2 * hh + dc, 0:TS],
                           [b_kr[2 * hh + dc], b_qr[2 * hh + dc]]) for dc in range(2)])
                op(DVE, lambda h, pb=pb, hh=hh: h.tensor_tensor(
                    out=sT[hh][:TS, 0:TS], in0=PSf(pb)[:TS, 0:TS], in1=ctab[:TS, C_DECS + hh * 64:C_DECS + (hh + 1) * 64],
                    op=ALU.mult), reads=[PSB[pb], b_const], writes=[b_sT[hh]])
            for hh in range(4):
                p = ps_get()
                PsA.held.add(p)
                po.append(p)
                mm_group(PSf(p)[:TS, :], PSB[p], [(sT[hh][:TS, 0:TS], v_sb3[:TS, 0, hh * 512:(hh + 1) * 512],
                                                  [b_sT[hh], b_v[hh]])], first=True, last=False)
            units = [(b, hh) for b in range(NSEQ_S) for hh in range(4)]
            NU = len(units)

            def load_unit(u):
                b, hh = units[u]
                i = u % 6
                dma(SP, slot(f"S0_{i}"), S0[i].rearrange("p (dc v) -> p dc v", dc=2),
                    str_d[b, hh, :, :].rearrange("(dc p) v -> p dc v", p=128), writes=[b_S0[i]])
            for u0 in range(3):
                load_unit(u0)
            for u, (b, hh) in enumerate(units):
                if u + 3 < NU:
                    load_unit(u + 3)
                i4 = u % 6
                i2 = u % 2
                if hh == 0:
                    bi = b % 2
                    op(POOL, lambda h, bi=bi, b=b: h.tensor_tensor(
                        out=qxb[bi].rearrange("p (c t) -> p c t", c=8), in0=qdec3[:, :, 0:TS],
                        in1=ctab[:, C_MROW + 60 - 4 * b:C_MROW + 124 - 4 * b].unsqueeze(1).to_broadcast([128, 8, TS]),
                        op=ALU.mult), reads=b_qdec + [b_const], writes=[b_qxb[bi]])
                    op(ACT, lambda h, bi=bi, b=b: h.activation(out=kdx[bi][:TS, :], in_=kd[:TS, :], func=AF.Identity,
                                                             scale=ctab[:TS, C_MCOL + b:C_MCOL + b + 1], bias=0.0),
                       reads=[b_kd, b_const], writes=[b_kdx[bi]])
                bi = b % 2
                qx3 = qxb[bi].rearrange("p (c t) -> p c t", c=8)
                S0v = S0[i4].rearrange("p (dc v) -> p dc v", dc=2)
                S0bv = S0b[i2].rearrange("p (dc v) -> p dc v", dc=2)
                op(ACT, lambda h, i2=i2, i4=i4: h.activation(out=S0b[i2][:, :], in_=S0[i4][:, :], func=AF.Copy),
                   reads=[b_S0[i4]], writes=[b_S0b[i2]])
                lastu = (b == NSEQ_S - 1)
                mm_group(PSf(po[hh])[:TS, :], PSB[po[hh]],
                         [(qx3[:, 2 * hh + dc, :], S0bv[:, dc, :], [b_qxb[bi], b_S0b[i2]]) for dc in range(2)],
                         first=False, last=lastu)
                for dc in range(2):
                    pst = ps_get()
                    mm_group(PSf(pst)[:, :], PSB[pst],
                             [(kdx[bi][:TS, hh * 256 + dc * 128:hh * 256 + (dc + 1) * 128],
                               v_sb3[:TS, 0, hh * 512:(hh + 1) * 512], [b_kdx[bi], b_v[hh]])])
                    op(DVE, lambda h, dc=dc, pst=pst, hh=hh, S0v=S0v: h.scalar_tensor_tensor(
                        out=S0v[:, dc, :], in0=S0v[:, dc, :], scalar=G4[hh], in1=PSf(pst)[:, :],
                        op0=ALU.mult, op1=ALU.add), reads=[PSB[pst], b_S0[i4]], writes=[b_S0[i4]])
                dma(SP, slot(f"S1_{i4}"), nrs_d[b, hh, :, :].rearrange("(dc p) v -> p dc v", p=128), S0v,
                    reads=[b_S0[i4]])
            for hh in range(4):
                gn_and_T(po[hh], hh, TS)
                PsA.held.discard(po[hh])
            on_to_zT(0, TS)

        def emit_all():
            PsA.i = 0
            PsA.held = set()
            WS.cons = 0
            Cur.normed = False
            dma(SP, slot("const"), ctab[:, :], ctab_d[:, :], writes=[b_const])
            dma(SP, slot("const"), vecs[:, :], vecs_d[:, :], writes=[b_const])
            dma(SP, slot("const"), gfin[:, :], gfin_d[:, :], writes=[b_const])
            op(DVE, lambda h: h.tensor_copy(out=identb[:, :], in_=ident), reads=[b_const], writes=[b_const])
            op(DVE, lambda h: h.memset(epsc[:, :], EPS), writes=[b_const])
            op(DVE, lambda h: h.memset(eps5[:, :], 1e-5), writes=[b_const])
            op(DVE, lambda h: h.memset(mhalf[:, :], -0.5), writes=[b_const])
            order = [(False, ti) for ti in range(NT)] + [(True, 0)]
            load_tile(order[0][0], order[0][1], 0)
            Cur.pending = []
            for n_, (smp, ti) in enumerate(order):
                Cur.par = n_ % 2
                Cur.nxt = order[n_ + 1] if n_ + 1 < len(order) else None
                do_tile(smp, ti)
            flush_pending()

        Ctx.dry = True
        emit_all()
        Ctx.dry = False
        WS.nb = len(WS.specs) // (NT + 1)
        assert WS.nb * (NT + 1) == len(WS.specs)
        WS.cache = nc.dram_tensor("wcache", [WS.nb, 128, WBLK], BF16).ap()
        WS.wall = din("wall", [WS.nb, 128, WBLK])
        WS.cidx = {sp: i for i, sp in enumerate(WS.specs[:WS.nb])}
        assert len(WS.cidx) == WS.nb
        for t_ in range(NT + 1):
            assert sorted(WS.specs[t_ * WS.nb:(t_ + 1) * WS.nb]) == sorted(WS.specs[:WS.nb])
        emit_all()
        assert WS.cons == len(WS.specs)

        for sl in slots.values():
            SP.prog.append(("w", sl, sl.count))

        engs = [PE, ACT, DVE, POOL]
        for e in engs:
            e.sem = es.enter_context(nc.semaphore(f"s_{e.name}"))
        for name, sl in slots.items():
            sl.sem = es.enter_context(nc.semaphore(f"d_{name}"))

        def replay(eng, h):
            fuse = eng in (ACT, DVE, POOL, SP)
            pend = []
            for it in eng.prog:
                if it[0] == "w":
                    if fuse:
                        pend.append((it[1].sem, it[2]))
                    else:
                        h.wait_ge(it[1].sem, it[2])
                    continue
                for sm_, v_ in pend[:-1]:
                    h.wait_ge(sm_, v_)
                if it[0] == "o":
                    ins = it[1](h)
                    if pend:
                        ins._wait_ge(pend[-1][0], pend[-1][1])
                    if it[2]:
                        ins.then_inc(eng.sem, 1)
                else:
                    ins = h.dma_start(out=it[1], in_=it[2])
                    if pend:
                        ins._wait_ge(pend[-1][0], pend[-1][1])
                    ins.then_inc(it[3].sem, 16)
                pend = []
            for sm_, v_ in pend:
                h.wait_ge(sm_, v_)

        with nc.Block() as block:
            @block.sync
            def _(h):
                replay(SP, h)

            @block.tensor
            def _(h):
                replay(PE, h)

            @block.scalar
            def _(h):
                replay(ACT, h)

            @block.vector
            def _(h):
                replay(DVE, h)

            @block.gpsimd
            def _(h):
                replay(POOL, h)
        print("instr counts:", {e.name: len(e.prog) for e in engs + [SP]})
    nc._ws_specs = list(WS.specs[:WS.nb])
    return nc


_CACHE = {}


def kernel(**inputs):
    f = lambda a: np.ascontiguousarray(np.asarray(a, dtype=np.float32))
    x_prompt = f(inputs["x_prompt"])
    x_sample = f(inputs["x_sample"])
    state_conv = f(inputs["state_conv"])[0]
    state_ret = f(inputs["state_ret"])[0]
    p_prompt = f(inputs["p_prompt"])[0]
    p_sample = f(inputs["p_sample"])[0]

    if "nc" not in _CACHE:
        _CACHE["nc"] = build_program()
        _CACHE["consts"] = _make_consts()
    nc = _CACHE["nc"]
    ctab, rotp, rots = _CACHE["consts"]

    def cols(v, n):
        return np.ascontiguousarray(np.asarray(v, np.float32).reshape(n, 128).T)

    vecs = np.zeros((128, NV), np.float32)
    vecs[:, V_LNMIX:V_LNMIX + 8] = cols(inputs["ln_mix_g"][0], 8)
    vecs[:, V_LNFFN:V_LNFFN + 8] = cols(inputs["ln_ffn_g"][0], 8)
    vecs[:, V_LNPLE:V_LNPLE + 8] = cols(inputs["ln_ple_g"][0], 8)
    vecs[:, V_CB:V_CB + 8] = cols(inputs["conv_b"][0], 8)
    vecs[:, V_CLG:V_CLG + 8] = cols(inputs["conv_ln_g"][0], 8)
    vecs[:, V_CLB:V_CLB + 8] = cols(inputs["conv_ln_b"][0], 8)
    vecs[:, V_GNG:V_GNG + 16] = cols(inputs["ret_gn_g"][0], 16)
    vecs[:, V_GNB:V_GNB + 16] = cols(inputs["ret_gn_b"][0], 16)
    cw = np.asarray(inputs["conv_w"], np.float32)[0]
    vecs[:, V_CW:V_CW + 8 * CW] = cw.reshape(CW, 8, 128).transpose(2, 1, 0).reshape(128, 8 * CW)
    gfin = np.ascontiguousarray(np.broadcast_to(np.asarray(inputs["ln_final_g"], np.float32)[None, :], (128, D)))

    Wh = {
        "w_in": f(inputs["w_in"])[0], "w_co": f(inputs["w_conv_out"])[0], "w_ro": f(inputs["w_ret_out"])[0],
        "w_o": f(inputs["w_o"])[0], "w_fg": f(inputs["w_ffn_gate"])[0], "w_fu": f(inputs["w_ffn_up"])[0],
        "w_fd": f(inputs["w_ffn_down"])[0], "w_pg": f(inputs["w_ple_gate"])[0], "w_pp": f(inputs["w_ple_proj"])[0],
    }
    specs = nc._ws_specs
    wall = np.zeros((len(specs), 128, WBLK), np.float32)
    for jj, (name, k0, nk, c0, ncol) in enumerate(specs):
        if name == "convD":
            continue
        blk = Wh[name][k0 * 128:(k0 + nk) * 128, c0:c0 + ncol]
        wall[jj, :, :nk * ncol] = blk.reshape(nk, 128, ncol).transpose(1, 0, 2).reshape(128, nk * ncol)
    shared = {"wall": wall, "ctab": ctab, "rotp": rotp, "rots": rots, "vecs": vecs, "gfin": gfin}
    in_maps = []
    for i in range(NCORE):
        m = dict(shared)
        m["xp"] = x_prompt[i]
        m["xs"] = x_sample[i * NSEQ_S:(i + 1) * NSEQ_S].reshape(TS, D)
        m["stc"] = state_conv[i * NSEQ_S:(i + 1) * NSEQ_S]
        m["str"] = state_ret[i * NSEQ_S:(i + 1) * NSEQ_S]
        m["pp"] = p_prompt[i]
        m["ps"] = p_sample[i * NSEQ_S:(i + 1) * NSEQ_S].reshape(TS, 256)
        in_maps.append(m)
    res = run_bass_kernel_spmd(nc, in_maps, core_ids=list(range(NCORE)))
    R = res.results
    y_prompt = np.stack([R[i]["yp"] for i in range(NCORE)], 0).astype(np.float32)
    y_sample = np.concatenate([R[i]["ys"].reshape(NSEQ_S, DEC, D) for i in range(NCORE)], 0).astype(np.float32)
    ncp = np.stack([R[i]["ncp"] for i in range(NCORE)], 0)[None].astype(np.float32)
    nrp = np.stack([R[i]["nrp"] for i in range(NCORE)], 0)[None].astype(np.float32)
    ncs = np.concatenate([R[i]["ncs"] for i in range(NCORE)], 0)[None].astype(np.float32)
    nrs = np.concatenate([R[i]["nrs"] for i in range(NCORE)], 0)[None].astype(np.float32)
    return (y_prompt, y_sample, ncp, nrp, ncs, nrs)
```

```python
import numpy as np
import concourse.bass as bass
import concourse.mybir as mybir
from concourse.bass_utils import run_bass_kernel_spmd

F32 = mybir.dt.float32
BF16 = mybir.dt.bfloat16
AF = mybir.ActivationFunctionType
ALU = mybir.AluOpType

D = 1024
SEQ = 2048
NCORE = 8
NSEQ_S = 16
DEC = 4
TS = NSEQ_S * DEC
CW = 31
PAST = 16384
DFF = 2816
T = 256
NT = SEQ // T
WBLK = 2048
NWST = 3
NWBF = 3
AHEAD = 2
NCAST = 1
EPS = 1e-6

O_AV, O_AG, O_Q, O_K, O_V, O_G, O_GA, O_GB = 0, 1024, 2048, 3072, 4096, 6144, 8192, 9216

C_ID = 0
C_DECP = C_ID + 128
C_DECS = C_DECP + 512
C_GLP = C_DECS + 256
C_GLS = C_GLP + 1024
C_KDP = C_GLS + 256
C_KDS = C_KDP + 4
C_MCOL = C_KDS + 4
C_MROW = C_MCOL + 16
C_ONES = C_MROW + 128
NCT = C_ONES + 128

V_LNMIX, V_LNFFN, V_LNPLE, V_CB, V_CLG, V_CLB = 0, 8, 16, 24, 32, 40
V_GNG, V_GNB = 48, 64
V_CW = 80
NV = V_CW + 8 * CW


def _log_g():
    h = np.arange(4, dtype=np.float64)
    return np.log1p(-np.exp2(-5.0 - h))


def _make_consts():
    lg = _log_g()
    ct = np.zeros((128, NCT), np.float64)
    ct[:, C_ID:C_ID + 128] = np.eye(128)
    m = np.arange(128)[:, None]
    l = np.arange(128)[None, :]
    for h in range(4):
        dec = np.where(l >= m, np.exp(lg[h] * np.maximum(l - m, 0)), 0.0)
        ct[:, C_DECP + h * 128:C_DECP + (h + 1) * 128] = dec
        ms = np.arange(64)[:, None]
        ls = np.arange(64)[None, :]
        same = (ms // 4) == (ls // 4)
        decs = np.where(same & (ls >= ms), np.exp(lg[h] * np.maximum(ls - ms, 0)), 0.0)
        ct[:64, C_DECS + h * 64:C_DECS + (h + 1) * 64] = decs
        t = np.arange(256)
        ct[:, C_GLP + h * 256:C_GLP + (h + 1) * 256] = np.exp(lg[h] * ((t % 128) + 1.0))[None, :]
        ts = np.arange(64)
        ct[:, C_GLS + h * 64:C_GLS + (h + 1) * 64] = np.exp(lg[h] * ((ts % 4) + 1.0))[None, :]
        ct[:, C_KDP + h] = np.exp(lg[h] * (127.0 - np.arange(128)))
        ct[:64, C_KDS + h] = np.exp(lg[h] * (3.0 - (np.arange(64) % 4)))
    for b in range(16):
        ct[:64, C_MCOL + b] = ((np.arange(64) // 4) == b)
    ct[:, C_MROW + 60:C_MROW + 64] = 1.0
    ct[:, C_ONES:C_ONES + 128] = 1.0 / 1024.0
    half = 128
    inv = (1.0 / (np.float32(10000.0) ** (np.arange(half, dtype=np.float32) / np.float32(half)))).astype(np.float32)
    posp = np.arange(SEQ, dtype=np.float32)
    angp = (posp[None, :] * inv[:, None]).astype(np.float32)
    rotp = np.stack([np.cos(angp), np.sin(angp)], axis=1).astype(np.float32)
    poss = (np.float32(PAST) + (np.arange(TS) % 4).astype(np.float32)).astype(np.float32)
    angs = (poss[None, :] * inv[:, None]).astype(np.float32)
    rots = np.stack([np.cos(angs), np.sin(angs)], axis=1).astype(np.float32)
    return ct.astype(np.float32), rotp, rots


class Buf:
    __slots__ = ("w", "r")

    def __init__(self):
        self.w = None
        self.r = {}


class Eng:
    def __init__(self, name, sync_self):
        self.name = name
        self.sem = None
        self.count = 0
        self.waited = {}
        self.sync_self = sync_self
        self.prog = []


class Slot:
    def __init__(self):
        self.sem = None
        self.count = 0


class Ctx:
    dry = False


def _deps(reads, writes):
    d = {}

    def add(s, v):
        if d.get(s, 0) < v:
            d[s] = v
    for b in reads:
        if b.w is not None:
            add(*b.w)
    for b in writes:
        if b.w is not None:
            add(*b.w)
        for s, v in b.r.items():
            add(s, v)
    return d


def _wait(eng, d):
    for s, v in d.items():
        if s is eng and not eng.sync_self:
            continue
        if eng.waited.get(s, 0) < v:
            eng.prog.append(("w", s, v))
            eng.waited[s] = v


def _record(ev, reads, writes):
    s, v = ev
    for b in reads:
        if b.r.get(s, 0) < v:
            b.r[s] = v
    for b in writes:
        b.w = ev
        b.r = {}


def op(eng, fn, reads=(), writes=(), inc=True):
    if Ctx.dry:
        return
    _wait(eng, _deps(reads, writes))
    eng.prog.append(("o", fn, inc))
    if inc:
        eng.count += 1
        ev = (eng, eng.count)
    else:
        ev = (eng, eng.count + 1)
    _record(ev, reads, writes)


def dma(q, slot, out, in_, reads=(), writes=()):
    if Ctx.dry:
        return
    _wait(q, _deps(reads, writes))
    q.prog.append(("d", out, in_, slot))
    slot.count += 16
    _record((slot, slot.count), reads, writes)


def inherit(new_bufs, old_bufs):
    if Ctx.dry:
        return
    d = {}
    for b in old_bufs:
        if b.w is not None and d.get(b.w[0], 0) < b.w[1]:
            d[b.w[0]] = b.w[1]
        for s, v in b.r.items():
            if d.get(s, 0) < v:
                d[s] = v
    for b in new_bufs:
        for s, v in d.items():
            if b.r.get(s, 0) < v:
                b.r[s] = v


def build_program():
    nc = bass.Bass("TRN2", target_bir_lowering=False)

    def din(name, shape):
        return nc.dram_tensor(name, list(shape), F32, kind="ExternalInput").ap()

    def dout(name, shape):
        return nc.dram_tensor(name, list(shape), F32, kind="ExternalOutput").ap()

    xp_d = din("xp", [SEQ, D])
    xs_d = din("xs", [TS, D])
    stc_d = din("stc", [NSEQ_S, 30, D])
    str_d = din("str", [NSEQ_S, 4, 256, 512])
    pp_d = din("pp", [SEQ, 256])
    ps_d = din("ps", [TS, 256])
    ctab_d = din("ctab", [128, NCT])
    rotp_d = din("rotp", [128, 2, SEQ])
    rots_d = din("rots", [128, 2, TS])
    vecs_d = din("vecs", [128, NV])
    gfin_d = din("gfin", [128, D])

    yp_d = dout("yp", [SEQ, D])
    ys_d = dout("ys", [TS, D])
    ncp_d = dout("ncp", [30, D])
    nrp_d = dout("nrp", [4, 256, 512])
    ncs_d = dout("ncs", [NSEQ_S, 30, D])
    nrs_d = dout("nrs", [NSEQ_S, 4, 256, 512])

    lg = _log_g()
    G128 = [float(np.float32(np.exp(lg[h] * 128.0))) for h in range(4)]
    G4 = [float(np.float32(np.exp(lg[h] * 4.0))) for h in range(4)]

    import contextlib
    es = contextlib.ExitStack()
    with es:
        ARENA_W = 52600
        arena = es.enter_context(nc.sbuf_tensor("arena", [128, ARENA_W], F32))
        psum = [es.enter_context(nc.psum_tensor(f"ps{i}", [128, 512], F32)) for i in range(8)]
        PSB = [Buf() for _ in range(8)]

        class Ar:
            off = 0
            peak = 0

        def alloc(nelem, dtype=F32):
            words = nelem if dtype == F32 else (nelem + 1) // 2
            words = (words + 1) // 2 * 2
            o = Ar.off
            Ar.off += words
            Ar.peak = max(Ar.peak, Ar.off)
            assert Ar.off <= ARENA_W, f"arena overflow {Ar.off}"
            v = arena[:, o:o + words]
            if dtype != F32:
                v = v.bitcast(dtype)[:, 0:nelem]
            else:
                v = v[:, 0:nelem]
            return v

        def r3(ap, a):
            return ap.rearrange("p (a b) -> p a b", a=a)

        ctab = alloc(NCT)
        vecs = alloc(NV)
        gfin = alloc(D)
        identb = alloc(128, BF16)
        rot_l = [alloc(2 * T) for _ in range(2)]
        xs_l = [alloc(2 * D) for _ in range(2)]
        pin_l = [alloc(2 * 256) for _ in range(2)]
        xn_l = [alloc(D), alloc(D)]
        xn = xn_l[0]
        hT_l = [alloc(8 * T, BF16) for _ in range(2)]
        m_a = alloc(8 * T, BF16)
        mixin = alloc(8 * T, BF16)
        small = alloc(96)
        wst = [alloc(WBLK) for _ in range(NWST)]
        wbf = [alloc(WBLK, BF16) for _ in range(NWBF)]
        S32 = alloc(8 * 512)
        Sbf = alloc(8 * 512, BF16)
        utail = alloc(8 * 64)
        utok = xn
        epsc = alloc(2)
        eps5 = alloc(2)
        mhalf = alloc(2)
        uprev = alloc(8 * 32, BF16)
        mark = Ar.off
        qr = alloc(8 * T, BF16)
        kr = alloc(8 * T, BF16)
        qdec = alloc(8 * T, BF16)
        rtmp = alloc(4 * T)
        v_sb = alloc(2 * 2048, BF16)
        kd = alloc(1024, BF16)
        sT = [alloc(128, BF16) for _ in range(4)]
        on = alloc(2048, BF16)
        sgg = alloc(2 * T)
        zT = alloc(16 * T, BF16)
        retc_end = Ar.off
        sg = alloc(2 * T)
        ucat = alloc(8 * 544, BF16)
        c_sb = alloc(8 * T)
        csq = alloc(2 * T)
        lnA = alloc(T)
        lnB = alloc(T)
        mean_sb = alloc(T)
        tmpc = alloc(2 * T)
        ca = alloc(8 * T, BF16)
        sga = alloc(8 * T, BF16)
        scg = alloc(D)
        conv_end = Ar.off
        Ar.off = retc_end
        S0 = [alloc(2 * 512) for _ in range(6)]
        S0b = [alloc(2 * 512, BF16) for _ in range(2)]
        kdx = [alloc(1024, BF16) for _ in range(2)]
        qxb = [alloc(8 * 64, BF16) for _ in range(2)]
        ret_end = Ar.off
        Ar.off = mark
        act = alloc(22 * T, BF16)
        sgate = alloc(2 * T)
        pT = alloc(2 * T, BF16)
        pproj = alloc(2 * D)
        gate_sb = [alloc(512) for _ in range(2)]
        ffn_end = Ar.off
        assert ffn_end <= retc_end, (ffn_end, retc_end)
        print("arena peak words", Ar.peak, "of", ARENA_W)

        ident = ctab[:, C_ID:C_ID + 128]
        ones_s = ctab[:, C_ONES:C_ONES + 128]
        hT3_l = [r3(hT_l[0], 8), r3(hT_l[1], 8)]
        m_a3 = r3(m_a, 8)
        mixin3 = r3(mixin, 8)
        S32_3 = r3(S32, 8)
        Sbf3 = r3(Sbf, 8)
        utail3 = r3(utail, 8)
        uprev3 = r3(uprev, 8)
        sg3 = r3(sg, 2)
        ucat3 = r3(ucat, 8)
        c_sb3 = r3(c_sb, 8)
        csq3 = r3(csq, 2)
        tmpc3 = r3(tmpc, 2)
        ca3 = r3(ca, 8)
        sga3 = r3(sga, 8)
        qr3 = r3(qr, 8)
        kr3 = r3(kr, 8)
        qdec3 = r3(qdec, 8)
        rtmp3 = r3(rtmp, 4)
        v_sb3 = r3(v_sb, 2)
        sgg3 = r3(sgg, 2)
        zT3 = r3(zT, 16)
        act3 = r3(act, 22)
        sgate3 = r3(sgate, 2)
        pT3 = r3(pT, 2)
        pproj3 = r3(pproj, 2)

        def vcol(off, c):
            return vecs[:, off + c:off + c + 1]

        B = {}

        def nb(name, n=None):
            if n is None:
                B[name] = Buf()
            else:
                B[name] = [Buf() for _ in range(n)]
            return B[name]

        b_const = nb("const")
        b_rot_l = [Buf(), Buf()]
        b_xs_l = [[Buf(), Buf()], [Buf(), Buf()]]
        b_pin_l = [Buf(), Buf()]
        b_xn_l = [Buf(), Buf()]
        b_xn = b_xn_l[0]
        b_hT_l = [[Buf() for _ in range(8)], [Buf() for _ in range(8)]]
        b_ma = nb("m_a", 8)
        b_mixin = nb("mixin", 8)
        b_small_l = [Buf() for _ in range(6)]
        b_wst = nb("wst", NWST)
        b_wbf = nb("wbf", NWBF)
        b_S32 = nb("S32", 8)
        b_Sbf = nb("Sbf", 8)
        b_utail = nb("utail")
        b_utok = b_xn
        b_uprev = nb("uprev", 8)
        b_sg = nb("sg", 2)
        b_ucat = nb("ucat", 8)
        b_csb = nb("c_sb", 8)
        b_csq = nb("csq", 2)
        b_ln = nb("ln")
        b_tmpc = nb("tmpc", 2)
        b_ca = nb("ca", 8)
        b_scg = nb("scg")
        b_sga = nb("sga", 8)
        G_CONV = b_sg + b_ucat + b_csb + b_csq + [b_ln] + b_tmpc + b_ca + [b_scg] + b_sga
        b_qr = nb("qr", 8)
        b_kr = nb("kr", 8)
        b_qdec = nb("qdec", 8)
        b_rtmp = nb("rtmp", 4)
        b_v = nb("v", 8)
        b_kd = nb("kd")
        b_sT = nb("sT", 4)
        b_on = nb("on", 4)
        b_sgg = nb("sgg", 2)
        b_zT = nb("zT", 16)
        b_S0 = nb("S0", 6)
        b_S0b = nb("S0b", 2)
        b_kdx = nb("kdx", 2)
        b_qxb = nb("qxb", 2)
        G_RETC = b_qr + b_kr + b_qdec + b_rtmp + b_v + [b_kd] + b_sT + b_on + b_sgg + b_zT
        G_SAMP = b_S0 + b_S0b + b_kdx + b_qxb
        b_act = nb("act", 22)
        b_sgate = nb("sgate", 2)
        b_pT = nb("pT")
        b_pproj = nb("pproj")
        b_gsb = nb("gate_sb", 2)
        G_FFN = b_act + b_sgate + [b_pT, b_pproj] + b_gsb

        PE = Eng("pe", False)
        ACT = Eng("act", True)
        DVE = Eng("dve", True)
        POOL = Eng("pool", True)
        SP = Eng("sp", False)
        slots = {}

        def slot(name):
            if name not in slots:
                slots[name] = Slot()
            return slots[name]

        class PsA:
            i = 0
            held = set()

        def ps_get():
            while True:
                b = PsA.i % 8
                PsA.i += 1
                if b not in PsA.held:
                    return b

        def PSf(b):
            return psum[b][:, :]

        def PSh(b):
            return psum[b][:, :].bitcast(BF16)

        class WS:
            specs = []
            issued = 0
            cons = 0
            nb = 0
            cache = None
            cidx = {}
            wall = None
            nst = 0
            ncast = 0
            dma_issued = 0
            stg = {}

        NWX = NWBF + 2 * NWST
        wbx = list(wbf) + [wst[i][:, h * (WBLK // 2):(h + 1) * (WBLK // 2)].bitcast(BF16) for i in range(NWST) for h in range(2)]
        b_wbx = list(b_wbf) + [Buf() for _ in range(2 * NWST)]
        b_wc = {}

        def wslot(j):
            if j < NCAST * WS.nb:
                return wbf[j % NWBF], b_wbf[j % NWBF]
            return wbx[j % NWX], b_wbx[j % NWX]

        def ws_meta(j):
            name, k0, nk, c0, ncol = WS.specs[j]
            jj = WS.cidx[WS.specs[j]]
            t = j // WS.nb
            ctile = 0 if name == "convD" else jj % NCAST
            return name, k0, nk, c0, ncol, jj, t, ctile

        def ws_is_fp32(j):
            name, k0, nk, c0, ncol, jj, t, ctile = ws_meta(j)
            return not (t > ctile or name == "convD")

        def ws_issue_dma(j):
            name, k0, nk, c0, ncol, jj, t, ctile = ws_meta(j)
            if t > ctile or name == "convD":
                return
            n = nk * ncol
            si = WS.nst % NWST
            WS.nst += 1
            WS.stg[j] = si
            dma(SP, slot(f"wst{si}"), wst[si][:, 0:n], WS.wall[jj, :, 0:n], writes=[b_wst[si]])

        def ws_issue_cast(j):
            name, k0, nk, c0, ncol, jj, t, ctile = ws_meta(j)
            n = nk * ncol
            bi = j % NWBF
            o_ = wbf[bi][:, 0:n]
            if t > ctile:
                dma(SP, slot(f"wlc{bi}"), o_, WS.cache[jj, :, 0:n], reads=[b_wc[jj]], writes=[b_wbf[bi]])
                return
            if name == "convD":
                wcol = vecs[:, V_CW + k0 * CW + c0:V_CW + k0 * CW + c0 + nk]
                op(POOL, lambda h, o_=o_, wcol=wcol, nk=nk: h.tensor_tensor(
                    out=o_.rearrange("p (j q) -> p j q", j=nk), in0=ident.unsqueeze(1).to_broadcast([128, nk, 128]),
                    in1=wcol.unsqueeze(2).to_broadcast([128, nk, 128]), op=ALU.mult),
                   reads=[b_const], writes=[b_wbf[bi]])
            else:
                si = WS.stg.pop(j)
                i_ = wst[si][:, 0:n]
                WS.ncast += 1
                if WS.ncast % 2 == 0:
                    op(ACT, lambda h, o_=o_, i_=i_: h.activation(out=o_, in_=i_, func=AF.Copy),
                       reads=[b_wst[si]], writes=[b_wbf[bi]])
                else:
                    op(DVE, lambda h, o_=o_, i_=i_: h.tensor_copy(out=o_, in_=i_),
                       reads=[b_wst[si]], writes=[b_wbf[bi]])
            if t == ctile:
                b_wc[jj] = Buf()
                dma(ACT, slot(f"wcw{bi}"), WS.cache[jj, :, 0:n], o_, reads=[b_wbf[bi]], writes=[b_wc[jj]])

        def ws_issue_cached(j):
            name, k0, nk, c0, ncol, jj, t, ctile = ws_meta(j)
            n = nk * ncol
            if j == NCAST * WS.nb:
                inherit(b_wbx[NWBF:], b_wst)
            ap_, bf_ = wslot(j)
            dma(SP, slot(f"wl{j % NWX}"), ap_[:, 0:n], WS.cache[jj, :, 0:n], reads=[b_wc[jj]], writes=[bf_])

        def ws_next(name, k0, nk, c0, ncol):
            spec = (name, k0, nk, c0, ncol)
            assert nk * ncol <= WBLK
            i = WS.cons
            WS.cons += 1
            if Ctx.dry:
                WS.specs.append(spec)
                return None, None
            assert WS.specs[i] == spec, (i, WS.specs[i], spec)
            ncr = NCAST * WS.nb
            if i < ncr:
                lim_d = min(ncr, i + NWST + 1)
                lim_c = min(ncr, i + 2)
                while True:
                    if WS.issued < lim_c and WS.dma_issued > WS.issued:
                        ws_issue_cast(WS.issued)
                        WS.issued += 1
                    elif WS.dma_issued < lim_d and (WS.nst - WS.ncast < NWST or not ws_is_fp32(WS.dma_issued)):
                        ws_issue_dma(WS.dma_issued)
                        WS.dma_issued += 1
                    else:
                        break
                assert WS.issued > i
            else:
                lim = min(len(WS.specs), i + NWX - 1)
                while WS.issued < lim:
                    ws_issue_cached(WS.issued)
                    WS.issued += 1
            ap_, bf_ = wslot(i)
            v = ap_[:, 0:nk * ncol].rearrange("p (k c) -> p k c", k=nk)
            return v, bf_

        def mm_group(out_ap, out_buf, terms, first=True, last=True):
            n = len(terms)
            for i, (l_, r_, rd) in enumerate(terms):
                st = first and i == 0
                sp = last and i == n - 1
                op(PE, lambda h, l_=l_, r_=r_, st=st, sp=sp: h.matmul(out_ap, l_, r_, start=st, stop=sp),
                   reads=list(rd), writes=[out_buf], inc=(i == n - 1))

        def transpose(out_ap, out_buf, in_ap, in_bufs, idn, inc=True):
            op(PE, lambda h: h.transpose(out_ap, in_ap, idn), reads=list(in_bufs) + [b_const],
               writes=[out_buf], inc=inc)

        def rms_stats(x_ap, npart, xbufs, si=0):
            o = si * 20
            bs = b_small_l[si]
            sm = small[:npart, o:o + 20]
            op(DVE, lambda h: h.bn_stats(out=sm[:, 0:6], in_=x_ap[:, 0:512]), reads=xbufs, writes=[bs])
            op(DVE, lambda h: h.bn_stats(out=sm[:, 6:12], in_=x_ap[:, 512:1024]), reads=xbufs, writes=[bs])
            op(DVE, lambda h: h.bn_aggr(out=sm[:, 12:14], in_=sm[:, 0:12]), reads=[bs], writes=[bs])
            op(DVE, lambda h: h.scalar_tensor_tensor(out=sm[:, 14:15], in0=sm[:, 12:13], scalar=sm[:, 12:13], in1=sm[:, 13:14],
                                                     op0=ALU.mult, op1=ALU.add), reads=[bs], writes=[bs])
            op(DVE, lambda h: h.tensor_scalar(out=sm[:, 15:16], in0=sm[:, 14:15], scalar1=EPS, scalar2=None, op0=ALU.add),
               reads=[bs], writes=[bs])
            op(POOL, lambda h: h.tensor_tensor(out=sm[:, 16:17], in0=sm[:, 15:16], in1=mhalf[:npart, 0:1], op=ALU.pow),
               reads=[bs, b_const], writes=[bs])
            return sm[:, 16:17], bs

        class Cur:
            xs3 = None
            b_xs = None
            hT3 = None
            b_hT = None
            par = 0
            nxt = None
            normed = False
            pending = []

        def flush_pending():
            for f_ in Cur.pending:
                f_()
            Cur.pending = []

        def norm_to_hT(subs, voff, xs3=None, b_xs=None, hT3=None, b_hT=None, filler=None):
            if xs3 is None:
                xs3, b_xs, hT3, b_hT = Cur.xs3, Cur.b_xs, Cur.hT3, Cur.b_hT
            for s, (t0, npart) in enumerate(subs):
                xa = xs3[:npart, s, :]
                rstd, bs = rms_stats(xa, npart, [b_xs[s]], si=s)
                xnv = xn_l[s]
                op(DVE, lambda h, xa=xa, rstd=rstd, npart=npart, xnv=xnv: h.tensor_scalar(
                    out=xnv[:npart, :], in0=xa, scalar1=rstd, scalar2=None, op0=ALU.mult),
                   reads=[b_xs[s], bs], writes=[b_xn_l[s]])
            if filler is not None:
                filler()
            for s, (t0, npart) in enumerate(subs):
                xnv = xn_l[s]
                for half in range(2):
                    pb = ps_get()
                    for i in range(4):
                        c = half * 4 + i
                        transpose(PSf(pb)[:, i * 128:i * 128 + npart], PSB[pb],
                                  xnv[:npart, c * 128:(c + 1) * 128], [b_xn_l[s]], ident[:npart, :npart], inc=(i == 3))
                    for i in range(4):
                        c = half * 4 + i
                        op(ACT, lambda h, pb=pb, i=i, c=c, t0=t0, npart=npart: h.activation(
                            out=hT3[:, c, t0:t0 + npart], in_=PSf(pb)[:, i * 128:i * 128 + npart],
                            func=AF.Identity, scale=vcol(voff, c), bias=0.0),
                           reads=[PSB[pb], b_const], writes=[b_hT[c]])

        def PS2(pb, TT):
            return psum[pb][:, :].rearrange("p (a t) -> p a t", a=2)[:, :, 0:TT]

        def fm_proj2(name, c0, in3, in_bufs, TT, evac2):
            wv, wb = ws_next(name, 0, 8, c0, 256)
            pb = ps_get()
            if not Ctx.dry:
                for j in range(2):
                    mm_group(PSf(pb)[:, j * 256:j * 256 + TT], PSB[pb],
                             [(wv[:, k, j * 128:(j + 1) * 128], in3[:, k, 0:TT], [wb, in_bufs[k]]) for k in range(8)])
            evac2(pb)

        def fm_proj(name, c0, ncols, in3, in_bufs, nk, TT, evac, kblk=None):
            bc = (WBLK // nk) // 128 * 128
            assert ncols % bc == 0
            oc = 0
            for cb in range(ncols // bc):
                wv, wb = ws_next(name, 0, nk, c0 + cb * bc, bc)
                for j in range(bc // 128):
                    pb = ps_get()
                    if not Ctx.dry:
                        mm_group(PSf(pb)[:, 0:TT], PSB[pb],
                                 [(wv[:, k, j * 128:(j + 1) * 128], in3[:, k, 0:TT], [wb, in_bufs[k]]) for k in range(nk)])
                    evac(oc, pb)
                    oc += 1

        def tm_proj(name, c0, ncols, in3, in_bufs, nk, subs, evac, k_split=4):
            for cg in range(ncols // 512):
                pbs = [ps_get() for _ in subs]
                for pb in pbs:
                    PsA.held.add(pb)
                kb = 0
                nblk = (nk + k_split - 1) // k_split
                for bi in range(nblk):
                    k0 = bi * k_split
                    kk = min(k_split, nk - k0)
                    wv, wb = ws_next(name, k0, kk, c0 + cg * 512, 512)
                    if Ctx.dry:
                        continue
                    for s, (t0, npart) in enumerate(subs):
                        mm_group(PSf(pbs[s])[:npart, :], PSB[pbs[s]],
                                 [(in3[:, k0 + k, t0:t0 + npart], wv[:, k, :], [wb, in_bufs[k0 + k]]) for k in range(kk)],
                                 first=(bi == 0), last=(bi == nblk - 1))
                for s, (t0, npart) in enumerate(subs):
                    evac(cg, s, pbs[s], npart)
                for pb in pbs:
                    PsA.held.discard(pb)

        def load_tile(sample, ti, par):
            xs3 = r3(xs_l[par], 2)
            pin3 = r3(pin_l[par], 2)
            rot3 = r3(rot_l[par], 2)
            b_xs = b_xs_l[par]
            if sample:
                dma(SP, slot(f"xs{par}_0"), xs3[:TS, 0, :], xs_d[:, :], writes=[b_xs[0]])
                dma(SP, slot(f"pin{par}"), pin3[:TS, 0, :], ps_d[:, :], writes=[b_pin_l[par]])
                dma(SP, slot(f"rot{par}"), rot3[:, :, 0:TS], rots_d[:, :, :], writes=[b_rot_l[par]])
            else:
                r0 = ti * T
                for s in range(2):
                    dma(SP, slot(f"xs{par}_{s}"), xs3[:, s, :], xp_d[r0 + s * 128:r0 + (s + 1) * 128, :], writes=[b_xs[s]])
                dma(SP, slot(f"pin{par}"), pin3[:, :, :], pp_d[r0:r0 + T, :].rearrange("(s p) c -> p s c", p=128),
                    writes=[b_pin_l[par]])
                dma(SP, slot(f"rot{par}"), rot3[:, :, 0:T], rotp_d[:, :, r0:r0 + T], writes=[b_rot_l[par]])

        def do_tile(sample, ti):
            TT = TS if sample else T
            subs = [(0, TS)] if sample else [(0, 128), (128, 128)]
            NS = len(subs)
            last_prompt = (not sample) and ti == NT - 1
            first_prompt = (not sample) and ti == 0

            par = Cur.par
            xs3 = r3(xs_l[par], 2)
            pin3 = r3(pin_l[par], 2)
            rot3 = r3(rot_l[par], 2)
            b_xs = b_xs_l[par]
            b_pin = b_pin_l[par]
            b_rot = b_rot_l[par]
            Cur.xs3 = xs3
            Cur.b_xs = b_xs
            hT3 = hT3_l[par]
            b_hT = b_hT_l[par]
            Cur.hT3 = hT3
            Cur.b_hT = b_hT
            cosT = rot3[:, 0, 0:TT]
            sinT = rot3[:, 1, 0:TT]

            if not Cur.normed:
                norm_to_hT(subs, V_LNMIX)

            if sample:
                ucat5 = ucat3.rearrange("p c (b w) -> p c b w", w=34)
            need_tail = sample or last_prompt
            ntail = TS if sample else 32

            def u_dst(c):
                if sample:
                    return ucat5[:, c, :, 30:34]
                return ucat3[:, c, 30:30 + T]

            def p_conv_hist():
                if sample:
                    for g in range(4):
                        dma(SP, slot("scg"), scg[:120, :], stc_d[4 * g:4 * g + 4, :, :].rearrange("b r c -> (b r) c"),
                            writes=[b_scg])
                        for half in range(2):
                            pb = ps_get()
                            for i in range(4):
                                c = half * 4 + i
                                transpose(PSf(pb)[:, i * 128:i * 128 + 120], PSB[pb], scg[:120, c * 128:(c + 1) * 128],
                                          [b_scg], ident[:120, :120], inc=(i == 3))
                            for i in range(4):
                                c = half * 4 + i
                                op(ACT, lambda h, pb=pb, i=i, c=c, g=g: h.activation(
                                    out=ucat5[:, c, 4 * g:4 * g + 4, 0:30],
                                    in_=PSf(pb)[:, i * 128:i * 128 + 120].rearrange("p (b r) -> p b r", r=30),
                                    func=AF.Copy), reads=[PSB[pb]], writes=[b_ucat[c]])
                    dma(SP, slot("ncs0"), ncs_d[:, 0:26, :], stc_d[:, 4:30, :])
                else:
                    for c in range(8):
                        if first_prompt:
                            op(POOL, lambda h, c=c: h.memset(uprev3[:, c, :], 0.0), writes=[b_uprev[c]])
                        op(POOL, lambda h, c=c: h.tensor_copy(out=ucat3[:, c, 0:30], in_=uprev3[:, c, 0:30]),
                           reads=[b_uprev[c]], writes=[b_ucat[c]])

            def g_agav():
                for jb in range(4):
                    def ev_gate(pb, jb=jb):
                        op(ACT, lambda h: h.activation(out=sg3[:, :, 0:TT], in_=PS2(pb, TT), func=AF.Sigmoid),
                           reads=[PSB[pb]], writes=b_sg)
                    fm_proj2("w_in", O_AG + jb * 256, hT3, b_hT, TT, ev_gate)
                    yield

                    def ev_val(pb, jb=jb):
                        c0_ = jb * 2
                        if sample:
                            for i in range(2):
                                c = c0_ + i
                                src = PSf(pb)[:, i * 256:i * 256 + TT].rearrange("p (b t) -> p b t", t=4)
                                s2 = sg3[:, i, 0:TT].rearrange("p (b t) -> p b t", t=4)
                                op(DVE, lambda h, c=c, src=src, s2=s2: h.tensor_tensor(out=u_dst(c), in0=src, in1=s2, op=ALU.mult),
                                   reads=[PSB[pb], b_sg[i]], writes=[b_ucat[c]])
                        else:
                            op(DVE, lambda h: h.tensor_tensor(out=ucat3[:, c0_:c0_ + 2, 30:30 + T], in0=PS2(pb, TT),
                                                             in1=sg3[:, :, 0:TT], op=ALU.mult),
                               reads=[PSB[pb]] + b_sg, writes=[b_ucat[c0_], b_ucat[c0_ + 1]])
                            op(POOL, lambda h: h.tensor_copy(out=uprev3[:, c0_:c0_ + 2, 0:30], in_=ucat3[:, c0_:c0_ + 2, T:T + 30]),
                               reads=[b_ucat[c0_], b_ucat[c0_ + 1]], writes=[b_uprev[c0_], b_uprev[c0_ + 1]])
                        if need_tail:
                            op(DVE, lambda h: h.tensor_tensor(out=utail3[:, c0_:c0_ + 2, 0:ntail], in0=PS2(pb, TT)[:, :, TT - ntail:TT],
                                                             in1=sg3[:, :, TT - ntail:TT], op=ALU.mult),
                               reads=[PSB[pb]] + b_sg, writes=[b_utail])
                    fm_proj2("w_in", O_AV + jb * 256, hT3, b_hT, TT, ev_val)
                    yield

            def p_tail():
                if not need_tail:
                    return
                nrow = TS if sample else 30
                c0t = 0 if sample else 2
                for half in range(2):
                    pb = ps_get()
                    for i in range(4):
                        c = half * 4 + i
                        transpose(PSf(pb)[:nrow, i * 128:(i + 1) * 128], PSB[pb], utail3[:, c, c0t:c0t + nrow],
                                  [b_utail], ident, inc=(i == 3))
                    op(ACT, lambda h, pb=pb, half=half, nrow=nrow: h.activation(
                        out=utok[:nrow, half * 512:(half + 1) * 512], in_=PSf(pb)[:nrow, :], func=AF.Copy),
                       reads=[PSB[pb]], writes=[b_utok])
                if sample:
                    for b in range(NSEQ_S):
                        dma(ACT, slot("ncs1"), ncs_d[b, 26:30, :], utok[4 * b:4 * b + 4, :], reads=[b_utok])
                else:
                    dma(ACT, slot("ncp"), ncp_d[:, :], utok[:30, :], reads=[b_utok])

            st_banks = {}

            def g_conv_chunks():
                pm = ps_get()
                PsA.held.add(pm)
                pe2 = ps_get()
                PsA.held.add(pe2)
                st_banks["pm"] = pm
                st_banks["pe2"] = pe2
                for c in range(8):
                    pb = ps_get()
                    for part, (j0, nj) in enumerate([(0, 16), (16, 15)]):
                        Dv, Db = ws_next("convD", c, nj, j0, 128)
                        if Ctx.dry:
                            continue
                        terms = []
                        for jj in range(nj):
                            j = j0 + jj
                            if sample:
                                rhs = ucat5[:, c, :, j:j + 4]
                            else:
                                rhs = ucat3[:, c, j:j + T]
                            terms.append((Dv[:, jj, :], rhs, [Db, b_ucat[c]]))
                        mm_group(PSf(pb)[:, 0:TT], PSB[pb], terms, first=(part == 0), last=(part == 1))
                        if part == 0 and not sample:
                            yield
                    qi = c % 2
                    op(ACT, lambda h, pb=pb, c=c: h.activation(out=c_sb3[:, c, 0:TT], in_=PSf(pb)[:, 0:TT], func=AF.Identity,
                                                             bias=vcol(V_CB, c), scale=1.0),
                       reads=[PSB[pb], b_const], writes=[b_csb[c]])
                    op(ACT, lambda h, pb=pb, c=c, qi=qi: h.activation(out=csq3[:, qi, 0:TT], in_=PSf(pb)[:, 0:TT], func=AF.Square,
                                                                    bias=vcol(V_CB, c), scale=1.0),
                       reads=[PSB[pb], b_const], writes=[b_csq[qi]])
                    def stats(cc):
                        qq = cc % 2
                        mm_group(PSf(pm)[:, 0:TT], PSB[pm], [(ones_s, c_sb3[:, cc, 0:TT], [b_const, b_csb[cc]])],
                                 first=(cc == 0), last=(cc == 7))
                        mm_group(PSf(pe2)[:, 0:TT], PSB[pe2], [(ones_s, csq3[:, qq, 0:TT], [b_const, b_csq[qq]])],
                                 first=(cc == 0), last=(cc == 7))
                    if c > 0:
                        stats(c - 1)
                    if c == 7:
                        yield
                        stats(7)
                    yield

            def g_ln():
                pm = st_banks["pm"]
                pe2 = st_banks["pe2"]
                op(ACT, lambda h: h.activation(out=mean_sb[:, 0:TT], in_=PSf(pm)[:, 0:TT], func=AF.Copy),
                   reads=[PSB[pm]], writes=[b_ln])
                op(DVE, lambda h: h.tensor_tensor(out=lnB[:, 0:TT], in0=mean_sb[:, 0:TT], in1=mean_sb[:, 0:TT], op=ALU.mult),
                   reads=[b_ln], writes=[b_ln])
                op(DVE, lambda h: h.tensor_tensor(out=lnA[:, 0:TT], in0=PSf(pe2)[:, 0:TT], in1=lnB[:, 0:TT], op=ALU.subtract),
                   reads=[b_ln, PSB[pe2]], writes=[b_ln])
                op(ACT, lambda h: h.activation(out=lnA[:, 0:TT], in_=lnA[:, 0:TT], func=AF.Sqrt, bias=eps5[:, 0:1], scale=1.0),
                   reads=[b_ln, b_const], writes=[b_ln])
                op(DVE, lambda h: h.reciprocal(out=lnA[:, 0:TT], in_=lnA[:, 0:TT]), reads=[b_ln], writes=[b_ln])
                op(DVE, lambda h: h.scalar_tensor_tensor(out=lnB[:, 0:TT], in0=mean_sb[:, 0:TT], scalar=-1.0, in1=lnA[:, 0:TT],
                                                         op0=ALU.mult, op1=ALU.mult), reads=[b_ln], writes=[b_ln])
                PsA.held.discard(pm)
                PsA.held.discard(pe2)
                yield
                for c in range(8):
                    qi = c % 2
                    op(DVE, lambda h, c=c, qi=qi: h.tensor_tensor(out=tmpc3[:, qi, 0:TT], in0=c_sb3[:, c, 0:TT], in1=lnA[:, 0:TT],
                                                                op=ALU.mult), reads=[b_csb[c], b_ln], writes=[b_tmpc[qi]])
                    op(DVE, lambda h, qi=qi: h.tensor_tensor(out=tmpc3[:, qi, 0:TT], in0=tmpc3[:, qi, 0:TT], in1=lnB[:, 0:TT],
                                                           op=ALU.add), reads=[b_tmpc[qi], b_ln], writes=[b_tmpc[qi]])
                    op(ACT, lambda h, c=c, qi=qi: h.activation(out=ca3[:, c, 0:TT], in_=tmpc3[:, qi, 0:TT], func=AF.Silu,
                                                             scale=vcol(V_CLG, c), bias=vcol(V_CLB, c)),
                       reads=[b_tmpc[qi], b_const], writes=[b_ca[c]])
                    yield

            def g_gate_a():
                for jb in range(4):
                    def ev_ga(pb, jb=jb):
                        c0_ = jb * 2
                        op(ACT, lambda h: h.activation(out=sga3[:, c0_:c0_ + 2, 0:TT], in_=PS2(pb, TT), func=AF.Sigmoid),
                           reads=[PSB[pb]], writes=[b_sga[c0_], b_sga[c0_ + 1]])
                    fm_proj2("w_in", O_GA + jb * 256, hT3, b_hT, TT, ev_ga)
                    yield

            def p_conv_out():
                for jb in range(4):
                    def ev_ya(pb, jb=jb):
                        c0_ = jb * 2
                        op(DVE, lambda h: h.tensor_tensor(out=m_a3[:, c0_:c0_ + 2, 0:TT], in0=PS2(pb, TT), in1=sga3[:, c0_:c0_ + 2, 0:TT],
                                                         op=ALU.mult),
                           reads=[PSB[pb], b_sga[c0_], b_sga[c0_ + 1]], writes=[b_ma[c0_], b_ma[c0_ + 1]])
                    fm_proj2("w_co", jb * 256, ca3, b_ca, TT, ev_ya)

            def g_qk():
                for which, (off, dst3, dbufs, scl) in enumerate([(O_Q, qr3, b_qr, 1.0), (O_K, kr3, b_kr, 0.0625)]):
                    for hh in range(4):
                        def ev_rot(pb, hh=hh, dst3=dst3, dbufs=dbufs, scl=scl):
                            x1 = PSf(pb)[:, 0:TT]
                            x2 = PSf(pb)[:, 256:256 + TT]
                            t = [rtmp3[:, i, 0:TT] for i in range(4)]
                            for i, (xx, tab) in enumerate([(x1, cosT), (x2, sinT), (x1, sinT), (x2, cosT)]):
                                op(DVE, lambda h, i=i, xx=xx, tab=tab: h.scalar_tensor_tensor(
                                    out=t[i], in0=xx, scalar=scl, in1=tab, op0=ALU.mult, op1=ALU.mult),
                                   reads=[PSB[pb], b_rot], writes=[b_rtmp[i]])
                            op(POOL, lambda h: h.tensor_tensor(out=dst3[:, 2 * hh, 0:TT], in0=t[0], in1=t[1], op=ALU.subtract),
                               reads=[b_rtmp[0], b_rtmp[1]], writes=[dbufs[2 * hh]])
                            op(POOL, lambda h: h.tensor_tensor(out=dst3[:, 2 * hh + 1, 0:TT], in0=t[2], in1=t[3], op=ALU.add),
                               reads=[b_rtmp[2], b_rtmp[3]], writes=[dbufs[2 * hh + 1]])
                        fm_proj2("w_in", off + hh * 256, hT3, b_hT, TT, ev_rot)
                        if which == 0:
                            gl_off = C_GLS if sample else C_GLP
                            gt = ctab[:, gl_off + hh * TT:gl_off + (hh + 1) * TT]
                            for dc in range(2):
                                c = 2 * hh + dc
                                op(POOL, lambda h, c=c, gt=gt: h.tensor_tensor(out=qdec3[:, c, 0:TT], in0=qr3[:, c, 0:TT], in1=gt,
                                                                             op=ALU.mult),
                                   reads=[b_qr[c], b_const], writes=[b_qdec[c]])
                        yield

            def p_v():
                for hh in range(4):
                    def ev_v(cg, s, pb, npart, hh=hh):
                        op(ACT, lambda h: h.activation(out=v_sb3[:npart, s, hh * 512:(hh + 1) * 512], in_=PSf(pb)[:npart, :],
                                                       func=AF.Copy), reads=[PSB[pb]], writes=[b_v[s * 4 + hh]])
                    tm_proj("w_in", O_V + hh * 512, 512, hT3, b_hT, 8, subs, ev_v)

            def g_retention():
                for s, (t0, npart) in enumerate(subs):
                    yield from retention_prompt(s, t0)

            def g_g():
                for jb in range(8):
                    def ev_g(pb, jb=jb):
                        c0_ = jb * 2
                        op(ACT, lambda h: h.activation(out=sgg3[:, :, 0:TT], in_=PS2(pb, TT), func=AF.Silu),
                           reads=[PSB[pb]], writes=b_sgg)
                        op(DVE, lambda h: h.tensor_tensor(out=zT3[:, c0_:c0_ + 2, 0:TT], in0=zT3[:, c0_:c0_ + 2, 0:TT],
                                                         in1=sgg3[:, :, 0:TT], op=ALU.mult),
                           reads=[b_zT[c0_], b_zT[c0_ + 1]] + b_sgg, writes=[b_zT[c0_], b_zT[c0_ + 1]])
                    fm_proj2("w_in", O_G + jb * 256, hT3, b_hT, TT, ev_g)
                    yield

            def p_yb():
                for jb in range(4):
                    def ev_gb(pb):
                        op(ACT, lambda h: h.activation(out=sgg3[:, :, 0:TT], in_=PS2(pb, TT), func=AF.Sigmoid),
                           reads=[PSB[pb]], writes=b_sgg)
                    fm_proj2("w_in", O_GB + jb * 256, hT3, b_hT, TT, ev_gb)

                    def ev_yb(oc, pb, jb=jb):
                        i = oc % 2
                        c = jb * 2 + i
                        op(DVE, lambda h: h.tensor_tensor(out=sgg3[:, i, 0:TT], in0=PSf(pb)[:, 0:TT], in1=sgg3[:, i, 0:TT],
                                                         op=ALU.mult), reads=[PSB[pb], b_sgg[i]], writes=[b_sgg[i]])
                        op(DVE, lambda h: h.tensor_tensor(out=mixin3[:, c, 0:TT], in0=sgg3[:, i, 0:TT], in1=m_a3[:, c, 0:TT],
                                                         op=ALU.add), reads=[b_sgg[i], b_ma[c]], writes=[b_mixin[c]])
                    fm_proj("w_ro", jb * 256, 256, zT3, b_zT, 16, TT, ev_yb)

            def run(g):
                for _ in g:
                    pass

            def rr(*gens):
                gens = list(gens)
                while gens:
                    for g in list(gens):
                        try:
                            next(g)
                        except StopIteration:
                            gens.remove(g)

            def chain(*gens):
                for g in gens:
                    yield from g

            if sample:
                inherit(G_CONV, G_RETC + G_SAMP + G_FFN)
                p_conv_hist()
                run(g_agav())
                flush_pending()
                p_tail()
                run(g_conv_chunks())
                run(g_gate_a())
                run(g_ln())
                p_conv_out()
                inherit(G_RETC + G_SAMP, G_CONV + G_FFN)
                run(g_qk())
                p_v()
                retention_sample()
                run(g_g())
                p_yb()
            else:
                inherit(G_CONV + G_RETC, G_FFN + G_SAMP)
                p_conv_hist()
                rr(g_agav(), g_qk())
                flush_pending()
                p_tail()
                p_v()
                if first_prompt:
                    for i in range(8):
                        op(POOL, lambda h, i=i: h.memset(S32_3[:, i, :], 0.0), writes=[b_S32[i]])
                        op(POOL, lambda h, i=i: h.memset(Sbf3[:, i, :], 0.0), writes=[b_Sbf[i]])
                rr(g_conv_chunks(), g_retention())
                rr(g_ln(), chain(g_g(), g_gate_a()))
                p_conv_out()
                p_yb()
                if last_prompt:
                    for hh in range(4):
                        dma(ACT, slot("nrp"), nrp_d[hh, :, :].rearrange("(dc p) v -> p dc v", p=128),
                            S32_3[:, 2 * hh:2 * hh + 2, :], reads=[b_S32[2 * hh], b_S32[2 * hh + 1]])

            def ev_res(cg, s, pb, npart):
                xa = xs3[:npart, s, cg * 512:(cg + 1) * 512]
                op(DVE, lambda h: h.tensor_tensor(out=xa, in0=PSf(pb)[:npart, :], in1=xa, op=ALU.add),
                   reads=[PSB[pb], b_xs[s]], writes=[b_xs[s]])
            tm_proj("w_o", 0, D, mixin3, b_mixin, 8, subs, ev_res)

            inherit(G_FFN, G_CONV + G_RETC + G_SAMP)
            if Cur.nxt is not None:
                load_tile(Cur.nxt[0], Cur.nxt[1], 1 - par)
            norm_to_hT(subs, V_LNFFN)
            for jb in range(11):
                def ev_fg(pb):
                    op(ACT, lambda h: h.activation(out=sgate3[:, :, 0:TT], in_=PS2(pb, TT), func=AF.Silu),
                       reads=[PSB[pb]], writes=b_sgate)
                fm_proj2("w_fg", jb * 256, hT3, b_hT, TT, ev_fg)

                def ev_fu(pb, jb=jb):
                    c0_ = jb * 2
                    op(DVE, lambda h: h.tensor_tensor(out=act3[:, c0_:c0_ + 2, 0:TT], in0=PS2(pb, TT), in1=sgate3[:, :, 0:TT],
                                                     op=ALU.mult),
                       reads=[PSB[pb]] + b_sgate, writes=[b_act[c0_], b_act[c0_ + 1]])
                fm_proj2("w_fu", jb * 256, hT3, b_hT, TT, ev_fu)
            tm_proj("w_fd", 0, D, act3, b_act, 22, subs, ev_res)

            def ple_pproj():
                for s, (t0, npart) in enumerate(subs):
                    pb = ps_get()
                    for kc in range(2):
                        transpose(PSf(pb)[:, kc * 128:kc * 128 + npart], PSB[pb], pin3[:npart, s, kc * 128:(kc + 1) * 128],
                                  [b_pin], ident[:npart, :npart], inc=(kc == 1))
                    op(ACT, lambda h, pb=pb, t0=t0, npart=npart: h.activation(
                        out=pT3[:, :, t0:t0 + npart], in_=PSf(pb)[:, 0:256].rearrange("p (k t) -> p k t", k=2)[:, :, 0:npart],
                        func=AF.Copy), reads=[PSB[pb]], writes=[b_pT])

                def ev_pp(cg, s, pb, npart):
                    op(ACT, lambda h: h.activation(out=pproj3[:npart, s, cg * 512:(cg + 1) * 512], in_=PSf(pb)[:npart, :],
                                                   func=AF.Copy), reads=[PSB[pb]], writes=[b_pproj])
                wv, wb = ws_next("w_pp", 0, 2, 0, 1024)
                if not Ctx.dry:
                    for cg in range(2):
                        for s, (t0, npart) in enumerate(subs):
                            pb = ps_get()
                            mm_group(PSf(pb)[:npart, :], PSB[pb],
                                     [(pT3[:, k, t0:t0 + npart], wv[:, k, cg * 512:(cg + 1) * 512], [wb, b_pT]) for k in range(2)])
                            ev_pp(cg, s, pb, npart)

            norm_to_hT(subs, V_LNPLE, filler=ple_pproj)

            def ev_pg(cg, s, pb, npart):
                gi = (cg * 2 + s) % 2
                ga = gate_sb[gi][:npart, :]
                xa = xs3[:npart, s, cg * 512:(cg + 1) * 512]
                op(ACT, lambda h: h.activation(out=ga, in_=PSf(pb)[:npart, :], func=AF.Sigmoid),
                   reads=[PSB[pb]], writes=[b_gsb[gi]])
                op(DVE, lambda h: h.tensor_tensor(out=ga, in0=ga, in1=pproj3[:npart, s, cg * 512:(cg + 1) * 512], op=ALU.mult),
                   reads=[b_gsb[gi], b_pproj], writes=[b_gsb[gi]])
                op(DVE, lambda h: h.tensor_tensor(out=xa, in0=xa, in1=ga, op=ALU.add),
                   reads=[b_gsb[gi], b_xs[s]], writes=[b_xs[s]])

            def ple_gate():
                tm_proj("w_pg", 0, D, hT3, b_hT, 8, subs, ev_pg)

            if Cur.nxt is not None:
                nsubs = [(0, TS)] if Cur.nxt[0] else [(0, 128), (128, 128)]
                norm_to_hT(nsubs, V_LNMIX, r3(xs_l[1 - par], 2), b_xs_l[1 - par], hT3_l[1 - par], b_hT_l[1 - par],
                           filler=ple_gate)
                Cur.normed = True
            else:
                ple_gate()

            for s, (t0, npart) in enumerate(subs):
                xa = xs3[:npart, s, :]
                rstd, bs = rms_stats(xa, npart, [b_xs[s]], si=s)
                op(DVE, lambda h, xa=xa, rstd=rstd, npart=npart, s=s: h.scalar_tensor_tensor(
                    out=xa, in0=xa, scalar=rstd, in1=gfin[:npart, :], op0=ALU.mult, op1=ALU.mult),
                   reads=[b_xs[s], bs, b_const], writes=[b_xs[s]])
                if sample:
                    Cur.pending.append(lambda par=par, xs3=xs3, b_xs=b_xs: dma(
                        ACT, slot(f"yout{par}_0"), ys_d[:, :], xs3[:TS, 0, :], reads=[b_xs[0]]))
                else:
                    r0 = ti * T + s * 128
                    Cur.pending.append(lambda par=par, s=s, r0=r0, xs3=xs3, b_xs=b_xs: dma(
                        ACT, slot(f"yout{par}_{s}"), yp_d[r0:r0 + 128, :], xs3[:, s, :], reads=[b_xs[s]]))

        def kd_build(s_off, npart, kdcol_off):
            pb = ps_get()
            for c in range(8):
                transpose(PSh(pb)[:npart, c * 128:(c + 1) * 128], PSB[pb], kr3[:, c, s_off:s_off + npart], [b_kr[c]],
                          identb, inc=(c == 7))
            for hh in range(4):
                op(ACT, lambda h, hh=hh, pb=pb: h.activation(
                    out=kd[:npart, hh * 256:(hh + 1) * 256], in_=PSh(pb)[:npart, hh * 256:(hh + 1) * 256], func=AF.Identity,
                    scale=ctab[:npart, kdcol_off + hh:kdcol_off + hh + 1], bias=0.0), reads=[PSB[pb], b_const], writes=[b_kd])

        def gn_and_T(pb, hh, npart):
            base = 40 + hh * 12
            bs = b_small_l[2 + hh]
            op(DVE, lambda h: h.bn_stats(out=small[:npart, base:base + 6], in_=PSf(pb)[:npart, :]),
               reads=[PSB[pb]], writes=[bs])
            op(DVE, lambda h: h.bn_aggr(out=small[:npart, base + 6:base + 8], in_=small[:npart, base:base + 6]),
               reads=[bs], writes=[bs])
            op(ACT, lambda h: h.activation(out=small[:npart, base + 8:base + 9], in_=small[:npart, base + 7:base + 8],
                                           func=AF.Sqrt, bias=eps5[:npart, 0:1], scale=1.0),
               reads=[bs, b_const], writes=[bs])
            op(DVE, lambda h: h.reciprocal(out=small[:npart, base + 9:base + 10], in_=small[:npart, base + 8:base + 9]),
               reads=[bs], writes=[bs])
            op(DVE, lambda h: h.tensor_scalar(out=on[:npart, hh * 512:(hh + 1) * 512], in0=PSf(pb)[:npart, :],
                                              scalar1=small[:npart, base + 6:base + 7],
                                              scalar2=small[:npart, base + 9:base + 10],
                                              op0=ALU.subtract, op1=ALU.mult),
               reads=[PSB[pb], bs], writes=[b_on[hh]])

        def on_to_zT(t0, npart):
            for half in range(2):
                pb = ps_get()
                for i in range(8):
                    fc = half * 8 + i
                    transpose(PSh(pb)[:, i * 128:i * 128 + npart], PSB[pb], on[:npart, fc * 128:(fc + 1) * 128],
                              [b_on[fc // 4]], identb[:npart, :npart], inc=(i == 7))
                for i in range(8):
                    fc = half * 8 + i
                    op(ACT, lambda h, pb=pb, i=i, fc=fc: h.activation(
                        out=zT3[:, fc, t0:t0 + npart], in_=PSh(pb)[:, i * 128:i * 128 + npart], func=AF.Identity,
                        scale=vcol(V_GNG, fc), bias=vcol(V_GNB, fc)), reads=[PSB[pb], b_const], writes=[b_zT[fc]])

        def retention_prompt(s, t0):
            kd_build(t0, 128, C_KDP)
            for hh in range(4):
                pb = ps_get()
                mm_group(PSf(pb)[:, 0:128], PSB[pb],
                         [(kr3[:, 2 * hh + dc, t0:t0 + 128], qr3[:, 2 * hh + dc, t0:t0 + 128],
                           [b_kr[2 * hh + dc], b_qr[2 * hh + dc]]) for dc in range(2)])
                op(DVE, lambda h, pb=pb, hh=hh: h.tensor_tensor(
                    out=sT[hh][:, :], in0=PSf(pb)[:, 0:128], in1=ctab[:, C_DECP + hh * 128:C_DECP + (hh + 1) * 128],
                    op=ALU.mult), reads=[PSB[pb], b_const], writes=[b_sT[hh]])
            yield
            for hh in range(4):
                po = ps_get()
                vv = v_sb3[:, s, hh * 512:(hh + 1) * 512]
                terms = [(sT[hh][:, :], vv, [b_sT[hh], b_v[s * 4 + hh]])]
                for dc in range(2):
                    terms.append((qdec3[:, 2 * hh + dc, t0:t0 + 128], Sbf3[:, 2 * hh + dc, :],
                                  [b_qdec[2 * hh + dc], b_Sbf[2 * hh + dc]]))
                mm_group(PSf(po)[:, :], PSB[po], terms)
                gn_and_T(po, hh, 128)
                for dc in range(2):
                    si = 2 * hh + dc
                    pst = ps_get()
                    mm_group(PSf(pst)[:, :], PSB[pst],
                             [(kd[:, hh * 256 + dc * 128:hh * 256 + (dc + 1) * 128], vv, [b_kd, b_v[s * 4 + hh]])])
                    op(DVE, lambda h, si=si, pst=pst, hh=hh: h.scalar_tensor_tensor(
                        out=S32_3[:, si, :], in0=S32_3[:, si, :], scalar=G128[hh], in1=PSf(pst)[:, :],
                        op0=ALU.mult, op1=ALU.add), reads=[PSB[pst], b_S32[si]], writes=[b_S32[si]])
                    op(ACT, lambda h, si=si: h.activation(out=Sbf3[:, si, :], in_=S32_3[:, si, :], func=AF.Copy),
                       reads=[b_S32[si]], writes=[b_Sbf[si]])
                yield
            on_to_zT(t0, 128)
            yield

        def retention_sample():
            kd_build(0, TS, C_KDS)
            po = []
            for hh in range(4):
                pb = ps_get()
                mm_group(PSf(pb)[:TS, 0:TS], PSB[pb],
                         [(kr3[:, 2 * hh + dc, 0:TS], qr3[:, 2 * hh + dc, 0:TS],
                           [b_kr[2 * hh + dc], b_qr[2 * hh + dc]]) for dc in range(2)])
                op(DVE, lambda h, pb=pb, hh=hh: h.tensor_tensor(
                    out=sT[hh][:TS, 0:TS], in0=PSf(pb)[:TS, 0:TS], in1=ctab[:TS, C_DECS + hh * 64:C_DECS + (hh + 1) * 64],
                    op=ALU.mult), reads=[PSB[pb], b_const], writes=[b_sT[hh]])
            for hh in range(4):
                p = ps_get()
                PsA.held.add(p)
                po.append(p)
                mm_group(PSf(p)[:TS, :], PSB[p], [(sT[hh][:TS, 0:TS], v_sb3[:TS, 0, hh * 512:(hh + 1) * 512],
                                                  [b_sT[hh], b_v[hh]])], first=True, last=False)
            units = [(b, hh) for b in range(NSEQ_S) for hh in range(4)]
            NU = len(units)

            def load_unit(u):
                b, hh = units[u]
                i = u % 6
                dma(SP, slot(f"S0_{i}"), S0[i].rearrange("p (dc v) -> p dc v", dc=2),
                    str_d[b, hh, :, :].rearrange("(dc p) v -> p dc v", p=128), writes=[b_S0[i]])
            for u0 in range(3):
                load_unit(u0)
            for u, (b, hh) in enumerate(units):
                if u + 3 < NU:
                    load_unit(u + 3)
                i4 = u % 6
                i2 = u % 2
                if hh == 0:
                    bi = b % 2
                    op(POOL, lambda h, bi=bi, b=b: h.tensor_tensor(
                        out=qxb[bi].rearrange("p (c t) -> p c t", c=8), in0=qdec3[:, :, 0:TS],
                        in1=ctab[:, C_MROW + 60 - 4 * b:C_MROW + 124 - 4 * b].unsqueeze(1).to_broadcast([128, 8, TS]),
                        op=ALU.mult), reads=b_qdec + [b_const], writes=[b_qxb[bi]])
                    op(ACT, lambda h, bi=bi, b=b: h.activation(out=kdx[bi][:TS, :], in_=kd[:TS, :], func=AF.Identity,
                                                             scale=ctab[:TS, C_MCOL + b:C_MCOL + b + 1], bias=0.0),
                       reads=[b_kd, b_const], writes=[b_kdx[bi]])
                bi = b % 2
                qx3 = qxb[bi].rearrange("p (c t) -> p c t", c=8)
                S0v = S0[i4].rearrange("p (dc v) -> p dc v", dc=2)
                S0bv = S0b[i2].rearrange("p (dc v) -> p dc v", dc=2)
                op(ACT, lambda h, i2=i2, i4=i4: h.activation(out=S0b[i2][:, :], in_=S0[i4][:, :], func=AF.Copy),
                   reads=[b_S0[i4]], writes=[b_S0b[i2]])
                lastu = (b == NSEQ_S - 1)
                mm_group(PSf(po[hh])[:TS, :], PSB[po[hh]],
                         [(qx3[:, 2 * hh + dc, :], S0bv[:, dc, :], [b_qxb[bi], b_S0b[i2]]) for dc in range(2)],
                         first=False, last=lastu)
                for dc in range(2):
                    pst = ps_get()
                    mm_group(PSf(pst)[:, :], PSB[pst],
                             [(kdx[bi][:TS, hh * 256 + dc * 128:hh * 256 + (dc + 1) * 128],
                               v_sb3[:TS, 0, hh * 512:(hh + 1) * 512], [b_kdx[bi], b_v[hh]])])
                    op(DVE, lambda h, dc=dc, pst=pst, hh=hh, S0v=S0v: h.scalar_tensor_tensor(
                        out=S0v[:, dc, :], in0=S0v[:, dc, :], scalar=G4[hh], in1=PSf(pst)[:, :],
                        op0=ALU.mult, op1=ALU.add), reads=[PSB[pst], b_S0[i4]], writes=[b_S0[i4]])
                dma(SP, slot(f"S1_{i4}"), nrs_d[b, hh, :, :].rearrange("(dc p) v -> p dc v", p=128), S0v,
                    reads=[b_S0[i4]])
            for hh in range(4):
                gn_and_T(po[hh], hh, TS)
                PsA.held.discard(po[hh])
            on_to_zT(0, TS)

        def emit_all():
            PsA.i = 0
            PsA.held = set()
            WS.cons = 0
            Cur.normed = False
            dma(SP, slot("const"), ctab[:, :], ctab_d[:, :], writes=[b_const])
            dma(SP, slot("const"), vecs[:, :], vecs_d[:, :], writes=[b_const])
            dma(SP, slot("const"), gfin[:, :], gfin_d[:, :], writes=[b_const])
            op(DVE, lambda h: h.tensor_copy(out=identb[:, :], in_=ident), reads=[b_const], writes=[b_const])
            op(DVE, lambda h: h.memset(epsc[:, :], EPS), writes=[b_const])
            op(DVE, lambda h: h.memset(eps5[:, :], 1e-5), writes=[b_const])
            op(DVE, lambda h: h.memset(mhalf[:, :], -0.5), writes=[b_const])
            order = [(False, ti) for ti in range(NT)] + [(True, 0)]
            load_tile(order[0][0], order[0][1], 0)
            Cur.pending = []
            for n_, (smp, ti) in enumerate(order):
                Cur.par = n_ % 2
                Cur.nxt = order[n_ + 1] if n_ + 1 < len(order) else None
                do_tile(smp, ti)
            flush_pending()

        Ctx.dry = True
        emit_all()
        Ctx.dry = False
        WS.nb = len(WS.specs) // (NT + 1)
        assert WS.nb * (NT + 1) == len(WS.specs)
        WS.cache = nc.dram_tensor("wcache", [WS.nb, 128, WBLK], BF16).ap()
        WS.wall = din("wall", [WS.nb, 128, WBLK])
        WS.cidx = {sp: i for i, sp in enumerate(WS.specs[:WS.nb])}
        assert len(WS.cidx) == WS.nb
        for t_ in range(NT + 1):
            assert sorted(WS.specs[t_ * WS.nb:(t_ + 1) * WS.nb]) == sorted(WS.specs[:WS.nb])
        emit_all()
        assert WS.cons == len(WS.specs)

        for sl in slots.values():
            SP.prog.append(("w", sl, sl.count))

        engs = [PE, ACT, DVE, POOL]
        for e in engs:
            e.sem = es.enter_context(nc.semaphore(f"s_{e.name}"))
        for name, sl in slots.items():
            sl.sem = es.enter_context(nc.semaphore(f"d_{name}"))

        def replay(eng, h):
            fuse = eng in (ACT, DVE, POOL, SP)
            pend = []
            for it in eng.prog:
                if it[0] == "w":
                    if fuse:
                        pend.append((it[1].sem, it[2]))
                    else:
                        h.wait_ge(it[1].sem, it[2])
                    continue
                for sm_, v_ in pend[:-1]:
                    h.wait_ge(sm_, v_)
                if it[0] == "o":
                    ins = it[1](h)
                    if pend:
                        ins._wait_ge(pend[-1][0], pend[-1][1])
                    if it[2]:
                        ins.then_inc(eng.sem, 1)
                else:
                    ins = h.dma_start(out=it[1], in_=it[2])
                    if pend:
                        ins._wait_ge(pend[-1][0], pend[-1][1])
                    ins.then_inc(it[3].sem, 16)
                pend = []
            for sm_, v_ in pend:
                h.wait_ge(sm_, v_)

        with nc.Block() as block:
            @block.sync
            def _(h):
                replay(SP, h)

            @block.tensor
            def _(h):
                replay(PE, h)

            @block.scalar
            def _(h):
                replay(ACT, h)

            @block.vector
            def _(h):
                replay(DVE, h)

            @block.gpsimd
            def _(h):
                replay(POOL, h)
        print("instr counts:", {e.name: len(e.prog) for e in engs + [SP]})
    nc._ws_specs = list(WS.specs[:WS.nb])
    return nc


_CACHE = {}


def kernel(**inputs):
    f = lambda a: np.ascontiguousarray(np.asarray(a, dtype=np.float32))
    x_prompt = f(inputs["x_prompt"])
    x_sample = f(inputs["x_sample"])
    state_conv = f(inputs["state_conv"])[0]
    state_ret = f(inputs["state_ret"])[0]
    p_prompt = f(inputs["p_prompt"])[0]
    p_sample = f(inputs["p_sample"])[0]

    if "nc" not in _CACHE:
        _CACHE["nc"] = build_program()
        _CACHE["consts"] = _make_consts()
    nc = _CACHE["nc"]
    ctab, rotp, rots = _CACHE["consts"]

    def cols(v, n):
        return np.ascontiguousarray(np.asarray(v, np.float32).reshape(n, 128).T)

    vecs = np.zeros((128, NV), np.float32)
    vecs[:, V_LNMIX:V_LNMIX + 8] = cols(inputs["ln_mix_g"][0], 8)
    vecs[:, V_LNFFN:V_LNFFN + 8] = cols(inputs["ln_ffn_g"][0], 8)
    vecs[:, V_LNPLE:V_LNPLE + 8] = cols(inputs["ln_ple_g"][0], 8)
    vecs[:, V_CB:V_CB + 8] = cols(inputs["conv_b"][0], 8)
    vecs[:, V_CLG:V_CLG + 8] = cols(inputs["conv_ln_g"][0], 8)
    vecs[:, V_CLB:V_CLB + 8] = cols(inputs["conv_ln_b"][0], 8)
    vecs[:, V_GNG:V_GNG + 16] = cols(inputs["ret_gn_g"][0], 16)
    vecs[:, V_GNB:V_GNB + 16] = cols(inputs["ret_gn_b"][0], 16)
    cw = np.asarray(inputs["conv_w"], np.float32)[0]
    vecs[:, V_CW:V_CW + 8 * CW] = cw.reshape(CW, 8, 128).transpose(2, 1, 0).reshape(128, 8 * CW)
    gfin = np.ascontiguousarray(np.broadcast_to(np.asarray(inputs["ln_final_g"], np.float32)[None, :], (128, D)))

    Wh = {
        "w_in": f(inputs["w_in"])[0], "w_co": f(inputs["w_conv_out"])[0], "w_ro": f(inputs["w_ret_out"])[0],
        "w_o": f(inputs["w_o"])[0], "w_fg": f(inputs["w_ffn_gate"])[0], "w_fu": f(inputs["w_ffn_up"])[0],
        "w_fd": f(inputs["w_ffn_down"])[0], "w_pg": f(inputs["w_ple_gate"])[0], "w_pp": f(inputs["w_ple_proj"])[0],
    }
    specs = nc._ws_specs
    wall = np.zeros((len(specs), 128, WBLK), np.float32)
    for jj, (name, k0, nk, c0, ncol) in enumerate(specs):
        if name == "convD":
            continue
        blk = Wh[name][k0 * 128:(k0 + nk) * 128, c0:c0 + ncol]
        wall[jj, :, :nk * ncol] = blk.reshape(nk, 128, ncol).transpose(1, 0, 2).reshape(128, nk * ncol)
    shared = {"wall": wall, "ctab": ctab, "rotp": rotp, "rots": rots, "vecs": vecs, "gfin": gfin}
    in_maps = []
    for i in range(NCORE):
        m = dict(shared)
        m["xp"] = x_prompt[i]
        m["xs"] = x_sample[i * NSEQ_S:(i + 1) * NSEQ_S].reshape(TS, D)
        m["stc"] = state_conv[i * NSEQ_S:(i + 1) * NSEQ_S]
        m["str"] = state_ret[i * NSEQ_S:(i + 1) * NSEQ_S]
        m["pp"] = p_prompt[i]
        m["ps"] = p_sample[i * NSEQ_S:(i + 1) * NSEQ_S].reshape(TS, 256)
        in_maps.append(m)
    res = run_bass_kernel_spmd(nc, in_maps, core_ids=list(range(NCORE)))
    R = res.results
    y_prompt = np.stack([R[i]["yp"] for i in range(NCORE)], 0).astype(np.float32)
    y_sample = np.concatenate([R[i]["ys"].reshape(NSEQ_S, DEC, D) for i in range(NCORE)], 0).astype(np.float32)
    ncp = np.stack([R[i]["ncp"] for i in range(NCORE)], 0)[None].astype(np.float32)
    nrp = np.stack([R[i]["nrp"] for i in range(NCORE)], 0)[None].astype(np.float32)
    ncs = np.concatenate([R[i]["ncs"] for i in range(NCORE)], 0)[None].astype(np.float32)
    nrs = np.concatenate([R[i]["nrs"] for i in range(NCORE)], 0)[None].astype(np.float32)
    return (y_prompt, y_sample, ncp, nrp, ncs, nrs)
```
